# Optimizing a Trainium2 kernel written in Bass

```python
import math
import jax, jax.numpy as jnp
from jax import lax
import numpy as np

D_MODEL = 1024
BATCH = 8
SEQ = 2048
DEPTH = 2

GRID_W = 64
CTX_LEN = 256
DIFF_HEADS = 4
DIFF_HEAD_DIM = 64
GQA_HEADS = 8
GQA_KV_HEADS = 2
GQA_HEAD_DIM = 64
ROPE_THETA = 10000.0
ROPE_PAIRS_PER_AXIS = GQA_HEAD_DIM // 4
FOURIER_GROUPS = 8
FOURIER_GROUP_DIM = D_MODEL // FOURIER_GROUPS
D_FF = -(-8 * D_MODEL // (3 * 256)) * 256
Q_BLOCK = 128
EPS = 1e-6
N_MOD = 6

A_Q_W = DIFF_HEADS * 2 * DIFF_HEAD_DIM
A_K_W = DIFF_HEADS * 2 * DIFF_HEAD_DIM
A_V_W = DIFF_HEADS * 2 * DIFF_HEAD_DIM
B_Q_W = GQA_HEADS * GQA_HEAD_DIM
B_K_W = GQA_KV_HEADS * GQA_HEAD_DIM
B_V_W = GQA_KV_HEADS * GQA_HEAD_DIM
OFF_AQ = 0
OFF_AK = OFF_AQ + A_Q_W
OFF_AV = OFF_AK + A_K_W
OFF_BQ = OFF_AV + A_V_W
OFF_BK = OFF_BQ + B_Q_W
OFF_BV = OFF_BK + B_K_W
IN_WIDTH = OFF_BV + B_V_W
MIX_WIDTH = DIFF_HEADS * 2 * DIFF_HEAD_DIM + GQA_HEADS * GQA_HEAD_DIM

kernel_name = "hybrid_diffattn_gqa_fnet_dit_block"


def rms_norm(x, g):
    xf = x.astype(jnp.float32)
    y = xf * lax.rsqrt(jnp.mean(xf * xf, axis=-1, keepdims=True) + EPS)
    return (y * g.astype(jnp.float32)).astype(x.dtype)


def modulate(h, shift, scale):
    return h * (1 + scale) + shift


def ada_params(cond, w, b):
    m = jax.nn.silu(cond) @ w + b
    return [t[..., None, :] for t in jnp.split(m, N_MOD, axis=-1)]


def rope_tables(n_tokens):
    rows_count = n_tokens // GRID_W
    row = jnp.repeat(jnp.arange(rows_count, dtype=jnp.int32), GRID_W)
    col = jnp.tile(jnp.arange(GRID_W, dtype=jnp.int32), rows_count)
    inv = ROPE_THETA ** (-jnp.arange(ROPE_PAIRS_PER_AXIS, dtype=jnp.float32) / ROPE_PAIRS_PER_AXIS)
    ang = jnp.stack([row.astype(jnp.float32)[:, None] * inv, col.astype(jnp.float32)[:, None] * inv], axis=1)
    return jnp.cos(ang), jnp.sin(ang)


def apply_rope(x, cos, sin):
    b, s, n, d = x.shape
    xr = x.astype(jnp.float32).reshape(b, s, n, 2, 2, ROPE_PAIRS_PER_AXIS)
    x1, x2 = xr[..., 0, :], xr[..., 1, :]
    cb, sb = cos[None, :, None], sin[None, :, None]
    out = jnp.stack([x1 * cb - x2 * sb, x2 * cb + x1 * sb], axis=-2)
    return out.reshape(b, s, n, d).astype(x.dtype)


def to_blocks(t):
    b, s = t.shape[:2]
    return jnp.moveaxis(t.reshape(b, s // Q_BLOCK, Q_BLOCK, *t.shape[2:]), 1, 0)


def from_blocks(t):
    nb, b, q = t.shape[:3]
    return jnp.moveaxis(t, 0, 1).reshape(b, nb * q, *t.shape[3:])


def diff_attention(q1, q2, k1, k2, v, lam):
    scale = DIFF_HEAD_DIM ** -0.5

    def block(qb):
        b1, b2 = qb
        s1 = jnp.einsum('bqhd,bkhd->bhqk', b1, k1, preferred_element_type=jnp.float32) * scale
        s2 = jnp.einsum('bqhd,bkhd->bhqk', b2, k2, preferred_element_type=jnp.float32) * scale
        w = jax.nn.softmax(s1, axis=-1) - lam * jax.nn.softmax(s2, axis=-1)
        return jnp.einsum('bhqk,bkhe->bqhe', w.astype(v.dtype), v)

    return from_blocks(lax.map(block, (to_blocks(q1), to_blocks(q2))))


def gqa_attention(q, k, v):
    b = q.shape[0]
    grp = GQA_HEADS // GQA_KV_HEADS
    scale = GQA_HEAD_DIM ** -0.5

    def block(qb):
        qg = qb.reshape(b, Q_BLOCK, GQA_KV_HEADS, grp, GQA_HEAD_DIM)
        s = jnp.einsum('bqhgd,bkhd->bhgqk', qg, k, preferred_element_type=jnp.float32) * scale
        p = jax.nn.softmax(s, axis=-1)
        o = jnp.einsum('bhgqk,bkhd->bqhgd', p.astype(v.dtype), v)
        return o.reshape(b, Q_BLOCK, GQA_HEADS, GQA_HEAD_DIM)

    return from_blocks(lax.map(block, to_blocks(q)))


def attention_mixer(x, ctx, cos, sin, mod, mod_ctx, p, layer_idx):
    b, s, _ = x.shape
    n_ctx = ctx.shape[1]
    shift, scale, gate = mod
    shift_c, scale_c = mod_ctx
    h = modulate(rms_norm(x, p['norm_mix']), shift, scale)
    hc = modulate(rms_norm(ctx, p['norm_mix']), shift_c, scale_c)
    w_in = p['w_in']
    proj = h @ w_in
    ctx_a = hc @ w_in[:, OFF_AK:OFF_BQ]
    ctx_b = hc @ w_in[:, OFF_BK:IN_WIDTH]

    q_a = apply_rope(proj[..., OFF_AQ:OFF_AK].reshape(b, s, 2 * DIFF_HEADS, DIFF_HEAD_DIM), cos, sin)
    q_a = q_a.reshape(b, s, DIFF_HEADS, 2, DIFF_HEAD_DIM)
    k_a = apply_rope(proj[..., OFF_AK:OFF_AV].reshape(b, s, 2 * DIFF_HEADS, DIFF_HEAD_DIM), cos, sin)
    k_a = k_a.reshape(b, s, DIFF_HEADS, 2, DIFF_HEAD_DIM)
    v_a = proj[..., OFF_AV:OFF_BQ].reshape(b, s, DIFF_HEADS, 2 * DIFF_HEAD_DIM)
    kc_a = ctx_a[..., :A_K_W].reshape(b, n_ctx, DIFF_HEADS, 2, DIFF_HEAD_DIM)
    vc_a = ctx_a[..., A_K_W:].reshape(b, n_ctx, DIFF_HEADS, 2 * DIFF_HEAD_DIM)
    k1 = jnp.concatenate([kc_a[..., 0, :], k_a[..., 0, :]], axis=1)
    k2 = jnp.concatenate([kc_a[..., 1, :], k_a[..., 1, :]], axis=1)
    v_all_a = jnp.concatenate([vc_a, v_a], axis=1)
    lam_init = 0.8 - 0.6 * math.exp(-0.3 * layer_idx)
    f32 = jnp.float32
    lam = (jnp.exp(jnp.sum(p['lambda_q1'].astype(f32) * p['lambda_k1'].astype(f32)))
           - jnp.exp(jnp.sum(p['lambda_q2'].astype(f32) * p['lambda_k2'].astype(f32))) + lam_init)
    o_a = diff_attention(q_a[..., 0, :], q_a[..., 1, :], k1, k2, v_all_a, lam)
    o_a = (rms_norm(o_a, p['subln']) * (1.0 - lam_init)).reshape(b, s, DIFF_HEADS * 2 * DIFF_HEAD_DIM)

    q_b = rms_norm(proj[..., OFF_BQ:OFF_BK].reshape(b, s, GQA_HEADS, GQA_HEAD_DIM), p['q_norm'])
    q_b = apply_rope(q_b, cos, sin)
    k_b = rms_norm(proj[..., OFF_BK:OFF_BV].reshape(b, s, GQA_KV_HEADS, GQA_HEAD_DIM), p['k_norm'])
    k_b = apply_rope(k_b, cos, sin)
    v_b = proj[..., OFF_BV:IN_WIDTH].reshape(b, s, GQA_KV_HEADS, GQA_HEAD_DIM)
    kc_b = rms_norm(ctx_b[..., :B_K_W].reshape(b, n_ctx, GQA_KV_HEADS, GQA_HEAD_DIM), p['k_norm'])
    vc_b = ctx_b[..., B_K_W:].reshape(b, n_ctx, GQA_KV_HEADS, GQA_HEAD_DIM)
    o_b = gqa_attention(q_b, jnp.concatenate([kc_b, k_b], axis=1), jnp.concatenate([vc_b, v_b], axis=1))
    o_b = o_b.reshape(b, s, GQA_HEADS * GQA_HEAD_DIM)

    y = jnp.concatenate([o_a, o_b], axis=-1) @ p['w_out']
    return x + gate * y


def fourier_mixer(x, mod, p):
    b, s, d = x.shape
    shift, scale, gate = mod
    h = modulate(rms_norm(x, p['norm_mix']), shift, scale)
    hg = h.reshape(b, s, FOURIER_GROUPS, FOURIER_GROUP_DIM).astype(jnp.float32)
    f = jnp.fft.fft2(hg, axes=(1, 3), norm='ortho').real.astype(h.dtype).reshape(b, s, d)
    return x + gate * (f @ p['w_out'])


def swiglu_ffn(x, mod, p):
    shift, scale, gate = mod
    h = modulate(rms_norm(x, p['norm_ffn']), shift, scale)
    g, u = jnp.split(h @ p['w_gate_up'], 2, axis=-1)
    return x + gate * ((jax.nn.silu(g) * u) @ p['w_down'])


def setup_inputs(seed: int = 0) -> dict:
    key = jax.random.key(seed)
    ks = iter(jax.random.split(key, 32))

    def nrm(shape, scale):
        return jax.random.normal(next(ks), shape, jnp.float32) * scale

    def gain(n):
        return 1.0 + nrm((n,), 0.05)

    fan = D_MODEL ** -0.5
    inp = {
        'x': nrm((BATCH, SEQ, D_MODEL), 1.0),
        'c': nrm((BATCH, D_MODEL), 1.0),
        'ctx': nrm((BATCH, CTX_LEN, D_MODEL), 1.0),
        'c_ctx': nrm((D_MODEL,), 1.0),
        'l0_ada_w': nrm((D_MODEL, N_MOD * D_MODEL), 0.5 * fan),
        'l0_ada_b': nrm((N_MOD * D_MODEL,), 0.02),
        'l0_norm_mix': gain(D_MODEL),
        'l0_w_in': nrm((D_MODEL, IN_WIDTH), fan),
        'l0_lambda_q1': nrm((DIFF_HEAD_DIM,), 0.1),
        'l0_lambda_k1': nrm((DIFF_HEAD_DIM,), 0.1),
        'l0_lambda_q2': nrm((DIFF_HEAD_DIM,), 0.1),
        'l0_lambda_k2': nrm((DIFF_HEAD_DIM,), 0.1),
        'l0_subln': gain(2 * DIFF_HEAD_DIM),
        'l0_q_norm': gain(GQA_HEAD_DIM),
        'l0_k_norm': gain(GQA_HEAD_DIM),
        'l0_w_out': nrm((MIX_WIDTH, D_MODEL), MIX_WIDTH ** -0.5),
        'l0_norm_ffn': gain(D_MODEL),
        'l0_w_gate_up': nrm((D_MODEL, 2 * D_FF), fan),
        'l0_w_down': nrm((D_FF, D_MODEL), D_FF ** -0.5),
        'l1_ada_w': nrm((D_MODEL, N_MOD * D_MODEL), 0.5 * fan),
        'l1_ada_b': nrm((N_MOD * D_MODEL,), 0.02),
        'l1_norm_mix': gain(D_MODEL),
        'l1_w_out': nrm((D_MODEL, D_MODEL), fan),
        'l1_norm_ffn': gain(D_MODEL),
        'l1_w_gate_up': nrm((D_MODEL, 2 * D_FF), fan),
        'l1_w_down': nrm((D_FF, D_MODEL), D_FF ** -0.5),
        'final_norm': gain(D_MODEL),
    }
    return inp


def reference(x, c, ctx, c_ctx,
              l0_ada_w, l0_ada_b, l0_norm_mix, l0_w_in, l0_lambda_q1, l0_lambda_k1, l0_lambda_q2, l0_lambda_k2,
              l0_subln, l0_q_norm, l0_k_norm, l0_w_out, l0_norm_ffn, l0_w_gate_up, l0_w_down,
              l1_ada_w, l1_ada_b, l1_norm_mix, l1_w_out, l1_norm_ffn, l1_w_gate_up, l1_w_down,
              final_norm):
    layers = [
        dict(ada_w=l0_ada_w, ada_b=l0_ada_b, norm_mix=l0_norm_mix, w_in=l0_w_in,
             lambda_q1=l0_lambda_q1, lambda_k1=l0_lambda_k1, lambda_q2=l0_lambda_q2, lambda_k2=l0_lambda_k2,
             subln=l0_subln, q_norm=l0_q_norm, k_norm=l0_k_norm, w_out=l0_w_out,
             norm_ffn=l0_norm_ffn, w_gate_up=l0_w_gate_up, w_down=l0_w_down),
        dict(ada_w=l1_ada_w, ada_b=l1_ada_b, norm_mix=l1_norm_mix, w_out=l1_w_out,
             norm_ffn=l1_norm_ffn, w_gate_up=l1_w_gate_up, w_down=l1_w_down),
    ]
    cos, sin = rope_tables(x.shape[1])
    for l in range(DEPTH):
        p = layers[l]
        mod = ada_params(c, p['ada_w'], p['ada_b'])
        if l % 2 == 0:
            mod_c = ada_params(c_ctx[None], p['ada_w'], p['ada_b'])
            x = attention_mixer(x, ctx, cos, sin, mod[0:3], (mod_c[0], mod_c[1]), p, l)
        else:
            x = fourier_mixer(x, mod[0:3], p)
        x = swiglu_ffn(x, mod[3:6], p)
    return rms_norm(x, final_norm)
```

```python
import contextlib
import numpy as np
import concourse.bass as bass
import concourse.mybir as mybir
from concourse.bass_utils import run_bass_kernel_spmd

F32 = mybir.dt.float32
BF16 = mybir.dt.bfloat16
ALU = mybir.AluOpType
AF = mybir.ActivationFunctionType
AX = mybir.AxisListType


class Tok:
    __slots__ = ("name", "w", "r", "rd")

    def __init__(self, name=""):
        self.name = name
        self.w = None
        self.r = {}
        self.rd = []


class _Op:
    __slots__ = ("emit", "waits", "dwaits", "dma", "snap")

    def __init__(self, emit, dma=None):
        self.emit = emit
        self.waits = []
        self.dwaits = []
        self.dma = dma
        self.snap = None


class Sched:
    STREAMS = ("pe", "act", "dve", "pool", "sp")
    KQ = {"sp": 12, "act": 2, "pool": 36}

    def __init__(self):
        self.ops = {s: [] for s in self.STREAMS}
        self.known = {s: {} for s in self.STREAMS}
        self.dknown = {s: set() for s in self.STREAMS}
        self.dmas = []
        self.qcount = {q: 0 for q in self.KQ}
        self.out_dmas = []

    def add(self, s, emit, reads=(), writes=(), dma=False, extra=(), dextra=()):
        ops = self.ops[s]
        idx = len(ops)
        deps = {}
        ddeps = set(dextra)
        for st, i in extra:
            if i >= 0 and deps.get(st, -1) < i:
                deps[st] = i

        def dep(st, i):
            if deps.get(st, -1) < i:
                deps[st] = i

        for t in reads:
            if t.w is not None:
                if t.w[0] == "dma":
                    ddeps.add(t.w[1])
                else:
                    dep(*t.w)
        for t in writes:
            if t.w is not None:
                if t.w[0] == "dma":
                    ddeps.add(t.w[1])
                elif dma or t.w[0] != s:
                    dep(*t.w)
            for st, i in t.r.items():
                if dma or st != s:
                    dep(st, i)
            for d in t.rd:
                ddeps.add(d)

        dma_id = None
        if dma:
            dma_id = len(self.dmas)
            n = self.qcount[s]
            self.qcount[s] += 1
            self.dmas.append((s, n))
        op = _Op(emit, dma=dma_id)
        kn = self.known[s]
        for st, i in sorted(deps.items()):
            if kn.get(st, -1) >= i:
                continue
            op.waits.append((st, i))
            sn = self.ops[st][i].snap
            for a, b in sn.items():
                if kn.get(a, -1) < b:
                    kn[a] = b
            if kn.get(st, -1) < i:
                kn[st] = i
        dk = self.dknown[s]
        for d in sorted(ddeps):
            if d in dk:
                continue
            dk.add(d)
            op.dwaits.append(d)
        op.snap = dict(kn)
        ops.append(op)
        me = ("dma", dma_id) if dma else (s, idx)
        for t in reads:
            if dma:
                t.rd.append(dma_id)
            else:
                t.r[s] = idx
        for t in writes:
            t.w = me
            t.r = {}
            t.rd = []
        return dma_id

    def dma(self, q, out, in_, reads=(), writes=(), is_out=False):
        def emit(e, out=out, in_=in_):
            return e.dma_start(out=out, in_=in_)
        d = self.add(q, emit, reads, writes, dma=True)
        if is_out:
            self.out_dmas.append(d)
        return d

    def barrier(self):
        last = []
        for st in ("pe", "act", "dve", "pool"):
            i = len(self.ops[st]) - 1
            while i >= 0 and (self.ops[st][i].dma is not None or self.ops[st][i].emit is None):
                i -= 1
            last.append((st, i))
        alld = range(len(self.dmas))
        self.add("pool", lambda e: e.memset(self.bar_ap, 0.0), extra=last, dextra=alld)
        bidx = len(self.ops["pool"]) - 1
        for st in ("pe", "act", "dve", "sp"):
            self.add(st, None, extra=[("pool", bidx)])
        for st in self.STREAMS:
            self.dknown[st].update(alld)

    def finish(self):
        op = _Op(None)
        op.dwaits = list(self.out_dmas)
        op.snap = {}
        self.ops["sp"].append(op)

    def emit_all(self, nc):
        needed = {s: set() for s in self.STREAMS}
        for s in self.STREAMS:
            for op in self.ops[s]:
                for st, i in op.waits:
                    needed[st].add(i)
        for s in self.STREAMS:
            for i in needed[s]:
                o = self.ops[s][i]
                assert o.dma is None and o.emit is not None, ("wait on non-compute op", s, i)
        rank = {}
        for s in self.STREAMS:
            rank[s] = {i: k + 1 for k, i in enumerate(sorted(needed[s]))}
        with contextlib.ExitStack() as es:
            psem = {s: es.enter_context(nc.semaphore("prog_" + s))
                    for s in ("pe", "act", "dve", "pool")}
            qsem = {q: [es.enter_context(nc.semaphore("dq_%s_%d" % (q, k)))
                        for k in range(K)] for q, K in self.KQ.items()}
            block = es.enter_context(nc.Block())

            def replay(s, e):
                for idx, op in enumerate(self.ops[s]):
                    for st, i in op.waits:
                        e.wait_ge(psem[st], rank[st][i])
                    for d in op.dwaits:
                        q, n = self.dmas[d]
                        K = self.KQ[q]
                        e.wait_ge(qsem[q][n % K], 16 * (n // K + 1))
                    if op.emit is None:
                        continue
                    if op.dma is not None:
                        q, n = self.dmas[op.dma]
                        K = self.KQ[q]
                        if n >= K:
                            e.wait_ge(qsem[q][n % K], 16 * (n // K))
                        ins = op.emit(e)
                        ins.then_inc(qsem[q][n % K], 16)
                    else:
                        ins = op.emit(e)
                        if idx in rank[s]:
                            ins.then_inc(psem[s], 1)

            @block.tensor
            def _(e):
                replay("pe", e)

            @block.scalar
            def _(e):
                replay("act", e)

            @block.vector
            def _(e):
                replay("dve", e)

            @block.gpsimd
            def _(e):
                replay("pool", e)

            @block.sync
            def _(e):
                replay("sp", e)
D = 1024
SEQ = 2048
NCTX = 256
NT = SEQ // 128
NKT = (SEQ + NCTX) // 128
DFF = 2816
NF = DFF // 128
EPS = 1e-6
INW = 2304
LAM_INIT0 = 0.8 - 0.6 * 1.0


class Arena:
    def __init__(self, nc, nbytes):
        self.t = nc.alloc_sbuf_tensor("arena", [128, nbytes // 4], F32)
        self.off = 0
        self.cap = nbytes
        self.peak = 0

    def _a(self, nb):
        o = self.off
        self.off += (nb + 31) // 32 * 32
        self.peak = max(self.peak, self.off)
        assert self.off <= self.cap, ("SBUF arena overflow", self.off, self.cap)
        return o

    def f32(self, n):
        o = self._a(n * 4) // 4
        return self.t[:, o:o + n]

    def b16(self, n):
        o = self._a(n * 2) // 4
        return self.t[:, o:o + n // 2].bitcast(BF16)

    def mark(self):
        return self.off

    def release(self, m):
        self.off = m


def v3(ap, a):
    return ap.rearrange("p (a b) -> p a b", a=a)


def v4(ap, a, b):
    return ap.rearrange("p (a b c) -> p a b c", a=a, b=b)


class B:
    def __init__(self, S):
        self.S = S

    def mm(self, out, lhsT, rhs, start, stop, r, w):
        self.S.add("pe", lambda e: e.matmul(out, lhsT=lhsT, rhs=rhs, start=start, stop=stop), r, w)

    def tr(self, out, in_, ident, r, w):
        self.S.add("pe", lambda e: e.transpose(out=out, in_=in_, identity=ident), r, w)

    def act(self, out, in_, func, r, w, scale=1.0, accum_out=None):
        if accum_out is None:
            self.S.add("act", lambda e: e.activation(out=out, in_=in_, func=func, scale=scale), r, w)
        else:
            self.S.add("act", lambda e: e.activation(out=out, in_=in_, func=func, scale=scale,
                                                      accum_out=accum_out), r, w)

    def tt(self, eng, out, in0, in1, op, r, w):
        self.S.add(eng, lambda e: e.tensor_tensor(out=out, in0=in0, in1=in1, op=op), r, w)

    def ts(self, eng, out, in0, s1, op0, r, w, s2=None, op1=None):
        if op1 is None:
            self.S.add(eng, lambda e: e.tensor_scalar(out=out, in0=in0, scalar1=s1, scalar2=None, op0=op0), r, w)
        else:
            self.S.add(eng, lambda e: e.tensor_scalar(out=out, in0=in0, scalar1=s1, scalar2=s2,
                                                      op0=op0, op1=op1), r, w)

    def stt(self, eng, out, in0, scalar, in1, op0, op1, r, w):
        self.S.add(eng, lambda e: e.scalar_tensor_tensor(out=out, in0=in0, scalar=scalar, in1=in1,
                                                         op0=op0, op1=op1), r, w)

    def cp(self, eng, out, in_, r, w):
        if eng == "act":
            self.S.add("act", lambda e: e.copy(out=out, in_=in_), r, w)
        else:
            self.S.add(eng, lambda e: e.tensor_copy(out=out, in_=in_), r, w)

    def red(self, eng, out, in_, r, w):
        self.S.add(eng, lambda e: e.tensor_reduce(out=out, in_=in_, axis=AX.X, op=ALU.add), r, w)

    def recip(self, out, in_, r, w):
        self.S.add("dve", lambda e: e.reciprocal(out=out, in_=in_), r, w)

    def memset(self, eng, ap, val, r, w):
        self.S.add(eng, lambda e: e.memset(ap, val), r, w)

    def rstd(self, out, ss, tmp, inv_n, r, w):
        t = [Tok()]
        self.ts("dve", tmp[0], ss, inv_n, ALU.mult, r, t, s2=EPS, op1=ALU.add)
        self.act(tmp[1], tmp[0], AF.Ln, t, t)
        self.act(out, tmp[1], AF.Exp, t, w, scale=-0.5)


def build_program(debug=False, nphase=6, ada1_in_attn=True, ada0_in_p1=True):
    nc = bass.Bass("TRN2", target_bir_lowering=False)

    def din(name, shape, dt=F32):
        return nc.dram_tensor(name, shape, dt, kind="ExternalInput").ap()

    x_in = din("x", [SEQ, D])
    ctx_in = din("ctx", [NCTX, D])
    cT_in = din("cT", [128, 16])
    ident_in = din("ident", [128, 128], BF16)
    ropeC_in = din("ropeC", [128, NT * 32])
    ropeS_in = din("ropeS", [128, NT * 32])
    cs128_in = din("cs128", [128, 256], BF16)
    dftc_in = din("dftc", [4, 128, NT * 512], BF16)
    dfts_in = din("dfts", [4, 128, NT * 512], BF16)
    W = {}
    for l in (0, 1):
        W["ada_w%d" % l] = din("l%d_ada_w" % l, [D, 6 * D])
        W["ada_b%d" % l] = din("l%d_ada_b" % l, [1, 6 * D])
        W["norm_mix%d" % l] = din("l%d_norm_mix" % l, [1, D])
        W["norm_ffn%d" % l] = din("l%d_norm_ffn" % l, [1, D])
        W["w_gu%d" % l] = din("l%d_w_gate_up" % l, [D, 2 * DFF])
        W["w_d%d" % l] = din("l%d_w_down" % l, [DFF, D])
        W["w_out%d" % l] = din("l%d_w_out" % l, [D, D])
    w_in = din("l0_w_in", [D, INW])
    lam_in = din("l0_lambda", [1, 256])
    subln_in = din("l0_subln", [1, 128])
    qn_in = din("l0_q_norm", [1, 64])
    kn_in = din("l0_k_norm", [1, 64])
    fin_in = din("final_norm", [1, D])
    okind = "ExternalOutput" if debug else "Internal"
    out_d = nc.dram_tensor("out", [SEQ, D], F32, kind="ExternalOutput").ap()
    x1_d = nc.dram_tensor("x1s", [SEQ, D], F32, kind=okind).ap()
    x2_d = nc.dram_tensor("x2s", [SEQ, D], F32, kind=okind).ap()
    x3_d = nc.dram_tensor("x3s", [SEQ, D], F32, kind=okind).ap()
    mod_d = nc.dram_tensor("mods", [2, 2, 6 * D], F32, kind=okind).ap()
    mix_d = nc.dram_tensor("mixd", [SEQ, D], BF16, kind="ExternalOutput").ap() if debug else None
    T_x1d, T_x2d, T_x3d = [[Tok() for _ in range(NT)] for _ in range(3)]
    T_outd = [Tok() for _ in range(NT)]
    T_modd = [Tok(), Tok()]

    S = Sched()
    b = B(S)
    A = Arena(nc, 212736)
    bpairs = [nc.alloc_psum_tensor("bpair%d" % i, [128, 1024], F32) for i in range(4)]
    banks = [bpairs[i // 2][:, (i % 2) * 512:(i % 2 + 1) * 512] for i in range(8)]
    T_bank = [Tok() for _ in range(8)]

    def bank16(i):
        return banks[i][:, 0:512].bitcast(BF16)

    S.bar_ap = A.f32(8)
    ident = A.b16(128); T_ident = Tok()
    S.dma("sp", ident, ident_in, writes=[T_ident])

    class Rot:
        def __init__(self, n, mk):
            self.bufs = [mk() for _ in range(n)]
            self.toks = [Tok() for _ in range(n)]
            self.i = 0

        def next(self):
            k = self.i % len(self.bufs)
            self.i += 1
            return self.bufs[k], self.toks[k]

    stat = Rot(6, lambda: A.f32(4))

    def norm_stats(xt, T_xt, junk, T_junk, ms_eng="pool"):
        st, T_st = stat.next()
        b.memset(ms_eng, st[:, 0:1], 0.0, [], [T_st])
        b.act(junk, xt, AF.Square, [T_xt, T_st], [T_junk, T_st], accum_out=st[:, 0:1])
        b.rstd(st[:, 3:4], st[:, 0:1], [st[:, 1:2], st[:, 2:3]], 1.0 / D, [T_st], [T_st])
        return st[:, 3:4], T_st

    def ada_chunks(l, CW):
        cT = A.f32(16); T_cT = Tok()
        S.dma("sp", cT, cT_in, writes=[T_cT])
        sc = A.b16(16); T_sc = Tok()
        b.act(sc, cT, AF.Silu, [T_cT], [T_sc])
        sc3 = v3(sc, 8)
        nch = 6 * D // CW
        wch = Rot(2, lambda: A.b16(8 * CW))
        bias_r = Rot(2, lambda: A.f32(CW))
        rows_r = Rot(2, lambda: A.f32(CW))
        aw = W["ada_w%d" % l].rearrange("(k p) n -> p k n", p=128)
        loaded = {}

        def load(j):
            if j >= nch or j in loaded:
                return
            wb, T_wb = wch.next()
            S.dma("pool", v3(wb, 8), aw[:, :, j * CW:(j + 1) * CW], writes=[T_wb])
            bs, T_bs = bias_r.next()
            S.dma("sp", bs[0:2, :], W["ada_b%d" % l][:, j * CW:(j + 1) * CW].partition_broadcast(2), writes=[T_bs])
            loaded[j] = (wb, T_wb, bs, T_bs)

        def chunk(j, bk):
            load(j)
            load(j + 1)
            wb, T_wb, bs, T_bs = loaded.pop(j)
            for k in range(8):
                b.mm(banks[bk][0:2, 0:CW], sc3[:, k, :], v3(wb, 8)[:, k, :], k == 0, k == 7,
                     [T_sc, T_wb], [T_bank[bk]])
            rw, T_rw = rows_r.next()
            b.tt("dve", rw[0:2, :], banks[bk][0:2, 0:CW], bs[0:2, :], ALU.add, [T_bank[bk], T_bs], [T_rw])
            S.dma("sp", mod_d[l, :, j * CW:(j + 1) * CW], rw[0:2, :], reads=[T_rw], writes=[T_modd[l]])

        return [(lambda bk, j=j: chunk(j, bk)) for j in range(nch)]

    def ada_phase(l):
        m0 = A.mark()
        for j, ch in enumerate(ada_chunks(l, 512)):
            ch(j % 2)
        S.barrier()
        A.release(m0)

    def load_mod(dst, T_dst, l, row, j, n=1):
        S.dma("sp", dst, mod_d[l, row:row + 1, j * D:(j + n) * D].partition_broadcast(128),
              reads=[T_modd[l]], writes=[T_dst])

    def load_gain(gain_ap):
        g = A.f32(D); T_g = Tok()
        S.dma("sp", g, gain_ap.partition_broadcast(128), writes=[T_g])
        return g, T_g

    def make_AB(l, row, which, g, T_g):
        ab = A.f32(2 * D); T_ab = Tok()
        load_mod(ab, T_ab, l, row, 3 * which, 2)
        ab3 = v3(ab, 2)
        b.stt("dve", ab3[:, 1, :], ab3[:, 1, :], 1.0, g, ALU.add, ALU.mult, [T_ab, T_g], [T_ab])
        return ab3[:, 1, :], ab3[:, 0, :], T_ab

    def pipeline_gen(n, stages):
        tot = n + max(sk for _, sk in stages)
        for step in range(tot):
            for fn, sk in stages:
                i = step - sk
                if 0 <= i < n:
                    fn(i)
            yield step

    def pipeline(n, stages):
        for _ in pipeline_gen(n, stages):
            pass

    class NormPipe:
        def __init__(self, load, ab_fn, add_eng, ms_eng="pool", nx=3):
            self.load, self.ab_fn, self.add_eng, self.ms_eng = load, ab_fn, add_eng, ms_eng
            self.xts = Rot(nx, lambda: A.f32(D))
            self.junks = Rot(1, lambda: A.b16(D))
            self.hbs = Rot(2, lambda: A.b16(D))
            self.tmps = Rot(2, lambda: A.f32(D))
            self.s1, self.s2 = {}, {}

        def S1(self, t):
            xt, T_xt = self.xts.next()
            self.load(t, xt, T_xt)
            jk, T_jk = self.junks.next()
            rstd, T_st = norm_stats(xt, T_xt, jk, T_jk, self.ms_eng)
            self.s1[t] = (xt, T_xt, rstd, T_st)

        def S2(self, t):
            xt, T_xt, rstd, T_st = self.s1.pop(t)
            Aap, Bap, T_ab = self.ab_fn(t)
            hb, T_hb = self.hbs.next()
            tmp, T_tmp = self.tmps.next()
            b.stt("dve", tmp, xt, rstd, Aap, ALU.mult, ALU.mult, [T_xt, T_st, T_ab], [T_tmp])
            b.tt(self.add_eng, hb, tmp, Bap, ALU.add, [T_tmp, T_ab], [T_hb])
            self.s2[t] = (hb, T_hb)

    def norm_mod(xt, T_xt, Aap, Bap, T_ab, hb, T_hb, tmp, T_tmp, add_eng="pool"):
        rstd, T_st = norm_stats(xt, T_xt, hb, T_hb)
        b.stt("dve", tmp, xt, rstd, Aap, ALU.mult, ALU.mult, [T_xt, T_st, T_ab], [T_tmp])
        b.tt(add_eng, hb, tmp, Bap, ALU.add, [T_tmp, T_ab], [T_hb])

    def attention_layer():
        m_layer = A.mark()
        ada0 = ada_chunks(0, 128) if ada0_in_p1 else []
        for j in range(16 if ada0 else 0):
            ada0.pop(0)(j % 2)
        qTa = A.b16(4 * SEQ); kTa = A.b16(4 * NKT * 128)
        qTb = A.b16(4 * SEQ); kTb = A.b16(2 * NKT * 128)
        Va = A.b16(NKT * 4 * 129); Vb = A.b16(NKT * 2 * 65)
        qTa3, kTa3, qTb3, kTb3 = v3(qTa, 4), v3(kTa, 4), v3(qTb, 4), v3(kTb, 2)
        Va4, Vb4 = v4(Va, NKT, 4), v4(Vb, NKT, 2)
        T_qk6 = [[Tok() for _ in range(6)] for _ in range(NKT)]
        T_v1 = Tok()
        b.memset("pool", Va, 1.0, [], [T_v1])
        b.memset("pool", Vb, 1.0, [], [T_v1])
        ropeC = A.f32(NT * 32); ropeS = A.f32(NT * 32); T_rope = Tok()
        S.dma("sp", ropeC, ropeC_in, writes=[T_rope])
        S.dma("sp", ropeS, ropeS_in, writes=[T_rope])
        ropeC3, ropeS3 = v3(ropeC, NT), v3(ropeS, NT)
        gq = A.f32(64); gk = A.f32(64); T_g = Tok()
        S.dma("sp", gq, qn_in.partition_broadcast(128), writes=[T_g])
        S.dma("sp", gk, kn_in.partition_broadcast(128), writes=[T_g])

        m1 = A.mark()
        gm, T_gm = load_gain(W["norm_mix0"])
        Aap, Bap, T_ab = make_AB(0, 0, 0, gm, T_gm)
        Acp, Bcp, T_abc = make_AB(0, 1, 0, gm, T_gm)
        win = A.b16(8 * INW); win3 = v3(win, 8)
        T_win = [Tok() for _ in range(5)]
        groups = [(0, 512), (512, 512), (1024, 512), (1536, 512), (2048, 256)]
        w_in_v = w_in.rearrange("(k p) n -> p k n", p=128)
        for gi in (1, 2, 4, 0, 3):
            c0, cw = groups[gi]
            S.dma("pool", win3[:, :, c0:c0 + cw], w_in_v[:, :, c0:c0 + cw], writes=[T_win[gi]])
        def p1_load(t, xt, T_xt):
            src = ctx_in[t * 128:(t + 1) * 128, :] if t < 2 else x_in[(t - 2) * 128:(t - 1) * 128, :]
            S.dma("sp", xt, src, writes=[T_xt])

        npipe = NormPipe(p1_load, lambda t: (Acp, Bcp, T_abc) if t < 2 else (Aap, Bap, T_ab), "pool", nx=2)
        hTs = Rot(2, lambda: A.b16(D))
        hT_of = {}
        stg = Rot(2, lambda: A.b16(512 * 3 + 256))
        rt = Rot(4, lambda: A.f32(4 * 256))
        xn_r = Rot(2, lambda: A.f32(512))
        sq_r = Rot(2, lambda: A.f32(512))
        st8 = Rot(4, lambda: A.f32(4 * 8))

        def rope(src3, dst3, nh, tl, r, T_dst):
            tmp, T_tmp = rt.next()
            t = v4(tmp[:, 0:4 * nh * 32], 4, nh)
            Cb = ropeC3[:, tl, :].unsqueeze(1).broadcast_to([128, nh, 32])
            Sb = ropeS3[:, tl, :].unsqueeze(1).broadcast_to([128, nh, 32])
            x1, x2 = src3[:, :, 0:32], src3[:, :, 32:64]
            b.tt("dve", t[:, 0, 0:nh, :], x1, Cb, ALU.mult, r + [T_rope], [T_tmp])
            b.tt("dve", t[:, 1, 0:nh, :], x2, Sb, ALU.mult, r + [T_rope], [T_tmp])
            b.tt("dve", t[:, 2, 0:nh, :], x2, Cb, ALU.mult, r + [T_rope], [T_tmp])
            b.tt("dve", t[:, 3, 0:nh, :], x1, Sb, ALU.mult, r + [T_rope], [T_tmp])
            b.tt("pool", dst3[:, :, 0:32], t[:, 0, 0:nh, :], t[:, 1, 0:nh, :], ALU.subtract, [T_tmp], [T_dst])
            b.tt("pool", dst3[:, :, 32:64], t[:, 2, 0:nh, :], t[:, 3, 0:nh, :], ALU.add, [T_tmp], [T_dst])

        def hn_stats(src3, nh, r):
            sq, T_sq = sq_r.next()
            xn, T_xn = xn_r.next()
            s8, T_s8 = st8.next()
            s84 = v3(s8, 4)
            sq3 = v3(sq[:, 0:nh * 64], nh)
            xn3 = v3(xn[:, 0:nh * 64], nh)
            b.cp("act", xn3, src3, r, [T_xn])
            b.act(sq3, src3, AF.Square, r, [T_sq])
            b.red("dve", s84[:, 0, 0:nh], sq3, [T_sq], [T_s8])
            return (xn3, T_xn, s84, T_s8, nh)

        def hn_rstd(hn):
            xn3, T_xn, s84, T_s8, nh = hn
            b.rstd(s84[:, 3, 0:nh], s84[:, 0, 0:nh], [s84[:, 1, 0:nh], s84[:, 2, 0:nh]], 1.0 / 64, [T_s8], [T_s8])

        def hn_apply(hn, gain):
            xn3, T_xn, s84, T_s8, nh = hn
            b.tt("dve", xn3, xn3, s84[:, 3, 0:nh].unsqueeze(2).broadcast_to([128, nh, 64]), ALU.mult,
                 [T_xn, T_s8], [T_xn])
            b.tt("pool", xn3, xn3, gain.unsqueeze(1).broadcast_to([128, nh, 64]), ALU.mult, [T_xn, T_g], [T_xn])
            return xn3, T_xn

        def p1_S3(t):
            hb, T_hb = npipe.s2.pop(t)
            hT, T_hT = hTs.next()
            for k in range(8):
                b.tr(bank16(0)[:, k * 128:(k + 1) * 128], hb[:, k * 128:(k + 1) * 128], ident,
                     [T_hb, T_ident], [T_bank[0]])
            b.cp("act", hT, bank16(0), [T_bank[0]], [T_hT])
            hT_of[t] = (hT, T_hT)

        stg_of = {}

        def p1_S4(t):
            is_ctx = t < 2
            tl = t - 2
            hT, T_hT = hT_of.pop(t)
            hT3 = v3(hT, 8)
            gl = (1, 4, 2) if is_ctx else (0, 1, 3, 4, 2)
            for gi in gl:
                c0, cw = groups[gi]
                for k in range(8):
                    b.mm(banks[1 + gi][:, 0:cw], hT3[:, k, :], win3[:, k, c0:c0 + cw], k == 0, k == 7,
                         [T_hT, T_win[gi]], [T_bank[1 + gi]])
            sg, T_sg = stg.next()
            stg_of[t] = (sg, T_sg)
            qa_tm, ka_tm, qb_tm, kb_tm = sg[:, 0:512], sg[:, 512:1024], sg[:, 1024:1536], sg[:, 1536:1792]
            if is_ctx:
                b.cp("dve", ka_tm, banks[2][:, 0:512], [T_bank[2]], [T_sg])
            else:
                rope(v3(banks[1][:, 0:512], 8), v3(qa_tm, 8), 8, tl, [T_bank[1]], T_sg)
                rope(v3(banks[2][:, 0:512], 8), v3(ka_tm, 8), 8, tl, [T_bank[2]], T_sg)
            hnq = None
            if not is_ctx:
                hnq = hn_stats(v3(banks[4][:, 0:512], 8), 8, [T_bank[4]])
            hnk = hn_stats(v3(banks[5][:, 0:128], 2), 2, [T_bank[5]])
            b.cp("act", Vb4[:, t, :, 0:64], v3(banks[5][:, 128:256], 2), [T_bank[5], T_v1], [T_qk6[t][5]])
            if hnq is not None:
                hn_rstd(hnq)
            hn_rstd(hnk)
            b.cp("act", Va4[:, t, :, 0:128], v3(banks[3][:, 0:512], 4), [T_bank[3], T_v1], [T_qk6[t][4]])
            if hnq is not None:
                xn3, T_xn = hn_apply(hnq, gq)
                rope(xn3, v3(qb_tm, 8), 8, tl, [T_xn], T_sg)
            xk3, T_xk = hn_apply(hnk, gk)
            kb4 = v4(kb_tm, 2, 2)
            if is_ctx:
                b.cp("dve", kb4[:, :, 0, :], xk3, [T_xk], [T_sg])
            else:
                rope(xk3, kb4[:, :, 0, :], 2, tl, [T_xk], T_sg)
            b.cp("pool", kb4[:, :, 1, :], kb4[:, :, 0, :], [T_sg], [T_sg])

        def p1_S5(t):
            is_ctx = t < 2
            tl = t - 2
            sg, T_sg = stg_of.pop(t)
            qa_tm, ka_tm, qb_tm, kb_tm = sg[:, 0:512], sg[:, 512:1024], sg[:, 1024:1536], sg[:, 1536:1792]
            if not is_ctx:
                for c in range(4):
                    b.tr(bank16(6)[:, c * 128:(c + 1) * 128], qa_tm[:, c * 128:(c + 1) * 128], ident,
                         [T_sg, T_ident], [T_bank[6]])
            for c in range(4):
                b.tr(bank16(6)[:, 512 + c * 128:512 + (c + 1) * 128], ka_tm[:, c * 128:(c + 1) * 128], ident,
                     [T_sg, T_ident], [T_bank[6]])
            if not is_ctx:
                for c in range(4):
                    b.tr(bank16(7)[:, c * 128:(c + 1) * 128], qb_tm[:, c * 128:(c + 1) * 128], ident,
                         [T_sg, T_ident], [T_bank[7]])
            for c in range(2):
                b.tr(bank16(7)[:, 512 + c * 128:512 + (c + 1) * 128], kb_tm[:, c * 128:(c + 1) * 128], ident,
                     [T_sg, T_ident], [T_bank[7]])
            if not is_ctx:
                b.cp("act", qTa3[:, :, tl * 128:(tl + 1) * 128], v3(bank16(6)[:, 0:512], 4), [T_bank[6]], [T_qk6[t][0]])
                b.cp("dve", qTb3[:, :, tl * 128:(tl + 1) * 128], v3(bank16(7)[:, 0:512], 4), [T_bank[7]], [T_qk6[t][2]])
            b.cp("act", kTa3[:, :, t * 128:(t + 1) * 128], v3(bank16(6)[:, 512:1024], 4), [T_bank[6]], [T_qk6[t][1]])
            b.cp("dve", kTb3[:, :, t * 128:(t + 1) * 128], v3(bank16(7)[:, 512:768], 2), [T_bank[7]], [T_qk6[t][3]])

        for _ in pipeline_gen(NKT, [(npipe.S1, 0), (npipe.S2, 1), (p1_S3, 2), (p1_S4, 3), (p1_S5, 4)]):
            for _k in range(2):
                if ada0:
                    ada0.pop(0)(2)
        while ada0:
            ada0.pop(0)(2)
        S.barrier()
        A.release(m1)

        G1 = A.f32(D); T_G1 = Tok()
        load_mod(G1, T_G1, 0, 0, 2)
        wo = A.b16(8 * D); T_wo = Tok()
        S.dma("pool", v3(wo, 8), W["w_out0"].rearrange("(k p) n -> p k n", p=128), writes=[T_wo])
        wo3 = v3(wo, 8)
        lamv = A.f32(256); T_lam = Tok()
        S.dma("sp", lamv, lam_in.partition_broadcast(128), writes=[T_lam])
        lst = A.f32(8)
        lam4 = v3(lamv, 4)
        b.tt("dve", lam4[:, 0, :], lam4[:, 0, :], lam4[:, 1, :], ALU.mult, [T_lam], [T_lam])
        b.tt("dve", lam4[:, 2, :], lam4[:, 2, :], lam4[:, 3, :], ALU.mult, [T_lam], [T_lam])
        b.red("dve", lst[:, 0:1], lam4[:, 0, :], [T_lam], [T_lam])
        b.red("dve", lst[:, 1:2], lam4[:, 2, :], [T_lam], [T_lam])
        b.act(lst[:, 2:4], lst[:, 0:2], AF.Exp, [T_lam], [T_lam])
        b.tt("dve", lst[:, 4:5], lst[:, 3:4], lst[:, 2:3], ALU.subtract, [T_lam], [T_lam])
        b.ts("dve", lst[:, 5:6], lst[:, 4:5], -LAM_INIT0, ALU.add, [T_lam], [T_lam])
        nlam = lst[:, 5:6]
        sub = A.f32(128); T_sub = Tok()
        S.dma("sp", sub, subln_in.partition_broadcast(128), writes=[T_sub])
        b.ts("pool", sub, sub, 1.0 - LAM_INIT0, ALU.mult, [T_sub], [T_sub])

        pslots = Rot(3, lambda: A.b16(1024))
        pslots.toks = [(Tok(), Tok()) for _ in range(3)]
        oacc_r = Rot(2, lambda: A.f32(8 * 129))
        otmp_r = Rot(2, lambda: A.f32(3 * 512))
        ost_r = Rot(2, lambda: A.f32(32))
        mix_r = Rot(2, lambda: A.b16(4 * D))
        mixT_r = Rot(2, lambda: A.b16(8 * 512))
        xts = Rot(2, lambda: A.f32(D))
        yt_r = Rot(2, lambda: A.f32(D))
        SC = 0.125
        ada1 = ada_chunks(1, 256) if ada1_in_attn else []
        pending_post = [None]

        def make_post(isA, pair, oacc, T_oa, Wd, mix3, T_mix):
            def post():
                o4 = v4(oacc[:, 0:8 * Wd], 2, 4)
                o3 = v3(oacc[:, 0:8 * Wd], 8)
                ost, T_os = ost_r.next()
                b.recip(ost[:, 0:8], o3[:, :, Wd - 1], [T_oa], [T_os])
                if isA:
                    otmp, T_ot = otmp_r.next()
                    u = v3(otmp[:, 0:512], 4); tt_ = v3(otmp[:, 512:1024], 4); sq = v3(otmp[:, 1024:1536], 4)
                    b.ts("dve", ost[:, 4:8], ost[:, 4:8], nlam, ALU.mult, [T_os, T_lam], [T_os])
                    b.tt("dve", u, o4[:, 0, :, 0:128], ost[:, 0:4].unsqueeze(2).broadcast_to([128, 4, 128]),
                         ALU.mult, [T_oa, T_os], [T_ot])
                    b.tt("dve", tt_, o4[:, 1, :, 0:128], ost[:, 4:8].unsqueeze(2).broadcast_to([128, 4, 128]),
                         ALU.mult, [T_oa, T_os], [T_ot])
                    b.tt("pool", u, u, tt_, ALU.add, [T_ot], [T_ot])
                    b.tt("pool", sq, u, u, ALU.mult, [T_ot], [T_ot])
                    b.red("dve", ost[:, 8:12], sq, [T_ot], [T_os])
                    b.rstd(ost[:, 20:24], ost[:, 8:12], [ost[:, 12:16], ost[:, 16:20]], 1.0 / 128, [T_os], [T_os])
                    b.tt("dve", u, u, ost[:, 20:24].unsqueeze(2).broadcast_to([128, 4, 128]), ALU.mult,
                         [T_ot, T_os], [T_ot])
                    b.tt("pool", mix3[:, :, pair * 128:(pair + 1) * 128], u,
                         sub.unsqueeze(1).broadcast_to([128, 4, 128]), ALU.mult, [T_ot, T_sub], [T_mix])
                else:
                    for m in range(2):
                        h = 2 * (pair - 4) + m
                        b.tt("dve" if m == 0 else "pool", mix3[:, :, 512 + h * 64:512 + (h + 1) * 64],
                             o4[:, m, :, 0:64], ost[:, 4 * m:4 * m + 4].unsqueeze(2).broadcast_to([128, 4, 64]),
                             ALU.mult, [T_oa, T_os], [T_mix])
            return post

        for qb in range(4):
            mix, T_mix = mix_r.next()
            mix3 = v3(mix, 4)
            T_q = [tk for j in range(4) for tk in T_qk6[2 + qb * 4 + j]]
            for pair in range(8):
                isA = pair < 4
                if isA:
                    qT3, kT3, ch, kch, Wd = qTa3, kTa3, pair, pair, 129
                else:
                    qT3, kT3, ch, kch, Wd = qTb3, kTb3, pair - 4, (pair - 4) // 2, 65
                pend = None
                if ada1:
                    ada1.pop(0)(4)
                for kt in range(NKT + 1):
                    if kt == 5 and pending_post[0] is not None:
                        pending_post[0]()
                        pending_post[0] = None
                    cur = None
                    if kt < NKT:
                        cur = []
                        ps2, T_ps2 = pslots.next()
                        for m in range(2):
                            sb = 2 * (kt % 2) + m
                            lo = 64 * m
                            b.mm(banks[sb][:, :], kT3[lo:lo + 64, kch, kt * 128:(kt + 1) * 128],
                                 qT3[lo:lo + 64, ch, qb * 512:(qb + 1) * 512], True, True,
                                 T_q + T_qk6[kt], [T_bank[sb]])
                            T_pm = T_ps2[m]
                            b.act(ps2[:, m * 512:(m + 1) * 512], banks[sb][:, :], AF.Exp, [T_bank[sb]], [T_pm], scale=SC)
                            cur.append((ps2[:, m * 512:(m + 1) * 512], T_pm))
                    if pend is not None:
                        pk, pl = pend
                        for m in range(2):
                            ps, T_ps = pl[m]
                            if isA:
                                rhs = Va4[:, pk, pair, :]
                            else:
                                rhs = Vb4[:, pk, kch, :]
                            for j in range(4):
                                ab_ = 4 + 2 * m + j // 2
                                b.mm(banks[ab_][:, (j % 2) * Wd:(j % 2) * Wd + Wd], ps[:, j * 128:(j + 1) * 128], rhs,
                                     pk == 0 and j % 2 == 0, pk == NKT - 1 and j % 2 == 1, [T_ps, T_v1] + T_qk6[pk], [T_bank[ab_]])
                    pend = (kt, cur) if cur is not None else None
                oacc, T_oa = oacc_r.next()
                o4 = v4(oacc[:, 0:8 * Wd], 2, 4)
                for m in range(2):
                    for hf in range(2):
                        ab_ = 4 + 2 * m + hf
                        b.cp("dve", o4[:, m, 2 * hf:2 * hf + 2, :],
                             v3(banks[ab_][:, 0:2 * Wd], 2), [T_bank[ab_]], [T_oa])
                pending_post[0] = make_post(isA, pair, oacc, T_oa, Wd, mix3, T_mix)
            pending_post[0]()
            pending_post[0] = None
            if debug:
                for j in range(4):
                    tl = qb * 4 + j
                    S.dma("sp", mix_d[tl * 128:(tl + 1) * 128, :], mix3[:, j, :], reads=[T_mix], is_out=True)
            mixT, T_mT = mixT_r.next()
            mixT3 = v3(mixT, 8)
            for j in range(4):
                bk = j % 2
                for k in range(8):
                    b.tr(bank16(bk)[:, k * 128:(k + 1) * 128], mix3[:, j, k * 128:(k + 1) * 128], ident,
                         [T_mix, T_ident], [T_bank[bk]])
                b.cp("act" if j % 2 == 0 else "dve", mixT3[:, :, j * 128:(j + 1) * 128], v3(bank16(bk), 8),
                     [T_bank[bk]], [T_mT])
            for j in range(4):
                tl = qb * 4 + j
                xt, T_xt = xts.next()
                S.dma("sp", xt, x_in[tl * 128:(tl + 1) * 128, :], writes=[T_xt])
                yt, T_yt = yt_r.next()
                for cg in range(2):
                    bk = 2 + cg
                    for k in range(8):
                        b.mm(banks[bk][:, :], mixT3[:, k, j * 128:(j + 1) * 128], wo3[:, k, cg * 512:(cg + 1) * 512],
                             k == 0, k == 7, [T_mT, T_wo], [T_bank[bk]])
                    b.tt("dve", yt[:, cg * 512:(cg + 1) * 512], banks[bk][:, :], G1[:, cg * 512:(cg + 1) * 512],
                         ALU.mult, [T_bank[bk], T_G1], [T_yt])
                b.tt("pool", yt, yt, xt, ALU.add, [T_yt, T_xt], [T_yt])
                S.dma("pool", x1_d[tl * 128:(tl + 1) * 128, :], yt, reads=[T_yt], writes=[T_x1d[tl]])
        while ada1:
            ada1.pop(0)(4)
        S.barrier()
        A.release(m_layer)

    def ffn_layer(l, src_d, T_src, dst_d, T_dst, final):
        m0 = A.mark()
        gf, T_gf = load_gain(W["norm_ffn%d" % l])
        Aap, Bap, T_ab = make_AB(l, 0, 1, gf, T_gf)
        G2 = A.f32(D); T_G2 = Tok()
        load_mod(G2, T_G2, l, 0, 5)
        if final:
            S.dma("sp", gf, fin_in.partition_broadcast(128), reads=[], writes=[T_gf])
        HT = 1024
        NTH = HT // 128
        wgu_v = W["w_gu%d" % l].rearrange("(k p) n -> p k n", p=128)
        wd_v = W["w_d%d" % l].rearrange("(f p) n -> p f n", p=128)
        ring = Rot(3, lambda: A.b16(8 * 256))
        chunks = [(h, f) for h in range(2) for f in range(NF)]
        ring_of = {}

        def ring_load(ci):
            if ci >= len(chunks):
                return
            h, f = chunks[ci]
            wb, T_wb = ring.next()
            wb3 = v3(wb, 8)
            S.dma("pool", wb3[:, :, 0:128], wgu_v[:, :, f * 128:(f + 1) * 128], writes=[T_wb])
            S.dma("pool", wb3[:, :, 128:256], wgu_v[:, :, DFF + f * 128:DFF + (f + 1) * 128], writes=[T_wb])
            ring_of[ci] = (wb3, T_wb)

        ring_load(0)
        ring_load(1)
        wd = A.b16(NF * D); wd3 = v3(wd, NF)
        NG = 11
        T_wd = [Tok() for _ in range(NG)]
        for g in range(NG):
            S.dma("pool", wd3[:, 2 * g:2 * g + 2, :], wd_v[:, 2 * g:2 * g + 2, :], writes=[T_wd[g]])
        h2T_r = Rot(2, lambda: A.b16(8 * HT))
        actT = A.b16(NF * HT); actT3 = v3(actT, NF); T_act = [Tok() for _ in range(NF)]
        sg_r = Rot(2, lambda: A.b16(512))
        cur_half = [0]

        def f_load(t, xt, T_xt):
            tl = cur_half[0] * NTH + t
            S.dma("sp", xt, src_d[tl * 128:(tl + 1) * 128, :], reads=[T_src[tl]], writes=[T_xt])

        npipe = NormPipe(f_load, lambda t: (Aap, Bap, T_ab), "pool", ms_eng="dve")
        xts, tmps, hbs = npipe.xts, npipe.tmps, npipe.hbs

        def prep_gen(h):
            h2T, T_h2 = h2T_r.next()
            h2T3 = v3(h2T, 8)

            def S3(t):
                hb, T_hb = npipe.s2.pop(t)
                for k in range(8):
                    b.tr(bank16(6)[:, k * 128:(k + 1) * 128], hb[:, k * 128:(k + 1) * 128], ident,
                         [T_hb, T_ident], [T_bank[6]])
                b.cp("act", h2T3[:, :, t * 128:(t + 1) * 128], v3(bank16(6), 8), [T_bank[6]], [T_h2])

            def gen():
                cur_half[0] = h
                yield from pipeline_gen(NTH, [(npipe.S1, 0), (npipe.S2, 1), (S3, 2)])
            return (h2T3, T_h2), gen()

        nxt, g0 = prep_gen(0)
        for _ in g0:
            pass
        ci = 0
        for h in range(2):
            h2T3, T_h2 = nxt
            pg = None
            for f in range(NF):
                ring_load(ci + 2)
                wb3, T_wb = ring_of.pop(ci)
                ci += 1
                for nb in range(2):
                    gb, ub = nb, 2 + nb
                    for k in range(8):
                        b.mm(banks[gb][:, :], wb3[:, k, 0:128], h2T3[:, k, nb * 512:(nb + 1) * 512], k == 0, k == 7,
                             [T_wb, T_h2], [T_bank[gb]])
                    for k in range(8):
                        b.mm(banks[ub][:, :], wb3[:, k, 128:256], h2T3[:, k, nb * 512:(nb + 1) * 512], k == 0, k == 7,
                             [T_wb, T_h2], [T_bank[ub]])
                    sg, T_sg = sg_r.next()
                    b.act(sg, banks[gb][:, :], AF.Silu, [T_bank[gb]], [T_sg])
                    b.tt("dve", actT3[:, f, nb * 512:(nb + 1) * 512], banks[ub][:, :], sg, ALU.mult,
                         [T_bank[ub], T_sg], [T_act[f]])
                if h == 0:
                    if f == 6:
                        nxt, pg = prep_gen(1)
                    if pg is not None:
                        next(pg, None)
            if pg is not None:
                for _ in pg:
                    pass
            for j in range(NTH):
                tl = h * NTH + j
                xt, T_xt = xts.next()
                S.dma("sp", xt, src_d[tl * 128:(tl + 1) * 128, :], reads=[T_src[tl]], writes=[T_xt])
                yt, T_yt = tmps.next()
                for cg in range(2):
                    bk = 4 + 2 * (j % 2) + cg
                    if bk == 6:
                        bk = 7 if False else 6
                    for f in range(NF):
                        b.mm(banks[bk][:, :], actT3[:, f, j * 128:(j + 1) * 128], wd3[:, f, cg * 512:(cg + 1) * 512],
                             f == 0, f == NF - 1, [T_act[f], T_wd[f // 2]], [T_bank[bk]])
                    b.tt("dve", yt[:, cg * 512:(cg + 1) * 512], banks[bk][:, :], G2[:, cg * 512:(cg + 1) * 512],
                         ALU.mult, [T_bank[bk], T_G2], [T_yt])
                b.tt("dve", yt, yt, xt, ALU.add, [T_yt, T_xt], [T_yt])
                if not final:
                    S.dma("pool", dst_d[tl * 128:(tl + 1) * 128, :], yt, reads=[T_yt], writes=[T_dst[tl]])
                else:
                    hbj, T_hbj = hbs.next()
                    rstd, T_st = norm_stats(yt, T_yt, hbj, T_hbj, "dve")
                    b.stt("dve", xt, yt, rstd, gf, ALU.mult, ALU.mult, [T_yt, T_st, T_gf], [T_xt])
                    S.dma("pool", dst_d[tl * 128:(tl + 1) * 128, :], xt, reads=[T_xt], writes=[T_dst[tl]],
                          is_out=True)
        S.barrier()
        A.release(m0)

    def fourier_layer():
        m0 = A.mark()
        gm, T_gm = load_gain(W["norm_mix1"])
        Aap, Bap, T_ab = make_AB(1, 0, 0, gm, T_gm)
        G1 = A.f32(D); T_G1 = Tok()
        load_mod(G1, T_G1, 1, 0, 2)
        cs = A.b16(256); T_cs = Tok()
        S.dma("sp", cs, cs128_in, writes=[T_cs])
        wo = A.b16(8 * D); T_wo = Tok()
        S.dma("pool", v3(wo, 8), W["w_out1"].rearrange("(k p) n -> p k n", p=128), writes=[T_wo])
        wo3 = v3(wo, 8)
        Y1 = A.b16(NT * D); Y2 = A.b16(NT * D)
        Y13, Y23 = v3(Y1, NT), v3(Y2, NT)
        T_Y = [Tok() for _ in range(NT)]
        T_Y2 = [Tok() for _ in range(NT)]
        dft_r = Rot(2, lambda: (A.b16(NT * 512), A.b16(NT * 512)))
        fT_r = Rot(1, lambda: A.b16(8 * 512))
        def f_load(t, xt, T_xt):
            S.dma("sp", xt, x2_d[t * 128:(t + 1) * 128, :], reads=[T_x2d[t]], writes=[T_xt])

        npipe = NormPipe(f_load, lambda t: (Aap, Bap, T_ab), "pool")
        xts, tmps = npipe.xts, npipe.tmps
        hTs = Rot(2, lambda: A.b16(D))
        hT_of = {}

        def f_S3(t):
            hb, T_hb = npipe.s2.pop(t)
            hT, T_hT = hTs.next()
            for k in range(8):
                b.tr(bank16(0)[:, k * 128:(k + 1) * 128], hb[:, k * 128:(k + 1) * 128], ident,
                     [T_hb, T_ident], [T_bank[0]])
            b.cp("act", hT, bank16(0), [T_bank[0]], [T_hT])
            hT_of[t] = (hT, T_hT)

        def f_S4(t):
            hT, T_hT = hT_of.pop(t)
            hT3 = v3(hT, 8)
            for g in range(8):
                bk = 1 + g // 2
                b.mm(banks[bk][:, (g % 2) * 256:(g % 2) * 256 + 256], hT3[:, g, :], cs, True, True,
                     [T_hT, T_cs], [T_bank[bk]])
            for q in range(4):
                bk = 1 + q
                bv = v3(banks[bk][:, :], 2)
                eng = "act" if q % 2 == 0 else "dve"
                b.cp(eng, v3(Y13[:, t, q * 256:(q + 1) * 256], 2), bv[:, :, 0:128], [T_bank[bk]], [T_Y[t]])
                b.cp(eng, v3(Y23[:, t, q * 256:(q + 1) * 256], 2), bv[:, :, 128:256], [T_bank[bk]], [T_Y2[t]])

        pipeline(NT, [(npipe.S1, 0), (npipe.S2, 1), (f_S3, 2), (f_S4, 3)])
        for ub in range(4):
            (dc, ds), T_d = dft_r.next()
            S.dma("sp", dc, dftc_in[ub], writes=[T_d])
            S.dma("sp", ds, dfts_in[ub], writes=[T_d])
            dc3, ds3 = v3(dc, NT), v3(ds, NT)
            fT, T_fT = fT_r.next()
            fT3 = v3(fT, 8)
            for n in range(8):
                bk = 5 + n % 2
                for st in range(NT):
                    b.mm(banks[bk][:, :], Y13[:, st, n * 128:(n + 1) * 128], dc3[:, st, :], st == 0, False,
                         [T_Y[st], T_d], [T_bank[bk]])
                for st in range(NT):
                    b.mm(banks[bk][:, :], Y23[:, st, n * 128:(n + 1) * 128], ds3[:, st, :], False, st == NT - 1,
                         [T_Y2[st], T_d], [T_bank[bk]])
                b.act(fT3[:, n, :], banks[bk][:, :], AF.Copy, [T_bank[bk]], [T_fT], scale=1.0 / 512)
            for j in range(4):
                t = ub * 4 + j
                xt, T_xt = xts.next()
                S.dma("sp", xt, x2_d[t * 128:(t + 1) * 128, :], reads=[T_x2d[t]], writes=[T_xt])
                yt, T_yt = tmps.next()
                for cg in range(2):
                    bk = 1 + cg
                    for k in range(8):
                        b.mm(banks[bk][:, :], fT3[:, k, j * 128:(j + 1) * 128], wo3[:, k, cg * 512:(cg + 1) * 512],
                             k == 0, k == 7, [T_fT, T_wo], [T_bank[bk]])
                    b.tt("dve", yt[:, cg * 512:(cg + 1) * 512], banks[bk][:, :], G1[:, cg * 512:(cg + 1) * 512],
                         ALU.mult, [T_bank[bk], T_G1], [T_yt])
                b.tt("pool", yt, yt, xt, ALU.add, [T_yt, T_xt], [T_yt])
                S.dma("pool", x3_d[t * 128:(t + 1) * 128, :], yt, reads=[T_yt], writes=[T_x3d[t]])
        S.barrier()
        A.release(m0)

    phases = [(lambda: None) if ada0_in_p1 else (lambda: ada_phase(0)), (lambda: None) if ada1_in_attn else (lambda: ada_phase(1)), attention_layer,
              lambda: ffn_layer(0, x1_d, T_x1d, x2_d, T_x2d, False), fourier_layer,
              lambda: ffn_layer(1, x3_d, T_x3d, out_d, T_outd, True)]
    for ph in phases[:nphase]:
        ph()
    S.finish()
    S.emit_all(nc)
    return nc, A.peak
_CACHE = {}


def _consts():
    if "c" in _CACHE:
        return _CACHE["c"]
    import ml_dtypes
    bf = ml_dtypes.bfloat16
    ident = np.eye(128, dtype=np.float32).astype(bf)
    tok = np.arange(SEQ)
    row = (tok // 64).astype(np.float32)
    col = (tok % 64).astype(np.float32)
    inv = (np.float32(10000.0) ** (-np.arange(16, dtype=np.float32) / np.float32(16))).astype(np.float32)
    ang = np.concatenate([row[:, None] * inv[None], col[:, None] * inv[None]], axis=1).astype(np.float32)
    C = np.cos(ang).astype(np.float32).reshape(NT, 128, 32).transpose(1, 0, 2).reshape(128, NT * 32)
    Sn = np.sin(ang).astype(np.float32).reshape(NT, 128, 32).transpose(1, 0, 2).reshape(128, NT * 32)
    cv = (np.arange(128)[:, None] * np.arange(128)[None, :]) % 128
    a128 = 2.0 * np.pi * cv / 128.0
    cs128 = np.concatenate([np.cos(a128), np.sin(a128)], axis=1).astype(np.float32).astype(bf)
    s_idx = np.arange(SEQ).reshape(NT, 128)
    u_idx = np.arange(SEQ).reshape(4, 512)
    prod = (s_idx[None, :, :, None] * u_idx[:, None, None, :]) % SEQ
    angs = (2.0 * np.pi / SEQ) * prod.astype(np.float64)
    dftc = np.cos(angs).transpose(0, 2, 1, 3).reshape(4, 128, NT * 512).astype(np.float32).astype(bf)
    dfts = (-np.sin(angs)).transpose(0, 2, 1, 3).reshape(4, 128, NT * 512).astype(np.float32).astype(bf)
    _CACHE["c"] = dict(ident=ident, ropeC=np.ascontiguousarray(C), ropeS=np.ascontiguousarray(Sn),
                       cs128=cs128, dftc=np.ascontiguousarray(dftc), dfts=np.ascontiguousarray(dfts))
    return _CACHE["c"]


def _perm64():
    new = np.zeros(64, dtype=np.int64)
    for h in range(2):
        for a in range(2):
            for p in range(16):
                new[h * 32 + a * 16 + p] = a * 32 + h * 16 + p
    return new


def kernel(_ncores=None, _nphase=6, **inp):
    f32 = np.float32
    x = np.asarray(inp["x"], f32)
    c = np.asarray(inp["c"], f32)
    ctx = np.asarray(inp["ctx"], f32)
    c_ctx = np.asarray(inp["c_ctx"], f32)
    nb = x.shape[0] if _ncores is None else _ncores
    p64 = _perm64()
    w_in = np.asarray(inp["l0_w_in"], f32)
    cols = np.arange(INW)
    for c0, nh in ((0, 8), (512, 8), (1536, 8), (2048, 2)):
        for h in range(nh):
            cols[c0 + h * 64:c0 + (h + 1) * 64] = c0 + h * 64 + p64
    w_in_p = np.ascontiguousarray(w_in[:, cols])
    shared = dict(_consts())
    def full(name):
        return np.ascontiguousarray(np.asarray(inp[name], f32))

    def row(name):
        return np.asarray(inp[name], f32).reshape(1, -1)

    shared["l0_ada_w"] = full("l0_ada_w")
    shared["l0_ada_b"] = row("l0_ada_b")
    shared["l0_norm_mix"] = row("l0_norm_mix")
    shared["l0_norm_ffn"] = row("l0_norm_ffn")
    shared["l0_w_gate_up"] = full("l0_w_gate_up")
    shared["l0_w_down"] = full("l0_w_down")
    shared["l0_w_out"] = full("l0_w_out")
    shared["l1_ada_w"] = full("l1_ada_w")
    shared["l1_ada_b"] = row("l1_ada_b")
    shared["l1_norm_mix"] = row("l1_norm_mix")
    shared["l1_norm_ffn"] = row("l1_norm_ffn")
    shared["l1_w_gate_up"] = full("l1_w_gate_up")
    shared["l1_w_down"] = full("l1_w_down")
    shared["l1_w_out"] = full("l1_w_out")
    shared["l0_w_in"] = w_in_p
    shared["l0_lambda"] = np.concatenate([np.asarray(inp["l0_lambda_q1"], f32), np.asarray(inp["l0_lambda_k1"], f32),
                                          np.asarray(inp["l0_lambda_q2"], f32), np.asarray(inp["l0_lambda_k2"], f32)]
                                         ).reshape(1, 256)
    shared["l0_subln"] = np.asarray(inp["l0_subln"], f32).reshape(1, -1)
    shared["l0_q_norm"] = np.ascontiguousarray(np.asarray(inp["l0_q_norm"], f32)[p64]).reshape(1, -1)
    shared["l0_k_norm"] = np.ascontiguousarray(np.asarray(inp["l0_k_norm"], f32)[p64]).reshape(1, -1)
    shared["final_norm"] = np.asarray(inp["final_norm"], f32).reshape(1, -1)
    in_maps = []
    for bi in range(nb):
        m = dict(shared)
        m["x"] = np.ascontiguousarray(x[bi])
        m["ctx"] = np.ascontiguousarray(ctx[bi])
        cT = np.stack([c[bi].reshape(8, 128).T, c_ctx.reshape(8, 128).T], axis=-1)
        m["cT"] = np.ascontiguousarray(cT.reshape(128, 16))
        in_maps.append(m)
    debug = bool(_CACHE.get("debug", False))
    key = ("nc", debug, _nphase)
    if key not in _CACHE:
        _CACHE[key] = build_program(debug=debug, nphase=_nphase, ada1_in_attn=bool(_CACHE.get("ada1", True)))[0]
    nc = _CACHE[key]
    res = run_bass_kernel_spmd(nc, in_maps, core_ids=list(range(nb)))
    if debug:
        _CACHE["last_results"] = res.results
    out = np.stack([np.asarray(r["out"], f32) for r in res.results], axis=0)
    return out
```

```python
import contextlib
import numpy as np
import concourse.bass as bass
import concourse.mybir as mybir
from concourse.bass_utils import run_bass_kernel_spmd

F32 = mybir.dt.float32
BF16 = mybir.dt.bfloat16
ALU = mybir.AluOpType
AF = mybir.ActivationFunctionType
AX = mybir.AxisListType


class Tok:
    __slots__ = ("name", "w", "r", "rd")

    def __init__(self, name=""):
        self.name = name
        self.w = None
        self.r = {}
        self.rd = []


class _Op:
    __slots__ = ("emit", "waits", "dwaits", "dma", "snap")

    def __init__(self, emit, dma=None):
        self.emit = emit
        self.waits = []
        self.dwaits = []
        self.dma = dma
        self.snap = None


class Sched:
    STREAMS = ("pe", "act", "dve", "pool", "sp")
    KQ = {"sp": 12, "act": 2, "pool": 36}

    def __init__(self):
        self.ops = {s: [] for s in self.STREAMS}
        self.known = {s: {} for s in self.STREAMS}
        self.dknown = {s: set() for s in self.STREAMS}
        self.dmas = []
        self.qcount = {q: 0 for q in self.KQ}
        self.out_dmas = []

    def add(self, s, emit, reads=(), writes=(), dma=False, extra=(), dextra=()):
        ops = self.ops[s]
        idx = len(ops)
        deps = {}
        ddeps = set(dextra)
        for st, i in extra:
            if i >= 0 and deps.get(st, -1) < i:
                deps[st] = i

        def dep(st, i):
            if deps.get(st, -1) < i:
                deps[st] = i

        for t in reads:
            if t.w is not None:
                if t.w[0] == "dma":
                    ddeps.add(t.w[1])
                else:
                    dep(*t.w)
        for t in writes:
            if t.w is not None:
                if t.w[0] == "dma":
                    ddeps.add(t.w[1])
                elif dma or t.w[0] != s:
                    dep(*t.w)
            for st, i in t.r.items():
                if dma or st != s:
                    dep(st, i)
            for d in t.rd:
                ddeps.add(d)

        dma_id = None
        if dma:
            dma_id = len(self.dmas)
            n = self.qcount[s]
            self.qcount[s] += 1
            self.dmas.append((s, n))
        op = _Op(emit, dma=dma_id)
        kn = self.known[s]
        for st, i in sorted(deps.items()):
            if kn.get(st, -1) >= i:
                continue
            op.waits.append((st, i))
            sn = self.ops[st][i].snap
            for a, b in sn.items():
                if kn.get(a, -1) < b:
                    kn[a] = b
            if kn.get(st, -1) < i:
                kn[st] = i
        dk = self.dknown[s]
        for d in sorted(ddeps):
            if d in dk:
                continue
            dk.add(d)
            op.dwaits.append(d)
        op.snap = dict(kn)
        ops.append(op)
        me = ("dma", dma_id) if dma else (s, idx)
        for t in reads:
            if dma:
                t.rd.append(dma_id)
            else:
                t.r[s] = idx
        for t in writes:
            t.w = me
            t.r = {}
            t.rd = []
        return dma_id

    def dma(self, q, out, in_, reads=(), writes=(), is_out=False):
        def emit(e, out=out, in_=in_):
            return e.dma_start(out=out, in_=in_)
        d = self.add(q, emit, reads, writes, dma=True)
        if is_out:
            self.out_dmas.append(d)
        return d

    def barrier(self):
        last = []
        for st in ("pe", "act", "dve", "pool"):
            i = len(self.ops[st]) - 1
            while i >= 0 and (self.ops[st][i].dma is not None or self.ops[st][i].emit is None):
                i -= 1
            last.append((st, i))
        alld = range(len(self.dmas))
        self.add("pool", lambda e: e.memset(self.bar_ap, 0.0), extra=last, dextra=alld)
        bidx = len(self.ops["pool"]) - 1
        for st in ("pe", "act", "dve", "sp"):
            self.add(st, None, extra=[("pool", bidx)])
        for st in self.STREAMS:
            self.dknown[st].update(alld)

    def finish(self):
        op = _Op(None)
        op.dwaits = list(self.out_dmas)
        op.snap = {}
        self.ops["sp"].append(op)

    def emit_all(self, nc):
        needed = {s: set() for s in self.STREAMS}
        for s in self.STREAMS:
            for op in self.ops[s]:
                for st, i in op.waits:
                    needed[st].add(i)
        for s in self.STREAMS:
            for i in needed[s]:
                o = self.ops[s][i]
                assert o.dma is None and o.emit is not None, ("wait on non-compute op", s, i)
        rank = {}
        for s in self.STREAMS:
            rank[s] = {i: k + 1 for k, i in enumerate(sorted(needed[s]))}
        with contextlib.ExitStack() as es:
            psem = {s: es.enter_context(nc.semaphore("prog_" + s))
                    for s in ("pe", "act", "dve", "pool")}
            qsem = {q: [es.enter_context(nc.semaphore("dq_%s_%d" % (q, k)))
                        for k in range(K)] for q, K in self.KQ.items()}
            block = es.enter_context(nc.Block())

            def replay(s, e):
                for idx, op in enumerate(self.ops[s]):
                    for st, i in op.waits:
                        e.wait_ge(psem[st], rank[st][i])
                    for d in op.dwaits:
                        q, n = self.dmas[d]
                        K = self.KQ[q]
                        e.wait_ge(qsem[q][n % K], 16 * (n // K + 1))
                    if op.emit is None:
                        continue
                    if op.dma is not None:
                        q, n = self.dmas[op.dma]
                        K = self.KQ[q]
                        if n >= K:
                            e.wait_ge(qsem[q][n % K], 16 * (n // K))
                        ins = op.emit(e)
                        ins.then_inc(qsem[q][n % K], 16)
                    else:
                        ins = op.emit(e)
                        if idx in rank[s]:
                            ins.then_inc(psem[s], 1)

            @block.tensor
            def _(e):
                replay("pe", e)

            @block.scalar
            def _(e):
                replay("act", e)

            @block.vector
            def _(e):
                replay("dve", e)

            @block.gpsimd
            def _(e):
                replay("pool", e)

            @block.sync
            def _(e):
                replay("sp", e)
D = 1024
SEQ = 2048
NCTX = 256
NT = SEQ // 128
NKT = (SEQ + NCTX) // 128
DFF = 2816
NF = DFF // 128
EPS = 1e-6
INW = 2304
LAM_INIT0 = 0.8 - 0.6 * 1.0


class Arena:
    def __init__(self, nc, nbytes):
        self.t = nc.alloc_sbuf_tensor("arena", [128, nbytes // 4], F32)
        self.off = 0
        self.cap = nbytes
        self.peak = 0

    def _a(self, nb):
        o = self.off
        self.off += (nb + 31) // 32 * 32
        self.peak = max(self.peak, self.off)
        assert self.off <= self.cap, ("SBUF arena overflow", self.off, self.cap)
        return o

    def f32(self, n):
        o = self._a(n * 4) // 4
        return self.t[:, o:o + n]

    def b16(self, n):
        o = self._a(n * 2) // 4
        return self.t[:, o:o + n // 2].bitcast(BF16)

    def mark(self):
        return self.off

    def release(self, m):
        self.off = m


def v3(ap, a):
    return ap.rearrange("p (a b) -> p a b", a=a)


def v4(ap, a, b):
    return ap.rearrange("p (a b c) -> p a b c", a=a, b=b)


class B:
    def __init__(self, S):
        self.S = S

    def mm(self, out, lhsT, rhs, start, stop, r, w):
        self.S.add("pe", lambda e: e.matmul(out, lhsT=lhsT, rhs=rhs, start=start, stop=stop), r, w)

    def tr(self, out, in_, ident, r, w):
        self.S.add("pe", lambda e: e.transpose(out=out, in_=in_, identity=ident), r, w)

    def act(self, out, in_, func, r, w, scale=1.0, accum_out=None):
        if accum_out is None:
            self.S.add("act", lambda e: e.activation(out=out, in_=in_, func=func, scale=scale), r, w)
        else:
            self.S.add("act", lambda e: e.activation(out=out, in_=in_, func=func, scale=scale,
                                                      accum_out=accum_out), r, w)

    def tt(self, eng, out, in0, in1, op, r, w):
        self.S.add(eng, lambda e: e.tensor_tensor(out=out, in0=in0, in1=in1, op=op), r, w)

    def ts(self, eng, out, in0, s1, op0, r, w, s2=None, op1=None):
        if op1 is None:
            self.S.add(eng, lambda e: e.tensor_scalar(out=out, in0=in0, scalar1=s1, scalar2=None, op0=op0), r, w)
        else:
            self.S.add(eng, lambda e: e.tensor_scalar(out=out, in0=in0, scalar1=s1, scalar2=s2,
                                                      op0=op0, op1=op1), r, w)

    def stt(self, eng, out, in0, scalar, in1, op0, op1, r, w):
        self.S.add(eng, lambda e: e.scalar_tensor_tensor(out=out, in0=in0, scalar=scalar, in1=in1,
                                                         op0=op0, op1=op1), r, w)

    def cp(self, eng, out, in_, r, w):
        if eng == "act":
            self.S.add("act", lambda e: e.copy(out=out, in_=in_), r, w)
        else:
            self.S.add(eng, lambda e: e.tensor_copy(out=out, in_=in_), r, w)

    def red(self, eng, out, in_, r, w):
        self.S.add(eng, lambda e: e.tensor_reduce(out=out, in_=in_, axis=AX.X, op=ALU.add), r, w)

    def recip(self, out, in_, r, w):
        self.S.add("dve", lambda e: e.reciprocal(out=out, in_=in_), r, w)

    def memset(self, eng, ap, val, r, w):
        self.S.add(eng, lambda e: e.memset(ap, val), r, w)

    def rstd(self, out, ss, tmp, inv_n, r, w):
        t = [Tok()]
        self.ts("dve", tmp[0], ss, inv_n, ALU.mult, r, t, s2=EPS, op1=ALU.add)
        self.act(tmp[1], tmp[0], AF.Ln, t, t)
        self.act(out, tmp[1], AF.Exp, t, w, scale=-0.5)


def build_program(debug=False, nphase=6, ada1_in_attn=True, ada0_in_p1=True):
    nc = bass.Bass("TRN2", target_bir_lowering=False)

    def din(name, shape, dt=F32):
        return nc.dram_tensor(name, shape, dt, kind="ExternalInput").ap()

    x_in = din("x", [SEQ, D])
    ctx_in = din("ctx", [NCTX, D])
    cT_in = din("cT", [128, 16])
    ident_in = din("ident", [128, 128], BF16)
    ropeC_in = din("ropeC", [128, NT * 32])
    ropeS_in = din("ropeS", [128, NT * 32])
    cs128_in = din("cs128", [128, 256], BF16)
    dftc_in = din("dftc", [4, 128, NT * 512], BF16)
    dfts_in = din("dfts", [4, 128, NT * 512], BF16)
    W = {}
    for l in (0, 1):
        W["ada_w%d" % l] = din("l%d_ada_w" % l, [D, 6 * D])
        W["ada_b%d" % l] = din("l%d_ada_b" % l, [1, 6 * D])
        W["norm_mix%d" % l] = din("l%d_norm_mix" % l, [1, D])
        W["norm_ffn%d" % l] = din("l%d_norm_ffn" % l, [1, D])
        W["w_gu%d" % l] = din("l%d_w_gate_up" % l, [D, 2 * DFF])
        W["w_d%d" % l] = din("l%d_w_down" % l, [DFF, D])
        W["w_out%d" % l] = din("l%d_w_out" % l, [D, D])
    w_in = din("l0_w_in", [D, INW])
    lam_in = din("l0_lambda", [1, 256])
    subln_in = din("l0_subln", [1, 128])
    qn_in = din("l0_q_norm", [1, 64])
    kn_in = din("l0_k_norm", [1, 64])
    fin_in = din("final_norm", [1, D])
    okind = "ExternalOutput" if debug else "Internal"
    out_d = nc.dram_tensor("out", [SEQ, D], F32, kind="ExternalOutput").ap()
    x1_d = nc.dram_tensor("x1s", [SEQ, D], F32, kind=okind).ap()
    x2_d = nc.dram_tensor("x2s", [SEQ, D], F32, kind=okind).ap()
    x3_d = nc.dram_tensor("x3s", [SEQ, D], F32, kind=okind).ap()
    mod_d = nc.dram_tensor("mods", [2, 2, 6 * D], F32, kind=okind).ap()
    mix_d = nc.dram_tensor("mixd", [SEQ, D], BF16, kind="ExternalOutput").ap() if debug else None
    T_x1d, T_x2d, T_x3d = [[Tok() for _ in range(NT)] for _ in range(3)]
    T_outd = [Tok() for _ in range(NT)]
    T_modd = [Tok(), Tok()]

    S = Sched()
    b = B(S)
    A = Arena(nc, 212736)
    bpairs = [nc.alloc_psum_tensor("bpair%d" % i, [128, 1024], F32) for i in range(4)]
    banks = [bpairs[i // 2][:, (i % 2) * 512:(i % 2 + 1) * 512] for i in range(8)]
    T_bank = [Tok() for _ in range(8)]

    def bank16(i):
        return banks[i][:, 0:512].bitcast(BF16)

    S.bar_ap = A.f32(8)
    ident = A.b16(128); T_ident = Tok()
    S.dma("sp", ident, ident_in, writes=[T_ident])

    class Rot:
        def __init__(self, n, mk):
            self.bufs = [mk() for _ in range(n)]
            self.toks = [Tok() for _ in range(n)]
            self.i = 0

        def next(self):
            k = self.i % len(self.bufs)
            self.i += 1
            return self.bufs[k], self.toks[k]

    stat = Rot(6, lambda: A.f32(4))

    def norm_stats(xt, T_xt, junk, T_junk, ms_eng="pool"):
        st, T_st = stat.next()
        b.memset(ms_eng, st[:, 0:1], 0.0, [], [T_st])
        b.act(junk, xt, AF.Square, [T_xt, T_st], [T_junk, T_st], accum_out=st[:, 0:1])
        b.rstd(st[:, 3:4], st[:, 0:1], [st[:, 1:2], st[:, 2:3]], 1.0 / D, [T_st], [T_st])
        return st[:, 3:4], T_st

    def ada_chunks(l, CW):
        cT = A.f32(16); T_cT = Tok()
        S.dma("sp", cT, cT_in, writes=[T_cT])
        sc = A.b16(16); T_sc = Tok()
        b.act(sc, cT, AF.Silu, [T_cT], [T_sc])
        sc3 = v3(sc, 8)
        nch = 6 * D // CW
        wch = Rot(2, lambda: A.b16(8 * CW))
        bias_r = Rot(2, lambda: A.f32(CW))
        rows_r = Rot(2, lambda: A.f32(CW))
        aw = W["ada_w%d" % l].rearrange("(k p) n -> p k n", p=128)
        loaded = {}

        def load(j):
            if j >= nch or j in loaded:
                return
            wb, T_wb = wch.next()
            S.dma("pool", v3(wb, 8), aw[:, :, j * CW:(j + 1) * CW], writes=[T_wb])
            bs, T_bs = bias_r.next()
            S.dma("sp", bs[0:2, :], W["ada_b%d" % l][:, j * CW:(j + 1) * CW].partition_broadcast(2), writes=[T_bs])
            loaded[j] = (wb, T_wb, bs, T_bs)

        def chunk(j, bk):
            load(j)
            load(j + 1)
            wb, T_wb, bs, T_bs = loaded.pop(j)
            for k in range(8):
                b.mm(banks[bk][0:2, 0:CW], sc3[:, k, :], v3(wb, 8)[:, k, :], k == 0, k == 7,
                     [T_sc, T_wb], [T_bank[bk]])
            rw, T_rw = rows_r.next()
            b.tt("dve", rw[0:2, :], banks[bk][0:2, 0:CW], bs[0:2, :], ALU.add, [T_bank[bk], T_bs], [T_rw])
            S.dma("sp", mod_d[l, :, j * CW:(j + 1) * CW], rw[0:2, :], reads=[T_rw], writes=[T_modd[l]])

        return [(lambda bk, j=j: chunk(j, bk)) for j in range(nch)]

    def ada_phase(l):
        m0 = A.mark()
        for j, ch in enumerate(ada_chunks(l, 512)):
            ch(j % 2)
        S.barrier()
        A.release(m0)

    def load_mod(dst, T_dst, l, row, j, n=1):
        S.dma("sp", dst, mod_d[l, row:row + 1, j * D:(j + n) * D].partition_broadcast(128),
              reads=[T_modd[l]], writes=[T_dst])

    def load_gain(gain_ap):
        g = A.f32(D); T_g = Tok()
        S.dma("sp", g, gain_ap.partition_broadcast(128), writes=[T_g])
        return g, T_g

    def make_AB(l, row, which, g, T_g):
        ab = A.f32(2 * D); T_ab = Tok()
        load_mod(ab, T_ab, l, row, 3 * which, 2)
        ab3 = v3(ab, 2)
        b.stt("dve", ab3[:, 1, :], ab3[:, 1, :], 1.0, g, ALU.add, ALU.mult, [T_ab, T_g], [T_ab])
        return ab3[:, 1, :], ab3[:, 0, :], T_ab

    def pipeline_gen(n, stages):
        tot = n + max(sk for _, sk in stages)
        for step in range(tot):
            for fn, sk in stages:
                i = step - sk
                if 0 <= i < n:
                    fn(i)
            yield step

    def pipeline(n, stages):
        for _ in pipeline_gen(n, stages):
            pass

    class NormPipe:
        def __init__(self, load, ab_fn, add_eng, ms_eng="pool", nx=3):
            self.load, self.ab_fn, self.add_eng, self.ms_eng = load, ab_fn, add_eng, ms_eng
            self.xts = Rot(nx, lambda: A.f32(D))
            self.junks = Rot(1, lambda: A.b16(D))
            self.hbs = Rot(2, lambda: A.b16(D))
            self.tmps = Rot(2, lambda: A.f32(D))
            self.s1, self.s2 = {}, {}

        def S1(self, t):
            xt, T_xt = self.xts.next()
            self.load(t, xt, T_xt)
            jk, T_jk = self.junks.next()
            rstd, T_st = norm_stats(xt, T_xt, jk, T_jk, self.ms_eng)
            self.s1[t] = (xt, T_xt, rstd, T_st)

        def S2(self, t):
            xt, T_xt, rstd, T_st = self.s1.pop(t)
            Aap, Bap, T_ab = self.ab_fn(t)
            hb, T_hb = self.hbs.next()
            tmp, T_tmp = self.tmps.next()
            b.stt("dve", tmp, xt, rstd, Aap, ALU.mult, ALU.mult, [T_xt, T_st, T_ab], [T_tmp])
            b.tt(self.add_eng, hb, tmp, Bap, ALU.add, [T_tmp, T_ab], [T_hb])
            self.s2[t] = (hb, T_hb)

    def norm_mod(xt, T_xt, Aap, Bap, T_ab, hb, T_hb, tmp, T_tmp, add_eng="pool"):
        rstd, T_st = norm_stats(xt, T_xt, hb, T_hb)
        b.stt("dve", tmp, xt, rstd, Aap, ALU.mult, ALU.mult, [T_xt, T_st, T_ab], [T_tmp])
        b.tt(add_eng, hb, tmp, Bap, ALU.add, [T_tmp, T_ab], [T_hb])

    def attention_layer():
        m_layer = A.mark()
        ada0 = ada_chunks(0, 256) if ada0_in_p1 else []
        for j in range(8 if ada0 else 0):
            ada0.pop(0)(j % 2)
        qTa = A.b16(4 * SEQ); kTa = A.b16(4 * NKT * 128)
        qTb = A.b16(4 * SEQ); kTb = A.b16(2 * NKT * 128)
        Va = A.b16(NKT * 4 * 129); Vb = A.b16(NKT * 2 * 65)
        qTa3, kTa3, qTb3, kTb3 = v3(qTa, 4), v3(kTa, 4), v3(qTb, 4), v3(kTb, 2)
        Va4, Vb4 = v4(Va, NKT, 4), v4(Vb, NKT, 2)
        T_qk6 = [[Tok() for _ in range(6)] for _ in range(NKT)]
        T_v1 = Tok()
        b.memset("pool", Va, 1.0, [], [T_v1])
        b.memset("pool", Vb, 1.0, [], [T_v1])
        ropeC = A.f32(NT * 32); ropeS = A.f32(NT * 32); T_rope = Tok()
        S.dma("sp", ropeC, ropeC_in, writes=[T_rope])
        S.dma("sp", ropeS, ropeS_in, writes=[T_rope])
        ropeC3, ropeS3 = v3(ropeC, NT), v3(ropeS, NT)
        gq = A.f32(64); gk = A.f32(64); T_g = Tok()
        S.dma("sp", gq, qn_in.partition_broadcast(128), writes=[T_g])
        S.dma("sp", gk, kn_in.partition_broadcast(128), writes=[T_g])

        m1 = A.mark()
        gm, T_gm = load_gain(W["norm_mix0"])
        Aap, Bap, T_ab = make_AB(0, 0, 0, gm, T_gm)
        Acp, Bcp, T_abc = make_AB(0, 1, 0, gm, T_gm)
        win = A.b16(8 * INW); win3 = v3(win, 8)
        T_win = [Tok() for _ in range(5)]
        groups = [(0, 512), (512, 512), (1024, 512), (1536, 512), (2048, 256)]
        w_in_v = w_in.rearrange("(k p) n -> p k n", p=128)
        for gi in (1, 2, 4, 0, 3):
            c0, cw = groups[gi]
            S.dma("pool", win3[:, :, c0:c0 + cw], w_in_v[:, :, c0:c0 + cw], writes=[T_win[gi]])
        def p1_load(t, xt, T_xt):
            src = ctx_in[t * 128:(t + 1) * 128, :] if t < 2 else x_in[(t - 2) * 128:(t - 1) * 128, :]
            S.dma("sp", xt, src, writes=[T_xt])

        npipe = NormPipe(p1_load, lambda t: (Acp, Bcp, T_abc) if t < 2 else (Aap, Bap, T_ab), "pool", nx=2)
        hTs = Rot(2, lambda: A.b16(D))
        hT_of = {}
        stg = Rot(2, lambda: A.b16(512 * 3 + 256))
        rt = Rot(2, lambda: A.f32(4 * 256))
        xn_r = Rot(2, lambda: A.f32(512))
        sq_r = Rot(2, lambda: A.f32(512))
        st8 = Rot(4, lambda: A.f32(4 * 8))

        def rope(src3, dst3, nh, tl, r, T_dst):
            tmp, T_tmp = rt.next()
            t = v4(tmp[:, 0:4 * nh * 32], 4, nh)
            Cb = ropeC3[:, tl, :].unsqueeze(1).broadcast_to([128, nh, 32])
            Sb = ropeS3[:, tl, :].unsqueeze(1).broadcast_to([128, nh, 32])
            x1, x2 = src3[:, :, 0:32], src3[:, :, 32:64]
            b.tt("dve", t[:, 0, 0:nh, :], x1, Cb, ALU.mult, r + [T_rope], [T_tmp])
            b.tt("dve", t[:, 1, 0:nh, :], x2, Sb, ALU.mult, r + [T_rope], [T_tmp])
            b.tt("dve", t[:, 2, 0:nh, :], x2, Cb, ALU.mult, r + [T_rope], [T_tmp])
            b.tt("dve", t[:, 3, 0:nh, :], x1, Sb, ALU.mult, r + [T_rope], [T_tmp])
            b.tt("pool", dst3[:, :, 0:32], t[:, 0, 0:nh, :], t[:, 1, 0:nh, :], ALU.subtract, [T_tmp], [T_dst])
            b.tt("pool", dst3[:, :, 32:64], t[:, 2, 0:nh, :], t[:, 3, 0:nh, :], ALU.add, [T_tmp], [T_dst])

        def hn_stats(src3, nh, r):
            sq, T_sq = sq_r.next()
            xn, T_xn = xn_r.next()
            s8, T_s8 = st8.next()
            s84 = v3(s8, 4)
            sq3 = v3(sq[:, 0:nh * 64], nh)
            xn3 = v3(xn[:, 0:nh * 64], nh)
            b.cp("act", xn3, src3, r, [T_xn])
            b.act(sq3, src3, AF.Square, r, [T_sq])
            b.red("dve", s84[:, 0, 0:nh], sq3, [T_sq], [T_s8])
            return (xn3, T_xn, s84, T_s8, nh)

        def hn_rstd(hn):
            xn3, T_xn, s84, T_s8, nh = hn
            b.rstd(s84[:, 3, 0:nh], s84[:, 0, 0:nh], [s84[:, 1, 0:nh], s84[:, 2, 0:nh]], 1.0 / 64, [T_s8], [T_s8])

        def hn_apply(hn, gain):
            xn3, T_xn, s84, T_s8, nh = hn
            b.tt("dve", xn3, xn3, s84[:, 3, 0:nh].unsqueeze(2).broadcast_to([128, nh, 64]), ALU.mult,
                 [T_xn, T_s8], [T_xn])
            b.tt("pool", xn3, xn3, gain.unsqueeze(1).broadcast_to([128, nh, 64]), ALU.mult, [T_xn, T_g], [T_xn])
            return xn3, T_xn

        def p1_S3(t):
            hb, T_hb = npipe.s2.pop(t)
            hT, T_hT = hTs.next()
            for k in range(8):
                b.tr(bank16(0)[:, k * 128:(k + 1) * 128], hb[:, k * 128:(k + 1) * 128], ident,
                     [T_hb, T_ident], [T_bank[0]])
            b.cp("act", hT, bank16(0), [T_bank[0]], [T_hT])
            hT_of[t] = (hT, T_hT)

        stg_of = {}

        def p1_S4(t):
            is_ctx = t < 2
            tl = t - 2
            hT, T_hT = hT_of.pop(t)
            hT3 = v3(hT, 8)
            gl = (1, 4, 2) if is_ctx else (0, 1, 3, 4, 2)
            for gi in gl:
                c0, cw = groups[gi]
                for k in range(8):
                    b.mm(banks[1 + gi][:, 0:cw], hT3[:, k, :], win3[:, k, c0:c0 + cw], k == 0, k == 7,
                         [T_hT, T_win[gi]], [T_bank[1 + gi]])
            sg, T_sg = stg.next()
            stg_of[t] = (sg, T_sg)
            qa_tm, ka_tm, qb_tm, kb_tm = sg[:, 0:512], sg[:, 512:1024], sg[:, 1024:1536], sg[:, 1536:1792]
            if is_ctx:
                b.cp("dve", ka_tm, banks[2][:, 0:512], [T_bank[2]], [T_sg])
            else:
                rope(v3(banks[1][:, 0:512], 8), v3(qa_tm, 8), 8, tl, [T_bank[1]], T_sg)
                rope(v3(banks[2][:, 0:512], 8), v3(ka_tm, 8), 8, tl, [T_bank[2]], T_sg)
            hnq = None
            if not is_ctx:
                hnq = hn_stats(v3(banks[4][:, 0:512], 8), 8, [T_bank[4]])
            hnk = hn_stats(v3(banks[5][:, 0:128], 2), 2, [T_bank[5]])
            b.cp("act", Vb4[:, t, :, 0:64], v3(banks[5][:, 128:256], 2), [T_bank[5], T_v1], [T_qk6[t][5]])
            if hnq is not None:
                hn_rstd(hnq)
            hn_rstd(hnk)
            b.cp("act", Va4[:, t, :, 0:128], v3(banks[3][:, 0:512], 4), [T_bank[3], T_v1], [T_qk6[t][4]])
            if hnq is not None:
                xn3, T_xn = hn_apply(hnq, gq)
                rope(xn3, v3(qb_tm, 8), 8, tl, [T_xn], T_sg)
            xk3, T_xk = hn_apply(hnk, gk)
            kb4 = v4(kb_tm, 2, 2)
            if is_ctx:
                b.cp("dve", kb4[:, :, 0, :], xk3, [T_xk], [T_sg])
            else:
                rope(xk3, kb4[:, :, 0, :], 2, tl, [T_xk], T_sg)
            b.cp("pool", kb4[:, :, 1, :], kb4[:, :, 0, :], [T_sg], [T_sg])

        def p1_S5(t):
            is_ctx = t < 2
            tl = t - 2
            sg, T_sg = stg_of.pop(t)
            qa_tm, ka_tm, qb_tm, kb_tm = sg[:, 0:512], sg[:, 512:1024], sg[:, 1024:1536], sg[:, 1536:1792]
            if not is_ctx:
                for c in range(4):
                    b.tr(bank16(6)[:, c * 128:(c + 1) * 128], qa_tm[:, c * 128:(c + 1) * 128], ident,
                         [T_sg, T_ident], [T_bank[6]])
            for c in range(4):
                b.tr(bank16(6)[:, 512 + c * 128:512 + (c + 1) * 128], ka_tm[:, c * 128:(c + 1) * 128], ident,
                     [T_sg, T_ident], [T_bank[6]])
            if not is_ctx:
                for c in range(4):
                    b.tr(bank16(7)[:, c * 128:(c + 1) * 128], qb_tm[:, c * 128:(c + 1) * 128], ident,
                         [T_sg, T_ident], [T_bank[7]])
            for c in range(2):
                b.tr(bank16(7)[:, 512 + c * 128:512 + (c + 1) * 128], kb_tm[:, c * 128:(c + 1) * 128], ident,
                     [T_sg, T_ident], [T_bank[7]])
            if not is_ctx:
                b.cp("act", qTa3[:, :, tl * 128:(tl + 1) * 128], v3(bank16(6)[:, 0:512], 4), [T_bank[6]], [T_qk6[t][0]])
                b.cp("dve", qTb3[:, :, tl * 128:(tl + 1) * 128], v3(bank16(7)[:, 0:512], 4), [T_bank[7]], [T_qk6[t][2]])
            b.cp("act", kTa3[:, :, t * 128:(t + 1) * 128], v3(bank16(6)[:, 512:1024], 4), [T_bank[6]], [T_qk6[t][1]])
            b.cp("dve", kTb3[:, :, t * 128:(t + 1) * 128], v3(bank16(7)[:, 512:768], 2), [T_bank[7]], [T_qk6[t][3]])

        pipeline(NKT, [(npipe.S1, 0), (npipe.S2, 1), (p1_S3, 2), (p1_S4, 3), (p1_S5, 4)])
        S.barrier()
        A.release(m1)

        G1 = A.f32(D); T_G1 = Tok()
        if not ada0:
            load_mod(G1, T_G1, 0, 0, 2)
        wo = A.b16(8 * D); T_wo = Tok()
        S.dma("pool", v3(wo, 8), W["w_out0"].rearrange("(k p) n -> p k n", p=128), writes=[T_wo])
        wo3 = v3(wo, 8)
        lamv = A.f32(256); T_lam = Tok()
        S.dma("sp", lamv, lam_in.partition_broadcast(128), writes=[T_lam])
        lst = A.f32(8)
        lam4 = v3(lamv, 4)
        b.tt("dve", lam4[:, 0, :], lam4[:, 0, :], lam4[:, 1, :], ALU.mult, [T_lam], [T_lam])
        b.tt("dve", lam4[:, 2, :], lam4[:, 2, :], lam4[:, 3, :], ALU.mult, [T_lam], [T_lam])
        b.red("dve", lst[:, 0:1], lam4[:, 0, :], [T_lam], [T_lam])
        b.red("dve", lst[:, 1:2], lam4[:, 2, :], [T_lam], [T_lam])
        b.act(lst[:, 2:4], lst[:, 0:2], AF.Exp, [T_lam], [T_lam])
        b.tt("dve", lst[:, 4:5], lst[:, 3:4], lst[:, 2:3], ALU.subtract, [T_lam], [T_lam])
        b.ts("dve", lst[:, 5:6], lst[:, 4:5], -LAM_INIT0, ALU.add, [T_lam], [T_lam])
        nlam = lst[:, 5:6]
        sub = A.f32(128); T_sub = Tok()
        S.dma("sp", sub, subln_in.partition_broadcast(128), writes=[T_sub])
        b.ts("pool", sub, sub, 1.0 - LAM_INIT0, ALU.mult, [T_sub], [T_sub])

        pslots = Rot(3, lambda: A.b16(1024))
        pslots.toks = [(Tok(), Tok()) for _ in range(3)]
        oacc_r = Rot(2, lambda: A.f32(8 * 129))
        otmp_r = Rot(2, lambda: A.f32(3 * 512))
        ost_r = Rot(2, lambda: A.f32(32))
        mix_r = Rot(2, lambda: A.b16(4 * D))
        mixT_r = Rot(2, lambda: A.b16(8 * 512))
        xts = Rot(2, lambda: A.f32(D))
        yt_r = Rot(2, lambda: A.f32(D))
        SC = 0.125
        ada1 = ada_chunks(1, 256) if ada1_in_attn else []
        pending_post = [None]

        def make_post(isA, pair, oacc, T_oa, Wd, mix3, T_mix):
            def post():
                o4 = v4(oacc[:, 0:8 * Wd], 2, 4)
                o3 = v3(oacc[:, 0:8 * Wd], 8)
                ost, T_os = ost_r.next()
                b.recip(ost[:, 0:8], o3[:, :, Wd - 1], [T_oa], [T_os])
                if isA:
                    otmp, T_ot = otmp_r.next()
                    u = v3(otmp[:, 0:512], 4); tt_ = v3(otmp[:, 512:1024], 4); sq = v3(otmp[:, 1024:1536], 4)
                    b.ts("dve", ost[:, 4:8], ost[:, 4:8], nlam, ALU.mult, [T_os, T_lam], [T_os])
                    b.tt("dve", u, o4[:, 0, :, 0:128], ost[:, 0:4].unsqueeze(2).broadcast_to([128, 4, 128]),
                         ALU.mult, [T_oa, T_os], [T_ot])
                    b.tt("dve", tt_, o4[:, 1, :, 0:128], ost[:, 4:8].unsqueeze(2).broadcast_to([128, 4, 128]),
                         ALU.mult, [T_oa, T_os], [T_ot])
                    b.tt("pool", u, u, tt_, ALU.add, [T_ot], [T_ot])
                    b.tt("pool", sq, u, u, ALU.mult, [T_ot], [T_ot])
                    b.red("dve", ost[:, 8:12], sq, [T_ot], [T_os])
                    b.rstd(ost[:, 20:24], ost[:, 8:12], [ost[:, 12:16], ost[:, 16:20]], 1.0 / 128, [T_os], [T_os])
                    b.tt("dve", u, u, ost[:, 20:24].unsqueeze(2).broadcast_to([128, 4, 128]), ALU.mult,
                         [T_ot, T_os], [T_ot])
                    b.tt("pool", mix3[:, :, pair * 128:(pair + 1) * 128], u,
                         sub.unsqueeze(1).broadcast_to([128, 4, 128]), ALU.mult, [T_ot, T_sub], [T_mix])
                else:
                    for m in range(2):
                        h = 2 * (pair - 4) + m
                        b.tt("dve" if m == 0 else "pool", mix3[:, :, 512 + h * 64:512 + (h + 1) * 64],
                             o4[:, m, :, 0:64], ost[:, 4 * m:4 * m + 4].unsqueeze(2).broadcast_to([128, 4, 64]),
                             ALU.mult, [T_oa, T_os], [T_mix])
            return post

        for qb in range(4):
            mix, T_mix = mix_r.next()
            mix3 = v3(mix, 4)
            T_q = [tk for j in range(4) for tk in T_qk6[2 + qb * 4 + j]]
            for pair in range(8):
                isA = pair < 4
                if isA:
                    qT3, kT3, ch, kch, Wd = qTa3, kTa3, pair, pair, 129
                else:
                    qT3, kT3, ch, kch, Wd = qTb3, kTb3, pair - 4, (pair - 4) // 2, 65
                pend = None
                if ada1:
                    ada1.pop(0)(4)
                if ada0:
                    ada0.pop(0)(5)
                for kt in range(NKT + 1):
                    if kt == 5 and pending_post[0] is not None:
                        pending_post[0]()
                        pending_post[0] = None
                    cur = None
                    if kt < NKT:
                        cur = []
                        ps2, T_ps2 = pslots.next()
                        for m in range(2):
                            sb = 2 * (kt % 2) + m
                            lo = 64 * m
                            b.mm(banks[sb][:, :], kT3[lo:lo + 64, kch, kt * 128:(kt + 1) * 128],
                                 qT3[lo:lo + 64, ch, qb * 512:(qb + 1) * 512], True, True,
                                 T_q + T_qk6[kt], [T_bank[sb]])
                            T_pm = T_ps2[m]
                            b.act(ps2[:, m * 512:(m + 1) * 512], banks[sb][:, :], AF.Exp, [T_bank[sb]], [T_pm], scale=SC)
                            cur.append((ps2[:, m * 512:(m + 1) * 512], T_pm))
                    if pend is not None:
                        pk, pl = pend
                        for m in range(2):
                            ps, T_ps = pl[m]
                            if isA:
                                rhs = Va4[:, pk, pair, :]
                            else:
                                rhs = Vb4[:, pk, kch, :]
                            for j in range(4):
                                ab_ = 4 + 2 * m + j // 2
                                b.mm(banks[ab_][:, (j % 2) * Wd:(j % 2) * Wd + Wd], ps[:, j * 128:(j + 1) * 128], rhs,
                                     pk == 0 and j % 2 == 0, pk == NKT - 1 and j % 2 == 1, [T_ps, T_v1] + T_qk6[pk], [T_bank[ab_]])
                    pend = (kt, cur) if cur is not None else None
                oacc, T_oa = oacc_r.next()
                o4 = v4(oacc[:, 0:8 * Wd], 2, 4)
                for m in range(2):
                    for hf in range(2):
                        ab_ = 4 + 2 * m + hf
                        b.cp("dve", o4[:, m, 2 * hf:2 * hf + 2, :],
                             v3(banks[ab_][:, 0:2 * Wd], 2), [T_bank[ab_]], [T_oa])
                pending_post[0] = make_post(isA, pair, oacc, T_oa, Wd, mix3, T_mix)
            pending_post[0]()
            pending_post[0] = None
            if qb == 0 and ada0_in_p1:
                load_mod(G1, T_G1, 0, 0, 2)
            if debug:
                for j in range(4):
                    tl = qb * 4 + j
                    S.dma("sp", mix_d[tl * 128:(tl + 1) * 128, :], mix3[:, j, :], reads=[T_mix], is_out=True)
            mixT, T_mT = mixT_r.next()
            mixT3 = v3(mixT, 8)
            for j in range(4):
                bk = j % 2
                for k in range(8):
                    b.tr(bank16(bk)[:, k * 128:(k + 1) * 128], mix3[:, j, k * 128:(k + 1) * 128], ident,
                         [T_mix, T_ident], [T_bank[bk]])
                b.cp("act" if j % 2 == 0 else "dve", mixT3[:, :, j * 128:(j + 1) * 128], v3(bank16(bk), 8),
                     [T_bank[bk]], [T_mT])
            for j in range(4):
                tl = qb * 4 + j
                xt, T_xt = xts.next()
                S.dma("sp", xt, x_in[tl * 128:(tl + 1) * 128, :], writes=[T_xt])
                yt, T_yt = yt_r.next()
                for cg in range(2):
                    bk = 2 + cg
                    for k in range(8):
                        b.mm(banks[bk][:, :], mixT3[:, k, j * 128:(j + 1) * 128], wo3[:, k, cg * 512:(cg + 1) * 512],
                             k == 0, k == 7, [T_mT, T_wo], [T_bank[bk]])
                    b.tt("dve", yt[:, cg * 512:(cg + 1) * 512], banks[bk][:, :], G1[:, cg * 512:(cg + 1) * 512],
                         ALU.mult, [T_bank[bk], T_G1], [T_yt])
                b.tt("pool", yt, yt, xt, ALU.add, [T_yt, T_xt], [T_yt])
                S.dma("pool", x1_d[tl * 128:(tl + 1) * 128, :], yt, reads=[T_yt], writes=[T_x1d[tl]])
        while ada1:
            ada1.pop(0)(4)
        while ada0:
            ada0.pop(0)(5)
        S.barrier()
        A.release(m_layer)

    def ffn_layer(l, src_d, T_src, dst_d, T_dst, final):
        m0 = A.mark()
        gf, T_gf = load_gain(W["norm_ffn%d" % l])
        Aap, Bap, T_ab = make_AB(l, 0, 1, gf, T_gf)
        G2 = A.f32(D); T_G2 = Tok()
        load_mod(G2, T_G2, l, 0, 5)
        if final:
            S.dma("sp", gf, fin_in.partition_broadcast(128), reads=[], writes=[T_gf])
        HT = 1024
        NTH = HT // 128
        wgu_v = W["w_gu%d" % l].rearrange("(k p) n -> p k n", p=128)
        wd_v = W["w_d%d" % l].rearrange("(f p) n -> p f n", p=128)
        ring = Rot(3, lambda: A.b16(8 * 256))
        chunks = [(h, f) for h in range(2) for f in range(NF)]
        ring_of = {}

        def ring_load(ci):
            if ci >= len(chunks):
                return
            h, f = chunks[ci]
            wb, T_wb = ring.next()
            wb3 = v3(wb, 8)
            S.dma("pool", wb3[:, :, 0:128], wgu_v[:, :, f * 128:(f + 1) * 128], writes=[T_wb])
            S.dma("pool", wb3[:, :, 128:256], wgu_v[:, :, DFF + f * 128:DFF + (f + 1) * 128], writes=[T_wb])
            ring_of[ci] = (wb3, T_wb)

        ring_load(0)
        ring_load(1)
        wd = A.b16(NF * D); wd3 = v3(wd, NF)
        NG = 11
        T_wd = [Tok() for _ in range(NG)]
        for g in range(NG):
            S.dma("pool", wd3[:, 2 * g:2 * g + 2, :], wd_v[:, 2 * g:2 * g + 2, :], writes=[T_wd[g]])
        h2T_r = Rot(2, lambda: A.b16(8 * HT))
        actT = A.b16(NF * HT); actT3 = v3(actT, NF); T_act = [Tok() for _ in range(NF)]
        sg_r = Rot(2, lambda: A.b16(512))
        cur_half = [0]

        def f_load(t, xt, T_xt):
            tl = cur_half[0] * NTH + t
            S.dma("sp", xt, src_d[tl * 128:(tl + 1) * 128, :], reads=[T_src[tl]], writes=[T_xt])

        npipe = NormPipe(f_load, lambda t: (Aap, Bap, T_ab), "pool", ms_eng="dve")
        xts, tmps, hbs = npipe.xts, npipe.tmps, npipe.hbs

        def prep_gen(h):
            h2T, T_h2 = h2T_r.next()
            h2T3 = v3(h2T, 8)

            def S3(t):
                hb, T_hb = npipe.s2.pop(t)
                for k in range(8):
                    b.tr(bank16(6)[:, k * 128:(k + 1) * 128], hb[:, k * 128:(k + 1) * 128], ident,
                         [T_hb, T_ident], [T_bank[6]])
                b.cp("act", h2T3[:, :, t * 128:(t + 1) * 128], v3(bank16(6), 8), [T_bank[6]], [T_h2])

            def gen():
                cur_half[0] = h
                yield from pipeline_gen(NTH, [(npipe.S1, 0), (npipe.S2, 1), (S3, 2)])
            return (h2T3, T_h2), gen()

        nxt, g0 = prep_gen(0)
        for _ in g0:
            pass
        ci = 0
        for h in range(2):
            h2T3, T_h2 = nxt
            pg = None
            for f in range(NF):
                ring_load(ci + 2)
                wb3, T_wb = ring_of.pop(ci)
                ci += 1
                for nb in range(2):
                    gb, ub = nb, 2 + nb
                    for k in range(8):
                        b.mm(banks[gb][:, :], wb3[:, k, 0:128], h2T3[:, k, nb * 512:(nb + 1) * 512], k == 0, k == 7,
                             [T_wb, T_h2], [T_bank[gb]])
                    for k in range(8):
                        b.mm(banks[ub][:, :], wb3[:, k, 128:256], h2T3[:, k, nb * 512:(nb + 1) * 512], k == 0, k == 7,
                             [T_wb, T_h2], [T_bank[ub]])
                    sg, T_sg = sg_r.next()
                    b.act(sg, banks[gb][:, :], AF.Silu, [T_bank[gb]], [T_sg])
                    b.tt("dve", actT3[:, f, nb * 512:(nb + 1) * 512], banks[ub][:, :], sg, ALU.mult,
                         [T_bank[ub], T_sg], [T_act[f]])
                if h == 0:
                    if f == 6:
                        nxt, pg = prep_gen(1)
                    if pg is not None:
                        next(pg, None)
            if pg is not None:
                for _ in pg:
                    pass
            for j in range(NTH):
                tl = h * NTH + j
                xt, T_xt = xts.next()
                S.dma("sp", xt, src_d[tl * 128:(tl + 1) * 128, :], reads=[T_src[tl]], writes=[T_xt])
                yt, T_yt = tmps.next()
                for cg in range(2):
                    bk = 4 + 2 * (j % 2) + cg
                    if bk == 6:
                        bk = 7 if False else 6
                    for f in range(NF):
                        b.mm(banks[bk][:, :], actT3[:, f, j * 128:(j + 1) * 128], wd3[:, f, cg * 512:(cg + 1) * 512],
                             f == 0, f == NF - 1, [T_act[f], T_wd[f // 2]], [T_bank[bk]])
                    b.tt("dve", yt[:, cg * 512:(cg + 1) * 512], banks[bk][:, :], G2[:, cg * 512:(cg + 1) * 512],
                         ALU.mult, [T_bank[bk], T_G2], [T_yt])
                b.tt("dve", yt, yt, xt, ALU.add, [T_yt, T_xt], [T_yt])
                if not final:
                    S.dma("pool", dst_d[tl * 128:(tl + 1) * 128, :], yt, reads=[T_yt], writes=[T_dst[tl]])
                else:
                    hbj, T_hbj = hbs.next()
                    rstd, T_st = norm_stats(yt, T_yt, hbj, T_hbj, "dve")
                    b.stt("dve", xt, yt, rstd, gf, ALU.mult, ALU.mult, [T_yt, T_st, T_gf], [T_xt])
                    S.dma("pool", dst_d[tl * 128:(tl + 1) * 128, :], xt, reads=[T_xt], writes=[T_dst[tl]],
                          is_out=True)
        S.barrier()
        A.release(m0)

    def fourier_layer():
        m0 = A.mark()
        gm, T_gm = load_gain(W["norm_mix1"])
        Aap, Bap, T_ab = make_AB(1, 0, 0, gm, T_gm)
        G1 = A.f32(D); T_G1 = Tok()
        load_mod(G1, T_G1, 1, 0, 2)
        cs = A.b16(256); T_cs = Tok()
        S.dma("sp", cs, cs128_in, writes=[T_cs])
        wo = A.b16(8 * D); T_wo = Tok()
        S.dma("pool", v3(wo, 8), W["w_out1"].rearrange("(k p) n -> p k n", p=128), writes=[T_wo])
        wo3 = v3(wo, 8)
        Y1 = A.b16(NT * D); Y2 = A.b16(NT * D)
        Y13, Y23 = v3(Y1, NT), v3(Y2, NT)
        T_Y = [Tok() for _ in range(NT)]
        T_Y2 = [Tok() for _ in range(NT)]
        dft_r = Rot(2, lambda: (A.b16(NT * 512), A.b16(NT * 512)))
        fT_r = Rot(1, lambda: A.b16(8 * 512))
        def f_load(t, xt, T_xt):
            S.dma("sp", xt, x2_d[t * 128:(t + 1) * 128, :], reads=[T_x2d[t]], writes=[T_xt])

        npipe = NormPipe(f_load, lambda t: (Aap, Bap, T_ab), "pool")
        xts, tmps = npipe.xts, npipe.tmps
        hTs = Rot(2, lambda: A.b16(D))
        hT_of = {}

        def f_S3(t):
            hb, T_hb = npipe.s2.pop(t)
            hT, T_hT = hTs.next()
            for k in range(8):
                b.tr(bank16(0)[:, k * 128:(k + 1) * 128], hb[:, k * 128:(k + 1) * 128], ident,
                     [T_hb, T_ident], [T_bank[0]])
            b.cp("act", hT, bank16(0), [T_bank[0]], [T_hT])
            hT_of[t] = (hT, T_hT)

        def f_S4(t):
            hT, T_hT = hT_of.pop(t)
            hT3 = v3(hT, 8)
            for g in range(8):
                bk = 1 + g // 2
                b.mm(banks[bk][:, (g % 2) * 256:(g % 2) * 256 + 256], hT3[:, g, :], cs, True, True,
                     [T_hT, T_cs], [T_bank[bk]])
            for q in range(4):
                bk = 1 + q
                bv = v3(banks[bk][:, :], 2)
                eng = "act" if q % 2 == 0 else "dve"
                b.cp(eng, v3(Y13[:, t, q * 256:(q + 1) * 256], 2), bv[:, :, 0:128], [T_bank[bk]], [T_Y[t]])
                b.cp(eng, v3(Y23[:, t, q * 256:(q + 1) * 256], 2), bv[:, :, 128:256], [T_bank[bk]], [T_Y2[t]])

        pipeline(NT, [(npipe.S1, 0), (npipe.S2, 1), (f_S3, 2), (f_S4, 3)])
        for ub in range(4):
            (dc, ds), T_d = dft_r.next()
            S.dma("sp", dc, dftc_in[ub], writes=[T_d])
            S.dma("sp", ds, dfts_in[ub], writes=[T_d])
            dc3, ds3 = v3(dc, NT), v3(ds, NT)
            fT, T_fT = fT_r.next()
            fT3 = v3(fT, 8)
            for n in range(8):
                bk = 5 + n % 2
                for st in range(NT):
                    b.mm(banks[bk][:, :], Y13[:, st, n * 128:(n + 1) * 128], dc3[:, st, :], st == 0, False,
                         [T_Y[st], T_d], [T_bank[bk]])
                for st in range(NT):
                    b.mm(banks[bk][:, :], Y23[:, st, n * 128:(n + 1) * 128], ds3[:, st, :], False, st == NT - 1,
                         [T_Y2[st], T_d], [T_bank[bk]])
                b.act(fT3[:, n, :], banks[bk][:, :], AF.Copy, [T_bank[bk]], [T_fT], scale=1.0 / 512)
            for j in range(4):
                t = ub * 4 + j
                xt, T_xt = xts.next()
                S.dma("sp", xt, x2_d[t * 128:(t + 1) * 128, :], reads=[T_x2d[t]], writes=[T_xt])
                yt, T_yt = tmps.next()
                for cg in range(2):
                    bk = 1 + cg
                    for k in range(8):
                        b.mm(banks[bk][:, :], fT3[:, k, j * 128:(j + 1) * 128], wo3[:, k, cg * 512:(cg + 1) * 512],
                             k == 0, k == 7, [T_fT, T_wo], [T_bank[bk]])
                    b.tt("dve", yt[:, cg * 512:(cg + 1) * 512], banks[bk][:, :], G1[:, cg * 512:(cg + 1) * 512],
                         ALU.mult, [T_bank[bk], T_G1], [T_yt])
                b.tt("pool", yt, yt, xt, ALU.add, [T_yt, T_xt], [T_yt])
                S.dma("pool", x3_d[t * 128:(t + 1) * 128, :], yt, reads=[T_yt], writes=[T_x3d[t]])
        S.barrier()
        A.release(m0)

    phases = [(lambda: None) if ada0_in_p1 else (lambda: ada_phase(0)), (lambda: None) if ada1_in_attn else (lambda: ada_phase(1)), attention_layer,
              lambda: ffn_layer(0, x1_d, T_x1d, x2_d, T_x2d, False), fourier_layer,
              lambda: ffn_layer(1, x3_d, T_x3d, out_d, T_outd, True)]
    for ph in phases[:nphase]:
        ph()
    S.finish()
    S.emit_all(nc)
    return nc, A.peak
_CACHE = {}


def _consts():
    if "c" in _CACHE:
        return _CACHE["c"]
    import ml_dtypes
    bf = ml_dtypes.bfloat16
    ident = np.eye(128, dtype=np.float32).astype(bf)
    tok = np.arange(SEQ)
    row = (tok // 64).astype(np.float32)
    col = (tok % 64).astype(np.float32)
    inv = (np.float32(10000.0) ** (-np.arange(16, dtype=np.float32) / np.float32(16))).astype(np.float32)
    ang = np.concatenate([row[:, None] * inv[None], col[:, None] * inv[None]], axis=1).astype(np.float32)
    C = np.cos(ang).astype(np.float32).reshape(NT, 128, 32).transpose(1, 0, 2).reshape(128, NT * 32)
    Sn = np.sin(ang).astype(np.float32).reshape(NT, 128, 32).transpose(1, 0, 2).reshape(128, NT * 32)
    cv = (np.arange(128)[:, None] * np.arange(128)[None, :]) % 128
    a128 = 2.0 * np.pi * cv / 128.0
    cs128 = np.concatenate([np.cos(a128), np.sin(a128)], axis=1).astype(np.float32).astype(bf)
    s_idx = np.arange(SEQ).reshape(NT, 128)
    u_idx = np.arange(SEQ).reshape(4, 512)
    prod = (s_idx[None, :, :, None] * u_idx[:, None, None, :]) % SEQ
    angs = (2.0 * np.pi / SEQ) * prod.astype(np.float64)
    dftc = np.cos(angs).transpose(0, 2, 1, 3).reshape(4, 128, NT * 512).astype(np.float32).astype(bf)
    dfts = (-np.sin(angs)).transpose(0, 2, 1, 3).reshape(4, 128, NT * 512).astype(np.float32).astype(bf)
    _CACHE["c"] = dict(ident=ident, ropeC=np.ascontiguousarray(C), ropeS=np.ascontiguousarray(Sn),
                       cs128=cs128, dftc=np.ascontiguousarray(dftc), dfts=np.ascontiguousarray(dfts))
    return _CACHE["c"]


def _perm64():
    new = np.zeros(64, dtype=np.int64)
    for h in range(2):
        for a in range(2):
            for p in range(16):
                new[h * 32 + a * 16 + p] = a * 32 + h * 16 + p
    return new


def kernel(_ncores=None, _nphase=6, **inp):
    f32 = np.float32
    x = np.asarray(inp["x"], f32)
    c = np.asarray(inp["c"], f32)
    ctx = np.asarray(inp["ctx"], f32)
    c_ctx = np.asarray(inp["c_ctx"], f32)
    nb = x.shape[0] if _ncores is None else _ncores
    p64 = _perm64()
    w_in = np.asarray(inp["l0_w_in"], f32)
    cols = np.arange(INW)
    for c0, nh in ((0, 8), (512, 8), (1536, 8), (2048, 2)):
        for h in range(nh):
            cols[c0 + h * 64:c0 + (h + 1) * 64] = c0 + h * 64 + p64
    w_in_p = np.ascontiguousarray(w_in[:, cols])
    shared = dict(_consts())
    def full(name):
        return np.ascontiguousarray(np.asarray(inp[name], f32))

    def row(name):
        return np.asarray(inp[name], f32).reshape(1, -1)

    shared["l0_ada_w"] = full("l0_ada_w")
    shared["l0_ada_b"] = row("l0_ada_b")
    shared["l0_norm_mix"] = row("l0_norm_mix")
    shared["l0_norm_ffn"] = row("l0_norm_ffn")
    shared["l0_w_gate_up"] = full("l0_w_gate_up")
    shared["l0_w_down"] = full("l0_w_down")
    shared["l0_w_out"] = full("l0_w_out")
    shared["l1_ada_w"] = full("l1_ada_w")
    shared["l1_ada_b"] = row("l1_ada_b")
    shared["l1_norm_mix"] = row("l1_norm_mix")
    shared["l1_norm_ffn"] = row("l1_norm_ffn")
    shared["l1_w_gate_up"] = full("l1_w_gate_up")
    shared["l1_w_down"] = full("l1_w_down")
    shared["l1_w_out"] = full("l1_w_out")
    shared["l0_w_in"] = w_in_p
    shared["l0_lambda"] = np.concatenate([np.asarray(inp["l0_lambda_q1"], f32), np.asarray(inp["l0_lambda_k1"], f32),
                                          np.asarray(inp["l0_lambda_q2"], f32), np.asarray(inp["l0_lambda_k2"], f32)]
                                         ).reshape(1, 256)
    shared["l0_subln"] = np.asarray(inp["l0_subln"], f32).reshape(1, -1)
    shared["l0_q_norm"] = np.ascontiguousarray(np.asarray(inp["l0_q_norm"], f32)[p64]).reshape(1, -1)
    shared["l0_k_norm"] = np.ascontiguousarray(np.asarray(inp["l0_k_norm"], f32)[p64]).reshape(1, -1)
    shared["final_norm"] = np.asarray(inp["final_norm"], f32).reshape(1, -1)
    in_maps = []
    for bi in range(nb):
        m = dict(shared)
        m["x"] = np.ascontiguousarray(x[bi])
        m["ctx"] = np.ascontiguousarray(ctx[bi])
        cT = np.stack([c[bi].reshape(8, 128).T, c_ctx.reshape(8, 128).T], axis=-1)
        m["cT"] = np.ascontiguousarray(cT.reshape(128, 16))
        in_maps.append(m)
    debug = bool(_CACHE.get("debug", False))
    key = ("nc", debug, _nphase)
    if key not in _CACHE:
        _CACHE[key] = build_program(debug=debug, nphase=_nphase, ada1_in_attn=bool(_CACHE.get("ada1", True)))[0]
    nc = _CACHE[key]
    res = run_bass_kernel_spmd(nc, in_maps, core_ids=list(range(nb)))
    if debug:
        _CACHE["last_results"] = res.results
    out = np.stack([np.asarray(r["out"], f32) for r in res.results], axis=0)
    return out
```

```python
import contextlib
import numpy as np
import concourse.bass as bass
import concourse.mybir as mybir
from concourse.bass_utils import run_bass_kernel_spmd

F32 = mybir.dt.float32
BF16 = mybir.dt.bfloat16
ALU = mybir.AluOpType
AF = mybir.ActivationFunctionType
AX = mybir.AxisListType


class Tok:
    __slots__ = ("name", "w", "r", "rd")

    def __init__(self, name=""):
        self.name = name
        self.w = None
        self.r = {}
        self.rd = []


class _Op:
    __slots__ = ("emit", "waits", "dwaits", "dma", "snap")

    def __init__(self, emit, dma=None):
        self.emit = emit
        self.waits = []
        self.dwaits = []
        self.dma = dma
        self.snap = None


class Sched:
    STREAMS = ("pe", "act", "dve", "pool", "sp")
    KQ = {"sp": 12, "act": 2, "pool": 36}

    def __init__(self):
        self.ops = {s: [] for s in self.STREAMS}
        self.known = {s: {} for s in self.STREAMS}
        self.dknown = {s: set() for s in self.STREAMS}
        self.dmas = []
        self.qcount = {q: 0 for q in self.KQ}
        self.out_dmas = []

    def add(self, s, emit, reads=(), writes=(), dma=False, extra=(), dextra=()):
        ops = self.ops[s]
        idx = len(ops)
        deps = {}
        ddeps = set(dextra)
        for st, i in extra:
            if i >= 0 and deps.get(st, -1) < i:
                deps[st] = i

        def dep(st, i):
            if deps.get(st, -1) < i:
                deps[st] = i

        for t in reads:
            if t.w is not None:
                if t.w[0] == "dma":
                    ddeps.add(t.w[1])
                else:
                    dep(*t.w)
        for t in writes:
            if t.w is not None:
                if t.w[0] == "dma":
                    ddeps.add(t.w[1])
                elif dma or t.w[0] != s:
                    dep(*t.w)
            for st, i in t.r.items():
                if dma or st != s:
                    dep(st, i)
            for d in t.rd:
                ddeps.add(d)

        dma_id = None
        if dma:
            dma_id = len(self.dmas)
            n = self.qcount[s]
            self.qcount[s] += 1
            self.dmas.append((s, n))
        op = _Op(emit, dma=dma_id)
        kn = self.known[s]
        for st, i in sorted(deps.items()):
            if kn.get(st, -1) >= i:
                continue
            op.waits.append((st, i))
            sn = self.ops[st][i].snap
            for a, b in sn.items():
                if kn.get(a, -1) < b:
                    kn[a] = b
            if kn.get(st, -1) < i:
                kn[st] = i
        dk = self.dknown[s]
        for d in sorted(ddeps):
            if d in dk:
                continue
            dk.add(d)
            op.dwaits.append(d)
        op.snap = dict(kn)
        ops.append(op)
        me = ("dma", dma_id) if dma else (s, idx)
        for t in reads:
            if dma:
                t.rd.append(dma_id)
            else:
                t.r[s] = idx
        for t in writes:
            t.w = me
            t.r = {}
            t.rd = []
        return dma_id

    def dma(self, q, out, in_, reads=(), writes=(), is_out=False):
        def emit(e, out=out, in_=in_):
            return e.dma_start(out=out, in_=in_)
        d = self.add(q, emit, reads, writes, dma=True)
        if is_out:
            self.out_dmas.append(d)
        return d

    def barrier(self):
        last = []
        for st in ("pe", "act", "dve", "pool"):
            i = len(self.ops[st]) - 1
            while i >= 0 and (self.ops[st][i].dma is not None or self.ops[st][i].emit is None):
                i -= 1
            last.append((st, i))
        alld = range(len(self.dmas))
        self.add("pool", lambda e: e.memset(self.bar_ap, 0.0), extra=last, dextra=alld)
        bidx = len(self.ops["pool"]) - 1
        for st in ("pe", "act", "dve", "sp"):
            self.add(st, None, extra=[("pool", bidx)])
        for st in self.STREAMS:
            self.dknown[st].update(alld)

    def finish(self):
        op = _Op(None)
        op.dwaits = list(self.out_dmas)
        op.snap = {}
        self.ops["sp"].append(op)

    def emit_all(self, nc):
        needed = {s: set() for s in self.STREAMS}
        for s in self.STREAMS:
            for op in self.ops[s]:
                for st, i in op.waits:
                    needed[st].add(i)
        for s in self.STREAMS:
            for i in needed[s]:
                o = self.ops[s][i]
                assert o.dma is None and o.emit is not None, ("wait on non-compute op", s, i)
        rank = {}
        for s in self.STREAMS:
            rank[s] = {i: k + 1 for k, i in enumerate(sorted(needed[s]))}
        with contextlib.ExitStack() as es:
            psem = {s: es.enter_context(nc.semaphore("prog_" + s))
                    for s in ("pe", "act", "dve", "pool")}
            qsem = {q: [es.enter_context(nc.semaphore("dq_%s_%d" % (q, k)))
                        for k in range(K)] for q, K in self.KQ.items()}
            block = es.enter_context(nc.Block())

            def replay(s, e):
                for idx, op in enumerate(self.ops[s]):
                    for st, i in op.waits:
                        e.wait_ge(psem[st], rank[st][i])
                    for d in op.dwaits:
                        q, n = self.dmas[d]
                        K = self.KQ[q]
                        e.wait_ge(qsem[q][n % K], 16 * (n // K + 1))
                    if op.emit is None:
                        continue
                    if op.dma is not None:
                        q, n = self.dmas[op.dma]
                        K = self.KQ[q]
                        if n >= K:
                            e.wait_ge(qsem[q][n % K], 16 * (n // K))
                        ins = op.emit(e)
                        ins.then_inc(qsem[q][n % K], 16)
                    else:
                        ins = op.emit(e)
                        if idx in rank[s]:
                            ins.then_inc(psem[s], 1)

            @block.tensor
            def _(e):
                replay("pe", e)

            @block.scalar
            def _(e):
                replay("act", e)

            @block.vector
            def _(e):
                replay("dve", e)

            @block.gpsimd
            def _(e):
                replay("pool", e)

            @block.sync
            def _(e):
                replay("sp", e)
D = 1024
SEQ = 2048
NCTX = 256
NT = SEQ // 128
NKT = (SEQ + NCTX) // 128
DFF = 2816
NF = DFF // 128
EPS = 1e-6
INW = 2304
LAM_INIT0 = 0.8 - 0.6 * 1.0


class Arena:
    def __init__(self, nc, nbytes):
        self.t = nc.alloc_sbuf_tensor("arena", [128, nbytes // 4], F32)
        self.off = 0
        self.cap = nbytes
        self.peak = 0

    def _a(self, nb):
        o = self.off
        self.off += (nb + 31) // 32 * 32
        self.peak = max(self.peak, self.off)
        assert self.off <= self.cap, ("SBUF arena overflow", self.off, self.cap)
        return o

    def f32(self, n):
        o = self._a(n * 4) // 4
        return self.t[:, o:o + n]

    def b16(self, n):
        o = self._a(n * 2) // 4
        return self.t[:, o:o + n // 2].bitcast(BF16)

    def mark(self):
        return self.off

    def release(self, m):
        self.off = m


def v3(ap, a):
    return ap.rearrange("p (a b) -> p a b", a=a)


def v4(ap, a, b):
    return ap.rearrange("p (a b c) -> p a b c", a=a, b=b)


class B:
    def __init__(self, S):
        self.S = S

    def mm(self, out, lhsT, rhs, start, stop, r, w):
        self.S.add("pe", lambda e: e.matmul(out, lhsT=lhsT, rhs=rhs, start=start, stop=stop), r, w)

    def tr(self, out, in_, ident, r, w):
        self.S.add("pe", lambda e: e.transpose(out=out, in_=in_, identity=ident), r, w)

    def act(self, out, in_, func, r, w, scale=1.0, accum_out=None):
        if accum_out is None:
            self.S.add("act", lambda e: e.activation(out=out, in_=in_, func=func, scale=scale), r, w)
        else:
            self.S.add("act", lambda e: e.activation(out=out, in_=in_, func=func, scale=scale,
                                                      accum_out=accum_out), r, w)

    def tt(self, eng, out, in0, in1, op, r, w):
        self.S.add(eng, lambda e: e.tensor_tensor(out=out, in0=in0, in1=in1, op=op), r, w)

    def ts(self, eng, out, in0, s1, op0, r, w, s2=None, op1=None):
        if op1 is None:
            self.S.add(eng, lambda e: e.tensor_scalar(out=out, in0=in0, scalar1=s1, scalar2=None, op0=op0), r, w)
        else:
            self.S.add(eng, lambda e: e.tensor_scalar(out=out, in0=in0, scalar1=s1, scalar2=s2,
                                                      op0=op0, op1=op1), r, w)

    def stt(self, eng, out, in0, scalar, in1, op0, op1, r, w):
        self.S.add(eng, lambda e: e.scalar_tensor_tensor(out=out, in0=in0, scalar=scalar, in1=in1,
                                                         op0=op0, op1=op1), r, w)

    def cp(self, eng, out, in_, r, w):
        if eng == "act":
            self.S.add("act", lambda e: e.copy(out=out, in_=in_), r, w)
        else:
            self.S.add(eng, lambda e: e.tensor_copy(out=out, in_=in_), r, w)

    def red(self, eng, out, in_, r, w):
        self.S.add(eng, lambda e: e.tensor_reduce(out=out, in_=in_, axis=AX.X, op=ALU.add), r, w)

    def recip(self, out, in_, r, w):
        self.S.add("dve", lambda e: e.reciprocal(out=out, in_=in_), r, w)

    def memset(self, eng, ap, val, r, w):
        self.S.add(eng, lambda e: e.memset(ap, val), r, w)

    def rstd(self, out, ss, tmp, inv_n, r, w):
        t = [Tok()]
        self.ts("dve", tmp[0], ss, inv_n, ALU.mult, r, t, s2=EPS, op1=ALU.add)
        self.act(tmp[1], tmp[0], AF.Ln, t, t)
        self.act(out, tmp[1], AF.Exp, t, w, scale=-0.5)


def build_program(debug=False, nphase=6, ada1_in_attn=True, ada0_in_p1=True):
    nc = bass.Bass("TRN2", target_bir_lowering=False)

    def din(name, shape, dt=F32):
        return nc.dram_tensor(name, shape, dt, kind="ExternalInput").ap()

    x_in = din("x", [SEQ, D])
    ctx_in = din("ctx", [NCTX, D])
    cT_in = din("cT", [128, 16])
    ident_in = din("ident", [128, 128], BF16)
    ropeC_in = din("ropeC", [128, NT * 32])
    ropeS_in = din("ropeS", [128, NT * 32])
    cs128_in = din("cs128", [128, 256], BF16)
    dftc_in = din("dftc", [4, 128, NT * 512], BF16)
    dfts_in = din("dfts", [4, 128, NT * 512], BF16)
    W = {}
    for l in (0, 1):
        W["ada_w%d" % l] = din("l%d_ada_w" % l, [D, 6 * D])
        W["ada_b%d" % l] = din("l%d_ada_b" % l, [1, 6 * D])
        W["norm_mix%d" % l] = din("l%d_norm_mix" % l, [1, D])
        W["norm_ffn%d" % l] = din("l%d_norm_ffn" % l, [1, D])
        W["w_gu%d" % l] = din("l%d_w_gate_up" % l, [D, 2 * DFF])
        W["w_d%d" % l] = din("l%d_w_down" % l, [DFF, D])
        W["w_out%d" % l] = din("l%d_w_out" % l, [D, D])
    w_in = din("l0_w_in", [D, INW])
    lam_in = din("l0_lambda", [1, 256])
    subln_in = din("l0_subln", [1, 128])
    qn_in = din("l0_q_norm", [1, 64])
    kn_in = din("l0_k_norm", [1, 64])
    fin_in = din("final_norm", [1, D])
    okind = "ExternalOutput" if debug else "Internal"
    out_d = nc.dram_tensor("out", [SEQ, D], F32, kind="ExternalOutput").ap()
    x1_d = nc.dram_tensor("x1s", [SEQ, D], F32, kind=okind).ap()
    x2_d = nc.dram_tensor("x2s", [SEQ, D], F32, kind=okind).ap()
    x3_d = nc.dram_tensor("x3s", [SEQ, D], F32, kind=okind).ap()
    mod_d = nc.dram_tensor("mods", [2, 2, 6 * D], F32, kind=okind).ap()
    mix_d = nc.dram_tensor("mixd", [SEQ, D], BF16, kind="ExternalOutput").ap() if debug else None
    T_x1d, T_x2d, T_x3d = [[Tok() for _ in range(NT)] for _ in range(3)]
    T_outd = [Tok() for _ in range(NT)]
    T_modd = [Tok(), Tok()]

    S = Sched()
    b = B(S)
    A = Arena(nc, 212736)
    bpairs = [nc.alloc_psum_tensor("bpair%d" % i, [128, 1024], F32) for i in range(4)]
    banks = [bpairs[i // 2][:, (i % 2) * 512:(i % 2 + 1) * 512] for i in range(8)]
    T_bank = [Tok() for _ in range(8)]

    def bank16(i):
        return banks[i][:, 0:512].bitcast(BF16)

    S.bar_ap = A.f32(8)
    ident = A.b16(128); T_ident = Tok()
    S.dma("sp", ident, ident_in, writes=[T_ident])

    class Rot:
        def __init__(self, n, mk):
            self.bufs = [mk() for _ in range(n)]
            self.toks = [Tok() for _ in range(n)]
            self.i = 0

        def next(self):
            k = self.i % len(self.bufs)
            self.i += 1
            return self.bufs[k], self.toks[k]

    stat = Rot(6, lambda: A.f32(4))

    def norm_stats(xt, T_xt, junk, T_junk, ms_eng="pool"):
        st, T_st = stat.next()
        b.memset(ms_eng, st[:, 0:1], 0.0, [], [T_st])
        b.act(junk, xt, AF.Square, [T_xt, T_st], [T_junk, T_st], accum_out=st[:, 0:1])
        b.rstd(st[:, 3:4], st[:, 0:1], [st[:, 1:2], st[:, 2:3]], 1.0 / D, [T_st], [T_st])
        return st[:, 3:4], T_st

    def ada_chunks(l, CW):
        cT = A.f32(16); T_cT = Tok()
        S.dma("sp", cT, cT_in, writes=[T_cT])
        sc = A.b16(16); T_sc = Tok()
        b.act(sc, cT, AF.Silu, [T_cT], [T_sc])
        sc3 = v3(sc, 8)
        nch = 6 * D // CW
        wch = Rot(2, lambda: A.b16(8 * CW))
        bias_r = Rot(2, lambda: A.f32(CW))
        rows_r = Rot(2, lambda: A.f32(CW))
        aw = W["ada_w%d" % l].rearrange("(k p) n -> p k n", p=128)
        loaded = {}

        def load(j):
            if j >= nch or j in loaded:
                return
            wb, T_wb = wch.next()
            S.dma("pool", v3(wb, 8), aw[:, :, j * CW:(j + 1) * CW], writes=[T_wb])
            bs, T_bs = bias_r.next()
            S.dma("sp", bs[0:2, :], W["ada_b%d" % l][:, j * CW:(j + 1) * CW].partition_broadcast(2), writes=[T_bs])
            loaded[j] = (wb, T_wb, bs, T_bs)

        def chunk(j, bk):
            load(j)
            load(j + 1)
            wb, T_wb, bs, T_bs = loaded.pop(j)
            for k in range(8):
                b.mm(banks[bk][0:2, 0:CW], sc3[:, k, :], v3(wb, 8)[:, k, :], k == 0, k == 7,
                     [T_sc, T_wb], [T_bank[bk]])
            rw, T_rw = rows_r.next()
            b.tt("dve", rw[0:2, :], banks[bk][0:2, 0:CW], bs[0:2, :], ALU.add, [T_bank[bk], T_bs], [T_rw])
            S.dma("sp", mod_d[l, :, j * CW:(j + 1) * CW], rw[0:2, :], reads=[T_rw], writes=[T_modd[l]])

        return [(lambda bk, j=j: chunk(j, bk)) for j in range(nch)]

    def ada_phase(l):
        m0 = A.mark()
        for j, ch in enumerate(ada_chunks(l, 512)):
            ch(j % 2)
        S.barrier()
        A.release(m0)

    def load_mod(dst, T_dst, l, row, j, n=1):
        S.dma("sp", dst, mod_d[l, row:row + 1, j * D:(j + n) * D].partition_broadcast(128),
              reads=[T_modd[l]], writes=[T_dst])

    def load_gain(gain_ap):
        g = A.f32(D); T_g = Tok()
        S.dma("sp", g, gain_ap.partition_broadcast(128), writes=[T_g])
        return g, T_g

    def make_AB(l, row, which, g, T_g):
        ab = A.f32(2 * D); T_ab = Tok()
        load_mod(ab, T_ab, l, row, 3 * which, 2)
        ab3 = v3(ab, 2)
        b.stt("dve", ab3[:, 1, :], ab3[:, 1, :], 1.0, g, ALU.add, ALU.mult, [T_ab, T_g], [T_ab])
        return ab3[:, 1, :], ab3[:, 0, :], T_ab

    def pipeline_gen(n, stages):
        tot = n + max(sk for _, sk in stages)
        for step in range(tot):
            for fn, sk in stages:
                i = step - sk
                if 0 <= i < n:
                    fn(i)
            yield step

    def pipeline(n, stages):
        for _ in pipeline_gen(n, stages):
            pass

    class NormPipe:
        def __init__(self, load, ab_fn, add_eng, ms_eng="pool", nx=3):
            self.load, self.ab_fn, self.add_eng, self.ms_eng = load, ab_fn, add_eng, ms_eng
            self.xts = Rot(nx, lambda: A.f32(D))
            self.junks = Rot(1, lambda: A.b16(D))
            self.hbs = Rot(2, lambda: A.b16(D))
            self.tmps = Rot(2, lambda: A.f32(D))
            self.s1, self.s2 = {}, {}

        def S1(self, t):
            xt, T_xt = self.xts.next()
            self.load(t, xt, T_xt)
            jk, T_jk = self.junks.next()
            rstd, T_st = norm_stats(xt, T_xt, jk, T_jk, self.ms_eng)
            self.s1[t] = (xt, T_xt, rstd, T_st)

        def S2(self, t):
            xt, T_xt, rstd, T_st = self.s1.pop(t)
            Aap, Bap, T_ab = self.ab_fn(t)
            hb, T_hb = self.hbs.next()
            tmp, T_tmp = self.tmps.next()
            b.stt("dve", tmp, xt, rstd, Aap, ALU.mult, ALU.mult, [T_xt, T_st, T_ab], [T_tmp])
            b.tt(self.add_eng, hb, tmp, Bap, ALU.add, [T_tmp, T_ab], [T_hb])
            self.s2[t] = (hb, T_hb)

    def norm_mod(xt, T_xt, Aap, Bap, T_ab, hb, T_hb, tmp, T_tmp, add_eng="pool"):
        rstd, T_st = norm_stats(xt, T_xt, hb, T_hb)
        b.stt("dve", tmp, xt, rstd, Aap, ALU.mult, ALU.mult, [T_xt, T_st, T_ab], [T_tmp])
        b.tt(add_eng, hb, tmp, Bap, ALU.add, [T_tmp, T_ab], [T_hb])

    def attention_layer():
        m_layer = A.mark()
        ada0 = ada_chunks(0, 256) if ada0_in_p1 else []
        for j in range(8 if ada0 else 0):
            ada0.pop(0)(j % 2)
        qTa = A.b16(4 * SEQ); kTa = A.b16(4 * NKT * 128)
        qTb = A.b16(4 * SEQ); kTb = A.b16(2 * NKT * 128)
        Va = A.b16(NKT * 4 * 129); Vb = A.b16(NKT * 2 * 65)
        qTa3, kTa3, qTb3, kTb3 = v3(qTa, 4), v3(kTa, 4), v3(qTb, 4), v3(kTb, 2)
        Va4, Vb4 = v4(Va, NKT, 4), v4(Vb, NKT, 2)
        T_qk6 = [[Tok() for _ in range(6)] for _ in range(NKT)]
        T_v1 = Tok()
        b.memset("pool", Va, 1.0, [], [T_v1])
        b.memset("pool", Vb, 1.0, [], [T_v1])
        ropeC = A.f32(NT * 32); ropeS = A.f32(NT * 32); T_rope = Tok()
        S.dma("sp", ropeC, ropeC_in, writes=[T_rope])
        S.dma("sp", ropeS, ropeS_in, writes=[T_rope])
        ropeC3, ropeS3 = v3(ropeC, NT), v3(ropeS, NT)
        gq = A.f32(64); gk = A.f32(64); T_g = Tok()
        S.dma("sp", gq, qn_in.partition_broadcast(128), writes=[T_g])
        S.dma("sp", gk, kn_in.partition_broadcast(128), writes=[T_g])

        m1 = A.mark()
        gm, T_gm = load_gain(W["norm_mix0"])
        Aap, Bap, T_ab = make_AB(0, 0, 0, gm, T_gm)
        Acp, Bcp, T_abc = make_AB(0, 1, 0, gm, T_gm)
        win = A.b16(8 * INW); win3 = v3(win, 8)
        T_win = [Tok() for _ in range(5)]
        groups = [(0, 512), (512, 512), (1024, 512), (1536, 512), (2048, 256)]
        w_in_v = w_in.rearrange("(k p) n -> p k n", p=128)
        for gi in (1, 2, 4, 0, 3):
            c0, cw = groups[gi]
            S.dma("pool", win3[:, :, c0:c0 + cw], w_in_v[:, :, c0:c0 + cw], writes=[T_win[gi]])
        def p1_load(t, xt, T_xt):
            src = ctx_in[t * 128:(t + 1) * 128, :] if t < 2 else x_in[(t - 2) * 128:(t - 1) * 128, :]
            S.dma("sp", xt, src, writes=[T_xt])

        npipe = NormPipe(p1_load, lambda t: (Acp, Bcp, T_abc) if t < 2 else (Aap, Bap, T_ab), "pool", nx=2)
        hTs = Rot(2, lambda: A.b16(D))
        hT_of = {}
        stg = Rot(2, lambda: A.b16(512 * 3 + 256))
        rt = Rot(2, lambda: A.f32(4 * 256))
        xn_r = Rot(2, lambda: A.f32(512))
        sq_r = Rot(2, lambda: A.f32(512))
        st8 = Rot(4, lambda: A.f32(4 * 8))

        def rope(src3, dst3, nh, tl, r, T_dst):
            tmp, T_tmp = rt.next()
            t = v4(tmp[:, 0:4 * nh * 32], 4, nh)
            Cb = ropeC3[:, tl, :].unsqueeze(1).broadcast_to([128, nh, 32])
            Sb = ropeS3[:, tl, :].unsqueeze(1).broadcast_to([128, nh, 32])
            x1, x2 = src3[:, :, 0:32], src3[:, :, 32:64]
            b.tt("dve", t[:, 0, 0:nh, :], x1, Cb, ALU.mult, r + [T_rope], [T_tmp])
            b.tt("dve", t[:, 1, 0:nh, :], x2, Sb, ALU.mult, r + [T_rope], [T_tmp])
            b.tt("dve", t[:, 2, 0:nh, :], x2, Cb, ALU.mult, r + [T_rope], [T_tmp])
            b.tt("dve", t[:, 3, 0:nh, :], x1, Sb, ALU.mult, r + [T_rope], [T_tmp])
            b.tt("pool", dst3[:, :, 0:32], t[:, 0, 0:nh, :], t[:, 1, 0:nh, :], ALU.subtract, [T_tmp], [T_dst])
            b.tt("pool", dst3[:, :, 32:64], t[:, 2, 0:nh, :], t[:, 3, 0:nh, :], ALU.add, [T_tmp], [T_dst])

        def hn_stats(src3, nh, r):
            sq, T_sq = sq_r.next()
            xn, T_xn = xn_r.next()
            s8, T_s8 = st8.next()
            s84 = v3(s8, 4)
            sq3 = v3(sq[:, 0:nh * 64], nh)
            xn3 = v3(xn[:, 0:nh * 64], nh)
            b.cp("act", xn3, src3, r, [T_xn])
            b.act(sq3, src3, AF.Square, r, [T_sq])
            b.red("dve", s84[:, 0, 0:nh], sq3, [T_sq], [T_s8])
            return (xn3, T_xn, s84, T_s8, nh)

        def hn_rstd(hn):
            xn3, T_xn, s84, T_s8, nh = hn
            b.rstd(s84[:, 3, 0:nh], s84[:, 0, 0:nh], [s84[:, 1, 0:nh], s84[:, 2, 0:nh]], 1.0 / 64, [T_s8], [T_s8])

        def hn_apply(hn, gain):
            xn3, T_xn, s84, T_s8, nh = hn
            b.tt("dve", xn3, xn3, s84[:, 3, 0:nh].unsqueeze(2).broadcast_to([128, nh, 64]), ALU.mult,
                 [T_xn, T_s8], [T_xn])
            b.tt("pool", xn3, xn3, gain.unsqueeze(1).broadcast_to([128, nh, 64]), ALU.mult, [T_xn, T_g], [T_xn])
            return xn3, T_xn

        def p1_S3(t):
            hb, T_hb = npipe.s2.pop(t)
            hT, T_hT = hTs.next()
            for k in range(8):
                b.tr(bank16(0)[:, k * 128:(k + 1) * 128], hb[:, k * 128:(k + 1) * 128], ident,
                     [T_hb, T_ident], [T_bank[0]])
            b.cp("act", hT, bank16(0), [T_bank[0]], [T_hT])
            hT_of[t] = (hT, T_hT)

        stg_of = {}

        def p1_S4(t):
            is_ctx = t < 2
            tl = t - 2
            hT, T_hT = hT_of.pop(t)
            hT3 = v3(hT, 8)
            gl = (1, 4, 2) if is_ctx else (0, 1, 3, 4, 2)
            for gi in gl:
                c0, cw = groups[gi]
                for k in range(8):
                    b.mm(banks[1 + gi][:, 0:cw], hT3[:, k, :], win3[:, k, c0:c0 + cw], k == 0, k == 7,
                         [T_hT, T_win[gi]], [T_bank[1 + gi]])
            sg, T_sg = stg.next()
            stg_of[t] = (sg, T_sg)
            qa_tm, ka_tm, qb_tm, kb_tm = sg[:, 0:512], sg[:, 512:1024], sg[:, 1024:1536], sg[:, 1536:1792]
            if is_ctx:
                b.cp("dve", ka_tm, banks[2][:, 0:512], [T_bank[2]], [T_sg])
            else:
                rope(v3(banks[1][:, 0:512], 8), v3(qa_tm, 8), 8, tl, [T_bank[1]], T_sg)
                rope(v3(banks[2][:, 0:512], 8), v3(ka_tm, 8), 8, tl, [T_bank[2]], T_sg)
            hnq = None
            if not is_ctx:
                hnq = hn_stats(v3(banks[4][:, 0:512], 8), 8, [T_bank[4]])
            hnk = hn_stats(v3(banks[5][:, 0:128], 2), 2, [T_bank[5]])
            b.cp("act", Vb4[:, t, :, 0:64], v3(banks[5][:, 128:256], 2), [T_bank[5], T_v1], [T_qk6[t][5]])
            if hnq is not None:
                hn_rstd(hnq)
            hn_rstd(hnk)
            b.cp("act", Va4[:, t, :, 0:128], v3(banks[3][:, 0:512], 4), [T_bank[3], T_v1], [T_qk6[t][4]])
            if hnq is not None:
                xn3, T_xn = hn_apply(hnq, gq)
                rope(xn3, v3(qb_tm, 8), 8, tl, [T_xn], T_sg)
            xk3, T_xk = hn_apply(hnk, gk)
            kb4 = v4(kb_tm, 2, 2)
            if is_ctx:
                b.cp("dve", kb4[:, :, 0, :], xk3, [T_xk], [T_sg])
            else:
                rope(xk3, kb4[:, :, 0, :], 2, tl, [T_xk], T_sg)
            b.cp("pool", kb4[:, :, 1, :], kb4[:, :, 0, :], [T_sg], [T_sg])

        def p1_S5(t):
            is_ctx = t < 2
            tl = t - 2
            sg, T_sg = stg_of.pop(t)
            qa_tm, ka_tm, qb_tm, kb_tm = sg[:, 0:512], sg[:, 512:1024], sg[:, 1024:1536], sg[:, 1536:1792]
            if not is_ctx:
                for c in range(4):
                    b.tr(bank16(6)[:, c * 128:(c + 1) * 128], qa_tm[:, c * 128:(c + 1) * 128], ident,
                         [T_sg, T_ident], [T_bank[6]])
            for c in range(4):
                b.tr(bank16(6)[:, 512 + c * 128:512 + (c + 1) * 128], ka_tm[:, c * 128:(c + 1) * 128], ident,
                     [T_sg, T_ident], [T_bank[6]])
            if not is_ctx:
                for c in range(4):
                    b.tr(bank16(7)[:, c * 128:(c + 1) * 128], qb_tm[:, c * 128:(c + 1) * 128], ident,
                         [T_sg, T_ident], [T_bank[7]])
            for c in range(2):
                b.tr(bank16(7)[:, 512 + c * 128:512 + (c + 1) * 128], kb_tm[:, c * 128:(c + 1) * 128], ident,
                     [T_sg, T_ident], [T_bank[7]])
            if not is_ctx:
                b.cp("act", qTa3[:, :, tl * 128:(tl + 1) * 128], v3(bank16(6)[:, 0:512], 4), [T_bank[6]], [T_qk6[t][0]])
                b.cp("dve", qTb3[:, :, tl * 128:(tl + 1) * 128], v3(bank16(7)[:, 0:512], 4), [T_bank[7]], [T_qk6[t][2]])
            b.cp("act", kTa3[:, :, t * 128:(t + 1) * 128], v3(bank16(6)[:, 512:1024], 4), [T_bank[6]], [T_qk6[t][1]])
            b.cp("dve", kTb3[:, :, t * 128:(t + 1) * 128], v3(bank16(7)[:, 512:768], 2), [T_bank[7]], [T_qk6[t][3]])

        pipeline(NKT, [(npipe.S1, 0), (npipe.S2, 1), (p1_S3, 2), (p1_S4, 3), (p1_S5, 4)])
        S.barrier()
        A.release(m1)

        G1 = A.f32(D); T_G1 = Tok()
        if not ada0:
            load_mod(G1, T_G1, 0, 0, 2)
        wo = A.b16(8 * D); T_wo = Tok()
        S.dma("pool", v3(wo, 8), W["w_out0"].rearrange("(k p) n -> p k n", p=128), writes=[T_wo])
        wo3 = v3(wo, 8)
        lamv = A.f32(256); T_lam = Tok()
        S.dma("sp", lamv, lam_in.partition_broadcast(128), writes=[T_lam])
        lst = A.f32(8)
        lam4 = v3(lamv, 4)
        b.tt("dve", lam4[:, 0, :], lam4[:, 0, :], lam4[:, 1, :], ALU.mult, [T_lam], [T_lam])
        b.tt("dve", lam4[:, 2, :], lam4[:, 2, :], lam4[:, 3, :], ALU.mult, [T_lam], [T_lam])
        b.red("dve", lst[:, 0:1], lam4[:, 0, :], [T_lam], [T_lam])
        b.red("dve", lst[:, 1:2], lam4[:, 2, :], [T_lam], [T_lam])
        b.act(lst[:, 2:4], lst[:, 0:2], AF.Exp, [T_lam], [T_lam])
        b.tt("dve", lst[:, 4:5], lst[:, 3:4], lst[:, 2:3], ALU.subtract, [T_lam], [T_lam])
        b.ts("dve", lst[:, 5:6], lst[:, 4:5], -LAM_INIT0, ALU.add, [T_lam], [T_lam])
        nlam = lst[:, 5:6]
        sub = A.f32(128); T_sub = Tok()
        S.dma("sp", sub, subln_in.partition_broadcast(128), writes=[T_sub])
        b.ts("pool", sub, sub, 1.0 - LAM_INIT0, ALU.mult, [T_sub], [T_sub])

        pslots = Rot(3, lambda: A.b16(1024))
        pslots.toks = [(Tok(), Tok()) for _ in range(3)]
        oacc_r = Rot(2, lambda: A.f32(8 * 129))
        otmp_r = Rot(2, lambda: A.f32(3 * 512))
        ost_r = Rot(2, lambda: A.f32(32))
        mix_r = Rot(2, lambda: A.b16(4 * D))
        mixT_r = Rot(2, lambda: A.b16(8 * 512))
        xts = Rot(2, lambda: A.f32(D))
        yt_r = Rot(2, lambda: A.f32(D))
        SC = 0.125
        ada1 = ada_chunks(1, 256) if ada1_in_attn else []
        pending_post = [None]

        def make_post(isA, pair, oacc, T_oa, Wd, mix3, T_mix):
            def post():
                o4 = v4(oacc[:, 0:8 * Wd], 2, 4)
                o3 = v3(oacc[:, 0:8 * Wd], 8)
                ost, T_os = ost_r.next()
                b.recip(ost[:, 0:8], o3[:, :, Wd - 1], [T_oa], [T_os])
                if isA:
                    otmp, T_ot = otmp_r.next()
                    u = v3(otmp[:, 0:512], 4); tt_ = v3(otmp[:, 512:1024], 4); sq = v3(otmp[:, 1024:1536], 4)
                    b.ts("dve", ost[:, 4:8], ost[:, 4:8], nlam, ALU.mult, [T_os, T_lam], [T_os])
                    b.tt("dve", u, o4[:, 0, :, 0:128], ost[:, 0:4].unsqueeze(2).broadcast_to([128, 4, 128]),
                         ALU.mult, [T_oa, T_os], [T_ot])
                    b.tt("dve", tt_, o4[:, 1, :, 0:128], ost[:, 4:8].unsqueeze(2).broadcast_to([128, 4, 128]),
                         ALU.mult, [T_oa, T_os], [T_ot])
                    b.tt("pool", u, u, tt_, ALU.add, [T_ot], [T_ot])
                    b.tt("pool", sq, u, u, ALU.mult, [T_ot], [T_ot])
                    b.red("dve", ost[:, 8:12], sq, [T_ot], [T_os])
                    b.rstd(ost[:, 20:24], ost[:, 8:12], [ost[:, 12:16], ost[:, 16:20]], 1.0 / 128, [T_os], [T_os])
                    b.tt("dve", u, u, ost[:, 20:24].unsqueeze(2).broadcast_to([128, 4, 128]), ALU.mult,
                         [T_ot, T_os], [T_ot])
                    b.tt("pool", mix3[:, :, pair * 128:(pair + 1) * 128], u,
                         sub.unsqueeze(1).broadcast_to([128, 4, 128]), ALU.mult, [T_ot, T_sub], [T_mix])
                else:
                    for m in range(2):
                        h = 2 * (pair - 4) + m
                        b.tt("dve" if m == 0 else "pool", mix3[:, :, 512 + h * 64:512 + (h + 1) * 64],
                             o4[:, m, :, 0:64], ost[:, 4 * m:4 * m + 4].unsqueeze(2).broadcast_to([128, 4, 64]),
                             ALU.mult, [T_oa, T_os], [T_mix])
            return post

        for qb in range(4):
            mix, T_mix = mix_r.next()
            mix3 = v3(mix, 4)
            T_q = [tk for j in range(4) for tk in T_qk6[2 + qb * 4 + j]]
            for pair in range(8):
                isA = pair < 4
                if isA:
                    qT3, kT3, ch, kch, Wd = qTa3, kTa3, pair, pair, 129
                else:
                    qT3, kT3, ch, kch, Wd = qTb3, kTb3, pair - 4, (pair - 4) // 2, 65
                pend = None
                if ada1:
                    ada1.pop(0)(4)
                if ada0:
                    ada0.pop(0)(5)
                for kt in range(NKT + 1):
                    if kt == 5 and pending_post[0] is not None:
                        pending_post[0]()
                        pending_post[0] = None
                    cur = None
                    if kt < NKT:
                        cur = []
                        ps2, T_ps2 = pslots.next()
                        for m in range(2):
                            sb = 2 * (kt % 2) + m
                            lo = 64 * m
                            b.mm(banks[sb][:, :], kT3[lo:lo + 64, kch, kt * 128:(kt + 1) * 128],
                                 qT3[lo:lo + 64, ch, qb * 512:(qb + 1) * 512], True, True,
                                 T_q + T_qk6[kt], [T_bank[sb]])
                            T_pm = T_ps2[m]
                            b.act(ps2[:, m * 512:(m + 1) * 512], banks[sb][:, :], AF.Exp, [T_bank[sb]], [T_pm], scale=SC)
                            cur.append((ps2[:, m * 512:(m + 1) * 512], T_pm))
                    if pend is not None:
                        pk, pl = pend
                        for m in range(2):
                            ps, T_ps = pl[m]
                            if isA:
                                rhs = Va4[:, pk, pair, :]
                            else:
                                rhs = Vb4[:, pk, kch, :]
                            for j in range(4):
                                ab_ = 4 + 2 * m + j // 2
                                b.mm(banks[ab_][:, (j % 2) * Wd:(j % 2) * Wd + Wd], ps[:, j * 128:(j + 1) * 128], rhs,
                                     pk == 0 and j % 2 == 0, pk == NKT - 1 and j % 2 == 1, [T_ps, T_v1] + T_qk6[pk], [T_bank[ab_]])
                    pend = (kt, cur) if cur is not None else None
                oacc, T_oa = oacc_r.next()
                o4 = v4(oacc[:, 0:8 * Wd], 2, 4)
                for m in range(2):
                    for hf in range(2):
                        ab_ = 4 + 2 * m + hf
                        b.cp("dve", o4[:, m, 2 * hf:2 * hf + 2, :],
                             v3(banks[ab_][:, 0:2 * Wd], 2), [T_bank[ab_]], [T_oa])
                pending_post[0] = make_post(isA, pair, oacc, T_oa, Wd, mix3, T_mix)
            pending_post[0]()
            pending_post[0] = None
            if qb == 0 and ada0_in_p1:
                load_mod(G1, T_G1, 0, 0, 2)
            if debug:
                for j in range(4):
                    tl = qb * 4 + j
                    S.dma("sp", mix_d[tl * 128:(tl + 1) * 128, :], mix3[:, j, :], reads=[T_mix], is_out=True)
            mixT, T_mT = mixT_r.next()
            mixT3 = v3(mixT, 8)
            for j in range(4):
                bk = j % 2
                for k in range(8):
                    b.tr(bank16(bk)[:, k * 128:(k + 1) * 128], mix3[:, j, k * 128:(k + 1) * 128], ident,
                         [T_mix, T_ident], [T_bank[bk]])
                b.cp("act" if j % 2 == 0 else "dve", mixT3[:, :, j * 128:(j + 1) * 128], v3(bank16(bk), 8),
                     [T_bank[bk]], [T_mT])
            for j in range(4):
                tl = qb * 4 + j
                xt, T_xt = xts.next()
                S.dma("sp", xt, x_in[tl * 128:(tl + 1) * 128, :], writes=[T_xt])
                yt, T_yt = yt_r.next()
                for cg in range(2):
                    bk = 2 + cg
                    for k in range(8):
                        b.mm(banks[bk][:, :], mixT3[:, k, j * 128:(j + 1) * 128], wo3[:, k, cg * 512:(cg + 1) * 512],
                             k == 0, k == 7, [T_mT, T_wo], [T_bank[bk]])
                    b.tt("dve", yt[:, cg * 512:(cg + 1) * 512], banks[bk][:, :], G1[:, cg * 512:(cg + 1) * 512],
                         ALU.mult, [T_bank[bk], T_G1], [T_yt])
                b.tt("pool", yt, yt, xt, ALU.add, [T_yt, T_xt], [T_yt])
                S.dma("pool", x1_d[tl * 128:(tl + 1) * 128, :], yt, reads=[T_yt], writes=[T_x1d[tl]])
        while ada1:
            ada1.pop(0)(4)
        while ada0:
            ada0.pop(0)(5)
        S.barrier()
        A.release(m_layer)

    def ffn_layer(l, src_d, T_src, dst_d, T_dst, final):
        m0 = A.mark()
        gf, T_gf = load_gain(W["norm_ffn%d" % l])
        Aap, Bap, T_ab = make_AB(l, 0, 1, gf, T_gf)
        G2 = A.f32(D); T_G2 = Tok()
        load_mod(G2, T_G2, l, 0, 5)
        if final:
            S.dma("sp", gf, fin_in.partition_broadcast(128), reads=[], writes=[T_gf])
        HT = 1024
        NTH = HT // 128
        wgu_v = W["w_gu%d" % l].rearrange("(k p) n -> p k n", p=128)
        wd_v = W["w_d%d" % l].rearrange("(f p) n -> p f n", p=128)
        ring = Rot(3, lambda: A.b16(8 * 256))
        chunks = [(h, f) for h in range(2) for f in range(NF)]
        ring_of = {}

        def ring_load(ci):
            if ci >= len(chunks):
                return
            h, f = chunks[ci]
            wb, T_wb = ring.next()
            wb3 = v3(wb, 8)
            S.dma("pool", wb3[:, :, 0:128], wgu_v[:, :, f * 128:(f + 1) * 128], writes=[T_wb])
            S.dma("pool", wb3[:, :, 128:256], wgu_v[:, :, DFF + f * 128:DFF + (f + 1) * 128], writes=[T_wb])
            ring_of[ci] = (wb3, T_wb)

        ring_load(0)
        ring_load(1)
        wd = A.b16(NF * D); wd3 = v3(wd, NF)
        NG = 11
        T_wd = [Tok() for _ in range(NG)]
        for g in range(NG):
            S.dma("pool", wd3[:, 2 * g:2 * g + 2, :], wd_v[:, 2 * g:2 * g + 2, :], writes=[T_wd[g]])
        h2T_r = Rot(2, lambda: A.b16(8 * HT))
        actT = A.b16(NF * HT); actT3 = v3(actT, NF); T_act = [Tok() for _ in range(NF)]
        sg_r = Rot(2, lambda: A.b16(512))
        cur_half = [0]

        def f_load(t, xt, T_xt):
            tl = cur_half[0] * NTH + t
            S.dma("sp", xt, src_d[tl * 128:(tl + 1) * 128, :], reads=[T_src[tl]], writes=[T_xt])

        npipe = NormPipe(f_load, lambda t: (Aap, Bap, T_ab), "dve", ms_eng="dve")
        xts, tmps, hbs = npipe.xts, npipe.tmps, npipe.hbs

        def prep_gen(h):
            h2T, T_h2 = h2T_r.next()
            h2T3 = v3(h2T, 8)

            def S3(t):
                hb, T_hb = npipe.s2.pop(t)
                for k in range(8):
                    b.tr(bank16(6)[:, k * 128:(k + 1) * 128], hb[:, k * 128:(k + 1) * 128], ident,
                         [T_hb, T_ident], [T_bank[6]])
                b.cp("act", h2T3[:, :, t * 128:(t + 1) * 128], v3(bank16(6), 8), [T_bank[6]], [T_h2])

            def gen():
                cur_half[0] = h
                yield from pipeline_gen(NTH, [(npipe.S1, 0), (npipe.S2, 1), (S3, 2)])
            return (h2T3, T_h2), gen()

        nxt, g0 = prep_gen(0)
        for _ in g0:
            pass
        ci = 0
        for h in range(2):
            h2T3, T_h2 = nxt
            pg = None
            for f in range(NF):
                ring_load(ci + 2)
                wb3, T_wb = ring_of.pop(ci)
                ci += 1
                for nb in range(2):
                    gb, ub = nb, 2 + nb
                    for k in range(8):
                        b.mm(banks[gb][:, :], wb3[:, k, 0:128], h2T3[:, k, nb * 512:(nb + 1) * 512], k == 0, k == 7,
                             [T_wb, T_h2], [T_bank[gb]])
                    for k in range(8):
                        b.mm(banks[ub][:, :], wb3[:, k, 128:256], h2T3[:, k, nb * 512:(nb + 1) * 512], k == 0, k == 7,
                             [T_wb, T_h2], [T_bank[ub]])
                    sg, T_sg = sg_r.next()
                    b.act(sg, banks[gb][:, :], AF.Silu, [T_bank[gb]], [T_sg])
                    b.tt("dve", actT3[:, f, nb * 512:(nb + 1) * 512], banks[ub][:, :], sg, ALU.mult,
                         [T_bank[ub], T_sg], [T_act[f]])
                if h == 0:
                    if f == 6:
                        nxt, pg = prep_gen(1)
                    if pg is not None:
                        next(pg, None)
            if pg is not None:
                for _ in pg:
                    pass
            for j in range(NTH):
                tl = h * NTH + j
                xt, T_xt = xts.next()
                S.dma("sp", xt, src_d[tl * 128:(tl + 1) * 128, :], reads=[T_src[tl]], writes=[T_xt])
                yt, T_yt = tmps.next()
                for cg in range(2):
                    bk = 4 + 2 * (j % 2) + cg
                    if bk == 6:
                        bk = 7 if False else 6
                    for f in range(NF):
                        b.mm(banks[bk][:, :], actT3[:, f, j * 128:(j + 1) * 128], wd3[:, f, cg * 512:(cg + 1) * 512],
                             f == 0, f == NF - 1, [T_act[f], T_wd[f // 2]], [T_bank[bk]])
                    b.tt("dve", yt[:, cg * 512:(cg + 1) * 512], banks[bk][:, :], G2[:, cg * 512:(cg + 1) * 512],
                         ALU.mult, [T_bank[bk], T_G2], [T_yt])
                b.tt("dve", yt, yt, xt, ALU.add, [T_yt, T_xt], [T_yt])
                if not final:
                    S.dma("pool", dst_d[tl * 128:(tl + 1) * 128, :], yt, reads=[T_yt], writes=[T_dst[tl]])
                else:
                    hbj, T_hbj = hbs.next()
                    rstd, T_st = norm_stats(yt, T_yt, hbj, T_hbj, "dve")
                    b.stt("dve", xt, yt, rstd, gf, ALU.mult, ALU.mult, [T_yt, T_st, T_gf], [T_xt])
                    S.dma("pool", dst_d[tl * 128:(tl + 1) * 128, :], xt, reads=[T_xt], writes=[T_dst[tl]],
                          is_out=True)
        S.barrier()
        A.release(m0)

    def fourier_layer():
        m0 = A.mark()
        gm, T_gm = load_gain(W["norm_mix1"])
        Aap, Bap, T_ab = make_AB(1, 0, 0, gm, T_gm)
        G1 = A.f32(D); T_G1 = Tok()
        load_mod(G1, T_G1, 1, 0, 2)
        cs = A.b16(256); T_cs = Tok()
        S.dma("sp", cs, cs128_in, writes=[T_cs])
        wo = A.b16(8 * D); T_wo = Tok()
        S.dma("pool", v3(wo, 8), W["w_out1"].rearrange("(k p) n -> p k n", p=128), writes=[T_wo])
        wo3 = v3(wo, 8)
        Y1 = A.b16(NT * D); Y2 = A.b16(NT * D)
        Y13, Y23 = v3(Y1, NT), v3(Y2, NT)
        T_Y = [Tok() for _ in range(NT)]
        T_Y2 = [Tok() for _ in range(NT)]
        dft_r = Rot(2, lambda: (A.b16(NT * 512), A.b16(NT * 512)))
        fT_r = Rot(1, lambda: A.b16(8 * 512))
        def f_load(t, xt, T_xt):
            S.dma("sp", xt, x2_d[t * 128:(t + 1) * 128, :], reads=[T_x2d[t]], writes=[T_xt])

        npipe = NormPipe(f_load, lambda t: (Aap, Bap, T_ab), "pool")
        xts, tmps = npipe.xts, npipe.tmps
        hTs = Rot(2, lambda: A.b16(D))
        hT_of = {}

        def f_S3(t):
            hb, T_hb = npipe.s2.pop(t)
            hT, T_hT = hTs.next()
            for k in range(8):
                b.tr(bank16(0)[:, k * 128:(k + 1) * 128], hb[:, k * 128:(k + 1) * 128], ident,
                     [T_hb, T_ident], [T_bank[0]])
            b.cp("act", hT, bank16(0), [T_bank[0]], [T_hT])
            hT_of[t] = (hT, T_hT)

        def f_S4(t):
            hT, T_hT = hT_of.pop(t)
            hT3 = v3(hT, 8)
            for g in range(8):
                bk = 1 + g // 2
                b.mm(banks[bk][:, (g % 2) * 256:(g % 2) * 256 + 256], hT3[:, g, :], cs, True, True,
                     [T_hT, T_cs], [T_bank[bk]])
            for q in range(4):
                bk = 1 + q
                bv = v3(banks[bk][:, :], 2)
                eng = "act" if q % 2 == 0 else "dve"
                b.cp(eng, v3(Y13[:, t, q * 256:(q + 1) * 256], 2), bv[:, :, 0:128], [T_bank[bk]], [T_Y[t]])
                b.cp(eng, v3(Y23[:, t, q * 256:(q + 1) * 256], 2), bv[:, :, 128:256], [T_bank[bk]], [T_Y2[t]])

        pipeline(NT, [(npipe.S1, 0), (npipe.S2, 1), (f_S3, 2), (f_S4, 3)])
        for ub in range(4):
            (dc, ds), T_d = dft_r.next()
            S.dma("sp", dc, dftc_in[ub], writes=[T_d])
            S.dma("sp", ds, dfts_in[ub], writes=[T_d])
            dc3, ds3 = v3(dc, NT), v3(ds, NT)
            fT, T_fT = fT_r.next()
            fT3 = v3(fT, 8)
            for n in range(8):
                bk = 5 + n % 2
                for st in range(NT):
                    b.mm(banks[bk][:, :], Y13[:, st, n * 128:(n + 1) * 128], dc3[:, st, :], st == 0, False,
                         [T_Y[st], T_d], [T_bank[bk]])
                for st in range(NT):
                    b.mm(banks[bk][:, :], Y23[:, st, n * 128:(n + 1) * 128], ds3[:, st, :], False, st == NT - 1,
                         [T_Y2[st], T_d], [T_bank[bk]])
                b.act(fT3[:, n, :], banks[bk][:, :], AF.Copy, [T_bank[bk]], [T_fT], scale=1.0 / 512)
            for j in range(4):
                t = ub * 4 + j
                xt, T_xt = xts.next()
                S.dma("sp", xt, x2_d[t * 128:(t + 1) * 128, :], reads=[T_x2d[t]], writes=[T_xt])
                yt, T_yt = tmps.next()
                for cg in range(2):
                    bk = 1 + cg
                    for k in range(8):
                        b.mm(banks[bk][:, :], fT3[:, k, j * 128:(j + 1) * 128], wo3[:, k, cg * 512:(cg + 1) * 512],
                             k == 0, k == 7, [T_fT, T_wo], [T_bank[bk]])
                    b.tt("dve", yt[:, cg * 512:(cg + 1) * 512], banks[bk][:, :], G1[:, cg * 512:(cg + 1) * 512],
                         ALU.mult, [T_bank[bk], T_G1], [T_yt])
                b.tt("pool", yt, yt, xt, ALU.add, [T_yt, T_xt], [T_yt])
                S.dma("pool", x3_d[t * 128:(t + 1) * 128, :], yt, reads=[T_yt], writes=[T_x3d[t]])
        S.barrier()
        A.release(m0)

    phases = [(lambda: None) if ada0_in_p1 else (lambda: ada_phase(0)), (lambda: None) if ada1_in_attn else (lambda: ada_phase(1)), attention_layer,
              lambda: ffn_layer(0, x1_d, T_x1d, x2_d, T_x2d, False), fourier_layer,
              lambda: ffn_layer(1, x3_d, T_x3d, out_d, T_outd, True)]
    for ph in phases[:nphase]:
        ph()
    S.finish()
    S.emit_all(nc)
    return nc, A.peak
_CACHE = {}


def _consts():
    if "c" in _CACHE:
        return _CACHE["c"]
    import ml_dtypes
    bf = ml_dtypes.bfloat16
    ident = np.eye(128, dtype=np.float32).astype(bf)
    tok = np.arange(SEQ)
    row = (tok // 64).astype(np.float32)
    col = (tok % 64).astype(np.float32)
    inv = (np.float32(10000.0) ** (-np.arange(16, dtype=np.float32) / np.float32(16))).astype(np.float32)
    ang = np.concatenate([row[:, None] * inv[None], col[:, None] * inv[None]], axis=1).astype(np.float32)
    C = np.cos(ang).astype(np.float32).reshape(NT, 128, 32).transpose(1, 0, 2).reshape(128, NT * 32)
    Sn = np.sin(ang).astype(np.float32).reshape(NT, 128, 32).transpose(1, 0, 2).reshape(128, NT * 32)
    cv = (np.arange(128)[:, None] * np.arange(128)[None, :]) % 128
    a128 = 2.0 * np.pi * cv / 128.0
    cs128 = np.concatenate([np.cos(a128), np.sin(a128)], axis=1).astype(np.float32).astype(bf)
    s_idx = np.arange(SEQ).reshape(NT, 128)
    u_idx = np.arange(SEQ).reshape(4, 512)
    prod = (s_idx[None, :, :, None] * u_idx[:, None, None, :]) % SEQ
    angs = (2.0 * np.pi / SEQ) * prod.astype(np.float64)
    dftc = np.cos(angs).transpose(0, 2, 1, 3).reshape(4, 128, NT * 512).astype(np.float32).astype(bf)
    dfts = (-np.sin(angs)).transpose(0, 2, 1, 3).reshape(4, 128, NT * 512).astype(np.float32).astype(bf)
    _CACHE["c"] = dict(ident=ident, ropeC=np.ascontiguousarray(C), ropeS=np.ascontiguousarray(Sn),
                       cs128=cs128, dftc=np.ascontiguousarray(dftc), dfts=np.ascontiguousarray(dfts))
    return _CACHE["c"]


def _perm64():
    new = np.zeros(64, dtype=np.int64)
    for h in range(2):
        for a in range(2):
            for p in range(16):
                new[h * 32 + a * 16 + p] = a * 32 + h * 16 + p
    return new


def kernel(_ncores=None, _nphase=6, **inp):
    f32 = np.float32
    x = np.asarray(inp["x"], f32)
    c = np.asarray(inp["c"], f32)
    ctx = np.asarray(inp["ctx"], f32)
    c_ctx = np.asarray(inp["c_ctx"], f32)
    nb = x.shape[0] if _ncores is None else _ncores
    p64 = _perm64()
    w_in = np.asarray(inp["l0_w_in"], f32)
    cols = np.arange(INW)
    for c0, nh in ((0, 8), (512, 8), (1536, 8), (2048, 2)):
        for h in range(nh):
            cols[c0 + h * 64:c0 + (h + 1) * 64] = c0 + h * 64 + p64
    w_in_p = np.ascontiguousarray(w_in[:, cols])
    shared = dict(_consts())
    def full(name):
        return np.ascontiguousarray(np.asarray(inp[name], f32))

    def row(name):
        return np.asarray(inp[name], f32).reshape(1, -1)

    shared["l0_ada_w"] = full("l0_ada_w")
    shared["l0_ada_b"] = row("l0_ada_b")
    shared["l0_norm_mix"] = row("l0_norm_mix")
    shared["l0_norm_ffn"] = row("l0_norm_ffn")
    shared["l0_w_gate_up"] = full("l0_w_gate_up")
    shared["l0_w_down"] = full("l0_w_down")
    shared["l0_w_out"] = full("l0_w_out")
    shared["l1_ada_w"] = full("l1_ada_w")
    shared["l1_ada_b"] = row("l1_ada_b")
    shared["l1_norm_mix"] = row("l1_norm_mix")
    shared["l1_norm_ffn"] = row("l1_norm_ffn")
    shared["l1_w_gate_up"] = full("l1_w_gate_up")
    shared["l1_w_down"] = full("l1_w_down")
    shared["l1_w_out"] = full("l1_w_out")
    shared["l0_w_in"] = w_in_p
    shared["l0_lambda"] = np.concatenate([np.asarray(inp["l0_lambda_q1"], f32), np.asarray(inp["l0_lambda_k1"], f32),
                                          np.asarray(inp["l0_lambda_q2"], f32), np.asarray(inp["l0_lambda_k2"], f32)]
                                         ).reshape(1, 256)
    shared["l0_subln"] = np.asarray(inp["l0_subln"], f32).reshape(1, -1)
    shared["l0_q_norm"] = np.ascontiguousarray(np.asarray(inp["l0_q_norm"], f32)[p64]).reshape(1, -1)
    shared["l0_k_norm"] = np.ascontiguousarray(np.asarray(inp["l0_k_norm"], f32)[p64]).reshape(1, -1)
    shared["final_norm"] = np.asarray(inp["final_norm"], f32).reshape(1, -1)
    in_maps = []
    for bi in range(nb):
        m = dict(shared)
        m["x"] = np.ascontiguousarray(x[bi])
        m["ctx"] = np.ascontiguousarray(ctx[bi])
        cT = np.stack([c[bi].reshape(8, 128).T, c_ctx.reshape(8, 128).T], axis=-1)
        m["cT"] = np.ascontiguousarray(cT.reshape(128, 16))
        in_maps.append(m)
    debug = bool(_CACHE.get("debug", False))
    key = ("nc", debug, _nphase)
    if key not in _CACHE:
        _CACHE[key] = build_program(debug=debug, nphase=_nphase, ada1_in_attn=bool(_CACHE.get("ada1", True)))[0]
    nc = _CACHE[key]
    res = run_bass_kernel_spmd(nc, in_maps, core_ids=list(range(nb)))
    if debug:
        _CACHE["last_results"] = res.results
    out = np.stack([np.asarray(r["out"], f32) for r in res.results], axis=0)
    return out
```

```python
import contextlib
import numpy as np
import concourse.bass as bass
import concourse.mybir as mybir
from concourse.bass_utils import run_bass_kernel_spmd

F32 = mybir.dt.float32
BF16 = mybir.dt.bfloat16
ALU = mybir.AluOpType
AF = mybir.ActivationFunctionType
AX = mybir.AxisListType


class Tok:
    __slots__ = ("name", "w", "r", "rd")

    def __init__(self, name=""):
        self.name = name
        self.w = None
        self.r = {}
        self.rd = []


class _Op:
    __slots__ = ("emit", "waits", "dwaits", "dma", "snap")

    def __init__(self, emit, dma=None):
        self.emit = emit
        self.waits = []
        self.dwaits = []
        self.dma = dma
        self.snap = None


class Sched:
    STREAMS = ("pe", "act", "dve", "pool", "sp")
    KQ = {"sp": 12, "act": 2, "pool": 36}

    def __init__(self):
        self.ops = {s: [] for s in self.STREAMS}
        self.known = {s: {} for s in self.STREAMS}
        self.dknown = {s: set() for s in self.STREAMS}
        self.dmas = []
        self.qcount = {q: 0 for q in self.KQ}
        self.out_dmas = []

    def add(self, s, emit, reads=(), writes=(), dma=False, extra=(), dextra=()):
        ops = self.ops[s]
        idx = len(ops)
        deps = {}
        ddeps = set(dextra)
        for st, i in extra:
            if i >= 0 and deps.get(st, -1) < i:
                deps[st] = i

        def dep(st, i):
            if deps.get(st, -1) < i:
                deps[st] = i

        for t in reads:
            if t.w is not None:
                if t.w[0] == "dma":
                    ddeps.add(t.w[1])
                else:
                    dep(*t.w)
        for t in writes:
            if t.w is not None:
                if t.w[0] == "dma":
                    ddeps.add(t.w[1])
                elif dma or t.w[0] != s:
                    dep(*t.w)
            for st, i in t.r.items():
                if dma or st != s:
                    dep(st, i)
            for d in t.rd:
                ddeps.add(d)

        dma_id = None
        if dma:
            dma_id = len(self.dmas)
            n = self.qcount[s]
            self.qcount[s] += 1
            self.dmas.append((s, n))
        op = _Op(emit, dma=dma_id)
        kn = self.known[s]
        for st, i in sorted(deps.items()):
            if kn.get(st, -1) >= i:
                continue
            op.waits.append((st, i))
            sn = self.ops[st][i].snap
            for a, b in sn.items():
                if kn.get(a, -1) < b:
                    kn[a] = b
            if kn.get(st, -1) < i:
                kn[st] = i
        dk = self.dknown[s]
        for d in sorted(ddeps):
            if d in dk:
                continue
            dk.add(d)
            op.dwaits.append(d)
        op.snap = dict(kn)
        ops.append(op)
        me = ("dma", dma_id) if dma else (s, idx)
        for t in reads:
            if dma:
                t.rd.append(dma_id)
            else:
                t.r[s] = idx
        for t in writes:
            t.w = me
            t.r = {}
            t.rd = []
        return dma_id

    def dma(self, q, out, in_, reads=(), writes=(), is_out=False):
        def emit(e, out=out, in_=in_):
            return e.dma_start(out=out, in_=in_)
        d = self.add(q, emit, reads, writes, dma=True)
        if is_out:
            self.out_dmas.append(d)
        return d

    def barrier(self):
        last = []
        for st in ("pe", "act", "dve", "pool"):
            i = len(self.ops[st]) - 1
            while i >= 0 and (self.ops[st][i].dma is not None or self.ops[st][i].emit is None):
                i -= 1
            last.append((st, i))
        alld = range(len(self.dmas))
        self.add("pool", lambda e: e.memset(self.bar_ap, 0.0), extra=last, dextra=alld)
        bidx = len(self.ops["pool"]) - 1
        for st in ("pe", "act", "dve", "sp"):
            self.add(st, None, extra=[("pool", bidx)])
        for st in self.STREAMS:
            self.dknown[st].update(alld)

    def finish(self):
        op = _Op(None)
        op.dwaits = list(self.out_dmas)
        op.snap = {}
        self.ops["sp"].append(op)

    def emit_all(self, nc):
        needed = {s: set() for s in self.STREAMS}
        for s in self.STREAMS:
            for op in self.ops[s]:
                for st, i in op.waits:
                    needed[st].add(i)
        for s in self.STREAMS:
            for i in needed[s]:
                o = self.ops[s][i]
                assert o.dma is None and o.emit is not None, ("wait on non-compute op", s, i)
        rank = {}
        for s in self.STREAMS:
            rank[s] = {i: k + 1 for k, i in enumerate(sorted(needed[s]))}
        with contextlib.ExitStack() as es:
            psem = {s: es.enter_context(nc.semaphore("prog_" + s))
                    for s in ("pe", "act", "dve", "pool")}
            qsem = {q: [es.enter_context(nc.semaphore("dq_%s_%d" % (q, k)))
                        for k in range(K)] for q, K in self.KQ.items()}
            block = es.enter_context(nc.Block())

            def replay(s, e):
                for idx, op in enumerate(self.ops[s]):
                    for st, i in op.waits:
                        e.wait_ge(psem[st], rank[st][i])
                    for d in op.dwaits:
                        q, n = self.dmas[d]
                        K = self.KQ[q]
                        e.wait_ge(qsem[q][n % K], 16 * (n // K + 1))
                    if op.emit is None:
                        continue
                    if op.dma is not None:
                        q, n = self.dmas[op.dma]
                        K = self.KQ[q]
                        if n >= K:
                            e.wait_ge(qsem[q][n % K], 16 * (n // K))
                        ins = op.emit(e)
                        ins.then_inc(qsem[q][n % K], 16)
                    else:
                        ins = op.emit(e)
                        if idx in rank[s]:
                            ins.then_inc(psem[s], 1)

            @block.tensor
            def _(e):
                replay("pe", e)

            @block.scalar
            def _(e):
                replay("act", e)

            @block.vector
            def _(e):
                replay("dve", e)

            @block.gpsimd
            def _(e):
                replay("pool", e)

            @block.sync
            def _(e):
                replay("sp", e)
D = 1024
SEQ = 2048
NCTX = 256
NT = SEQ // 128
NKT = (SEQ + NCTX) // 128
DFF = 2816
NF = DFF // 128
EPS = 1e-6
INW = 2304
LAM_INIT0 = 0.8 - 0.6 * 1.0


class Arena:
    def __init__(self, nc, nbytes):
        self.t = nc.alloc_sbuf_tensor("arena", [128, nbytes // 4], F32)
        self.off = 0
        self.cap = nbytes
        self.peak = 0

    def _a(self, nb):
        o = self.off
        self.off += (nb + 31) // 32 * 32
        self.peak = max(self.peak, self.off)
        assert self.off <= self.cap, ("SBUF arena overflow", self.off, self.cap)
        return o

    def f32(self, n):
        o = self._a(n * 4) // 4
        return self.t[:, o:o + n]

    def b16(self, n):
        o = self._a(n * 2) // 4
        return self.t[:, o:o + n // 2].bitcast(BF16)

    def mark(self):
        return self.off

    def release(self, m):
        self.off = m


def v3(ap, a):
    return ap.rearrange("p (a b) -> p a b", a=a)


def v4(ap, a, b):
    return ap.rearrange("p (a b c) -> p a b c", a=a, b=b)


class B:
    def __init__(self, S):
        self.S = S

    def mm(self, out, lhsT, rhs, start, stop, r, w):
        self.S.add("pe", lambda e: e.matmul(out, lhsT=lhsT, rhs=rhs, start=start, stop=stop), r, w)

    def tr(self, out, in_, ident, r, w):
        self.S.add("pe", lambda e: e.transpose(out=out, in_=in_, identity=ident), r, w)

    def act(self, out, in_, func, r, w, scale=1.0, accum_out=None):
        if accum_out is None:
            self.S.add("act", lambda e: e.activation(out=out, in_=in_, func=func, scale=scale), r, w)
        else:
            self.S.add("act", lambda e: e.activation(out=out, in_=in_, func=func, scale=scale,
                                                      accum_out=accum_out), r, w)

    def tt(self, eng, out, in0, in1, op, r, w):
        self.S.add(eng, lambda e: e.tensor_tensor(out=out, in0=in0, in1=in1, op=op), r, w)

    def ts(self, eng, out, in0, s1, op0, r, w, s2=None, op1=None):
        if op1 is None:
            self.S.add(eng, lambda e: e.tensor_scalar(out=out, in0=in0, scalar1=s1, scalar2=None, op0=op0), r, w)
        else:
            self.S.add(eng, lambda e: e.tensor_scalar(out=out, in0=in0, scalar1=s1, scalar2=s2,
                                                      op0=op0, op1=op1), r, w)

    def stt(self, eng, out, in0, scalar, in1, op0, op1, r, w):
        self.S.add(eng, lambda e: e.scalar_tensor_tensor(out=out, in0=in0, scalar=scalar, in1=in1,
                                                         op0=op0, op1=op1), r, w)

    def cp(self, eng, out, in_, r, w):
        if eng == "act":
            self.S.add("act", lambda e: e.copy(out=out, in_=in_), r, w)
        else:
            self.S.add(eng, lambda e: e.tensor_copy(out=out, in_=in_), r, w)

    def red(self, eng, out, in_, r, w):
        self.S.add(eng, lambda e: e.tensor_reduce(out=out, in_=in_, axis=AX.X, op=ALU.add), r, w)

    def recip(self, out, in_, r, w):
        self.S.add("dve", lambda e: e.reciprocal(out=out, in_=in_), r, w)

    def memset(self, eng, ap, val, r, w):
        self.S.add(eng, lambda e: e.memset(ap, val), r, w)

    def rstd(self, out, ss, tmp, inv_n, r, w):
        t = [Tok()]
        self.ts("dve", tmp[0], ss, inv_n, ALU.mult, r, t, s2=EPS, op1=ALU.add)
        self.act(tmp[1], tmp[0], AF.Ln, t, t)
        self.act(out, tmp[1], AF.Exp, t, w, scale=-0.5)


def build_program(debug=False, nphase=6, ada1_in_attn=True, ada0_in_p1=True):
    nc = bass.Bass("TRN2", target_bir_lowering=False)

    def din(name, shape, dt=F32):
        return nc.dram_tensor(name, shape, dt, kind="ExternalInput").ap()

    x_in = din("x", [SEQ, D])
    ctx_in = din("ctx", [NCTX, D])
    cT_in = din("cT", [128, 16])
    ident_in = din("ident", [128, 128], BF16)
    ropeC_in = din("ropeC", [128, NT * 32])
    ropeS_in = din("ropeS", [128, NT * 32])
    cs128_in = din("cs128", [128, 256], BF16)
    dftc_in = din("dftc", [4, 128, NT * 512], BF16)
    dfts_in = din("dfts", [4, 128, NT * 512], BF16)
    W = {}
    for l in (0, 1):
        W["ada_w%d" % l] = din("l%d_ada_w" % l, [D, 6 * D])
        W["ada_b%d" % l] = din("l%d_ada_b" % l, [1, 6 * D])
        W["norm_mix%d" % l] = din("l%d_norm_mix" % l, [1, D])
        W["norm_ffn%d" % l] = din("l%d_norm_ffn" % l, [1, D])
        W["w_gu%d" % l] = din("l%d_w_gate_up" % l, [D, 2 * DFF])
        W["w_d%d" % l] = din("l%d_w_down" % l, [DFF, D])
        W["w_out%d" % l] = din("l%d_w_out" % l, [D, D])
    w_in = din("l0_w_in", [D, INW])
    lam_in = din("l0_lambda", [1, 256])
    subln_in = din("l0_subln", [1, 128])
    qn_in = din("l0_q_norm", [1, 64])
    kn_in = din("l0_k_norm", [1, 64])
    fin_in = din("final_norm", [1, D])
    okind = "ExternalOutput" if debug else "Internal"
    out_d = nc.dram_tensor("out", [SEQ, D], F32, kind="ExternalOutput").ap()
    x1_d = nc.dram_tensor("x1s", [SEQ, D], F32, kind=okind).ap()
    x2_d = nc.dram_tensor("x2s", [SEQ, D], F32, kind=okind).ap()
    x3_d = nc.dram_tensor("x3s", [SEQ, D], F32, kind=okind).ap()
    mod_d = nc.dram_tensor("mods", [2, 2, 6 * D], F32, kind=okind).ap()
    mix_d = nc.dram_tensor("mixd", [SEQ, D], BF16, kind="ExternalOutput").ap() if debug else None
    T_x1d, T_x2d, T_x3d = [[Tok() for _ in range(NT)] for _ in range(3)]
    T_outd = [Tok() for _ in range(NT)]
    T_modd = [Tok(), Tok()]

    S = Sched()
    b = B(S)
    A = Arena(nc, 212736)
    bpairs = [nc.alloc_psum_tensor("bpair%d" % i, [128, 1024], F32) for i in range(4)]
    banks = [bpairs[i // 2][:, (i % 2) * 512:(i % 2 + 1) * 512] for i in range(8)]
    T_bank = [Tok() for _ in range(8)]

    def bank16(i):
        return banks[i][:, 0:512].bitcast(BF16)

    S.bar_ap = A.f32(8)
    ident = A.b16(128); T_ident = Tok()
    S.dma("sp", ident, ident_in, writes=[T_ident])

    class Rot:
        def __init__(self, n, mk):
            self.bufs = [mk() for _ in range(n)]
            self.toks = [Tok() for _ in range(n)]
            self.i = 0

        def next(self):
            k = self.i % len(self.bufs)
            self.i += 1
            return self.bufs[k], self.toks[k]

    stat = Rot(6, lambda: A.f32(4))

    def norm_stats(xt, T_xt, junk, T_junk, ms_eng="pool"):
        st, T_st = stat.next()
        b.memset(ms_eng, st[:, 0:1], 0.0, [], [T_st])
        b.act(junk, xt, AF.Square, [T_xt, T_st], [T_junk, T_st], accum_out=st[:, 0:1])
        b.rstd(st[:, 3:4], st[:, 0:1], [st[:, 1:2], st[:, 2:3]], 1.0 / D, [T_st], [T_st])
        return st[:, 3:4], T_st

    def ada_chunks(l, CW):
        cT = A.f32(16); T_cT = Tok()
        S.dma("sp", cT, cT_in, writes=[T_cT])
        sc = A.b16(16); T_sc = Tok()
        b.act(sc, cT, AF.Silu, [T_cT], [T_sc])
        sc3 = v3(sc, 8)
        nch = 6 * D // CW
        wch = Rot(2, lambda: A.b16(8 * CW))
        bias_r = Rot(2, lambda: A.f32(CW))
        rows_r = Rot(2, lambda: A.f32(CW))
        aw = W["ada_w%d" % l].rearrange("(k p) n -> p k n", p=128)
        loaded = {}

        def load(j):
            if j >= nch or j in loaded:
                return
            wb, T_wb = wch.next()
            S.dma("pool", v3(wb, 8), aw[:, :, j * CW:(j + 1) * CW], writes=[T_wb])
            bs, T_bs = bias_r.next()
            S.dma("sp", bs[0:2, :], W["ada_b%d" % l][:, j * CW:(j + 1) * CW].partition_broadcast(2), writes=[T_bs])
            loaded[j] = (wb, T_wb, bs, T_bs)

        def chunk(j, bk):
            load(j)
            load(j + 1)
            wb, T_wb, bs, T_bs = loaded.pop(j)
            for k in range(8):
                b.mm(banks[bk][0:2, 0:CW], sc3[:, k, :], v3(wb, 8)[:, k, :], k == 0, k == 7,
                     [T_sc, T_wb], [T_bank[bk]])
            rw, T_rw = rows_r.next()
            b.tt("dve", rw[0:2, :], banks[bk][0:2, 0:CW], bs[0:2, :], ALU.add, [T_bank[bk], T_bs], [T_rw])
            S.dma("sp", mod_d[l, :, j * CW:(j + 1) * CW], rw[0:2, :], reads=[T_rw], writes=[T_modd[l]])

        return [(lambda bk, j=j: chunk(j, bk)) for j in range(nch)]

    def ada_phase(l):
        m0 = A.mark()
        for j, ch in enumerate(ada_chunks(l, 512)):
            ch(j % 2)
        S.barrier()
        A.release(m0)

    def load_mod(dst, T_dst, l, row, j, n=1):
        S.dma("sp", dst, mod_d[l, row:row + 1, j * D:(j + n) * D].partition_broadcast(128),
              reads=[T_modd[l]], writes=[T_dst])

    def load_gain(gain_ap):
        g = A.f32(D); T_g = Tok()
        S.dma("sp", g, gain_ap.partition_broadcast(128), writes=[T_g])
        return g, T_g

    def make_AB(l, row, which, g, T_g):
        ab = A.f32(2 * D); T_ab = Tok()
        load_mod(ab, T_ab, l, row, 3 * which, 2)
        ab3 = v3(ab, 2)
        b.stt("dve", ab3[:, 1, :], ab3[:, 1, :], 1.0, g, ALU.add, ALU.mult, [T_ab, T_g], [T_ab])
        return ab3[:, 1, :], ab3[:, 0, :], T_ab

    def pipeline_gen(n, stages):
        tot = n + max(sk for _, sk in stages)
        for step in range(tot):
            for fn, sk in stages:
                i = step - sk
                if 0 <= i < n:
                    fn(i)
            yield step

    def pipeline(n, stages):
        for _ in pipeline_gen(n, stages):
            pass

    class NormPipe:
        def __init__(self, load, ab_fn, add_eng, ms_eng="pool", nx=3, nh=2):
            self.load, self.ab_fn, self.add_eng, self.ms_eng = load, ab_fn, add_eng, ms_eng
            self.xts = Rot(nx, lambda: A.f32(D))
            self.junks = Rot(1, lambda: A.b16(D))
            self.hbs = Rot(nh, lambda: A.b16(D))
            self.tmps = Rot(2, lambda: A.f32(D))
            self.s1, self.s2 = {}, {}

        def S1(self, t):
            xt, T_xt = self.xts.next()
            self.load(t, xt, T_xt)
            jk, T_jk = self.junks.next()
            rstd, T_st = norm_stats(xt, T_xt, jk, T_jk, self.ms_eng)
            self.s1[t] = (xt, T_xt, rstd, T_st)

        def S2(self, t):
            xt, T_xt, rstd, T_st = self.s1.pop(t)
            Aap, Bap, T_ab = self.ab_fn(t)
            hb, T_hb = self.hbs.next()
            tmp, T_tmp = self.tmps.next()
            b.stt("dve", tmp, xt, rstd, Aap, ALU.mult, ALU.mult, [T_xt, T_st, T_ab], [T_tmp])
            b.tt(self.add_eng, hb, tmp, Bap, ALU.add, [T_tmp, T_ab], [T_hb])
            self.s2[t] = (hb, T_hb)

        def S1p(self, p):
            its = []
            for t in (2 * p, 2 * p + 1):
                xt, T_xt = self.xts.next()
                self.load(t, xt, T_xt)
                st, T_st = stat.next()
                its.append((t, xt, T_xt, st, T_st))
            for t, xt, T_xt, st, T_st in its:
                b.memset(self.ms_eng, st[:, 0:1], 0.0, [], [T_st])
            for t, xt, T_xt, st, T_st in its:
                jk, T_jk = self.junks.next()
                b.act(jk, xt, AF.Square, [T_xt, T_st], [T_jk, T_st], accum_out=st[:, 0:1])
            for t, xt, T_xt, st, T_st in its:
                b.ts("dve", st[:, 1:2], st[:, 0:1], 1.0 / D, ALU.mult, [T_st], [T_st], s2=EPS, op1=ALU.add)
            for t, xt, T_xt, st, T_st in its:
                b.act(st[:, 2:3], st[:, 1:2], AF.Ln, [T_st], [T_st])
            for t, xt, T_xt, st, T_st in its:
                b.act(st[:, 3:4], st[:, 2:3], AF.Exp, [T_st], [T_st], scale=-0.5)
            for t, xt, T_xt, st, T_st in its:
                self.s1[t] = (xt, T_xt, st[:, 3:4], T_st)

        def S2p(self, p):
            its = []
            for t in (2 * p, 2 * p + 1):
                xt, T_xt, rstd, T_st = self.s1.pop(t)
                hb, T_hb = self.hbs.next()
                tmp, T_tmp = self.tmps.next()
                its.append((t, xt, T_xt, rstd, T_st, hb, T_hb, tmp, T_tmp))
            for t, xt, T_xt, rstd, T_st, hb, T_hb, tmp, T_tmp in its:
                Aap, Bap, T_ab = self.ab_fn(t)
                b.stt("dve", tmp, xt, rstd, Aap, ALU.mult, ALU.mult, [T_xt, T_st, T_ab], [T_tmp])
            for t, xt, T_xt, rstd, T_st, hb, T_hb, tmp, T_tmp in its:
                Aap, Bap, T_ab = self.ab_fn(t)
                b.tt(self.add_eng, hb, tmp, Bap, ALU.add, [T_tmp, T_ab], [T_hb])
                self.s2[t] = (hb, T_hb)

    def norm_mod(xt, T_xt, Aap, Bap, T_ab, hb, T_hb, tmp, T_tmp, add_eng="pool"):
        rstd, T_st = norm_stats(xt, T_xt, hb, T_hb)
        b.stt("dve", tmp, xt, rstd, Aap, ALU.mult, ALU.mult, [T_xt, T_st, T_ab], [T_tmp])
        b.tt(add_eng, hb, tmp, Bap, ALU.add, [T_tmp, T_ab], [T_hb])

    def attention_layer():
        m_layer = A.mark()
        ada0 = ada_chunks(0, 256) if ada0_in_p1 else []
        for j in range(8 if ada0 else 0):
            ada0.pop(0)(j % 2)
        qTa = A.b16(4 * SEQ); kTa = A.b16(4 * NKT * 128)
        qTb = A.b16(4 * SEQ); kTb = A.b16(2 * NKT * 128)
        Va = A.b16(NKT * 4 * 129); Vb = A.b16(NKT * 2 * 65)
        qTa3, kTa3, qTb3, kTb3 = v3(qTa, 4), v3(kTa, 4), v3(qTb, 4), v3(kTb, 2)
        Va4, Vb4 = v4(Va, NKT, 4), v4(Vb, NKT, 2)
        T_qk6 = [[Tok() for _ in range(6)] for _ in range(NKT)]
        T_v1 = Tok()
        b.memset("pool", Va, 1.0, [], [T_v1])
        b.memset("pool", Vb, 1.0, [], [T_v1])
        ropeC = A.f32(NT * 32); ropeS = A.f32(NT * 32); T_rope = Tok()
        S.dma("sp", ropeC, ropeC_in, writes=[T_rope])
        S.dma("sp", ropeS, ropeS_in, writes=[T_rope])
        ropeC3, ropeS3 = v3(ropeC, NT), v3(ropeS, NT)
        gq = A.f32(64); gk = A.f32(64); T_g = Tok()
        S.dma("sp", gq, qn_in.partition_broadcast(128), writes=[T_g])
        S.dma("sp", gk, kn_in.partition_broadcast(128), writes=[T_g])

        m1 = A.mark()
        gm, T_gm = load_gain(W["norm_mix0"])
        Aap, Bap, T_ab = make_AB(0, 0, 0, gm, T_gm)
        Acp, Bcp, T_abc = make_AB(0, 1, 0, gm, T_gm)
        win = A.b16(8 * INW); win3 = v3(win, 8)
        T_win = [Tok() for _ in range(5)]
        groups = [(0, 512), (512, 512), (1024, 512), (1536, 512), (2048, 256)]
        w_in_v = w_in.rearrange("(k p) n -> p k n", p=128)
        for gi in (1, 2, 4, 0, 3):
            c0, cw = groups[gi]
            S.dma("pool", win3[:, :, c0:c0 + cw], w_in_v[:, :, c0:c0 + cw], writes=[T_win[gi]])
        def p1_load(t, xt, T_xt):
            src = ctx_in[t * 128:(t + 1) * 128, :] if t < 2 else x_in[(t - 2) * 128:(t - 1) * 128, :]
            S.dma("sp", xt, src, writes=[T_xt])

        npipe = NormPipe(p1_load, lambda t: (Acp, Bcp, T_abc) if t < 2 else (Aap, Bap, T_ab), "pool", nx=2)
        hTs = Rot(2, lambda: A.b16(D))
        hT_of = {}
        stg = Rot(2, lambda: A.b16(512 * 3 + 256))
        rt = Rot(2, lambda: A.f32(4 * 256))
        xn_r = Rot(2, lambda: A.f32(512))
        sq_r = Rot(2, lambda: A.f32(512))
        st8 = Rot(4, lambda: A.f32(4 * 8))

        def rope(src3, dst3, nh, tl, r, T_dst):
            tmp, T_tmp = rt.next()
            t = v4(tmp[:, 0:4 * nh * 32], 4, nh)
            Cb = ropeC3[:, tl, :].unsqueeze(1).broadcast_to([128, nh, 32])
            Sb = ropeS3[:, tl, :].unsqueeze(1).broadcast_to([128, nh, 32])
            x1, x2 = src3[:, :, 0:32], src3[:, :, 32:64]
            b.tt("dve", t[:, 0, 0:nh, :], x1, Cb, ALU.mult, r + [T_rope], [T_tmp])
            b.tt("dve", t[:, 1, 0:nh, :], x2, Sb, ALU.mult, r + [T_rope], [T_tmp])
            b.tt("dve", t[:, 2, 0:nh, :], x2, Cb, ALU.mult, r + [T_rope], [T_tmp])
            b.tt("dve", t[:, 3, 0:nh, :], x1, Sb, ALU.mult, r + [T_rope], [T_tmp])
            b.tt("pool", dst3[:, :, 0:32], t[:, 0, 0:nh, :], t[:, 1, 0:nh, :], ALU.subtract, [T_tmp], [T_dst])
            b.tt("pool", dst3[:, :, 32:64], t[:, 2, 0:nh, :], t[:, 3, 0:nh, :], ALU.add, [T_tmp], [T_dst])

        def hn_stats(src3, nh, r):
            sq, T_sq = sq_r.next()
            xn, T_xn = xn_r.next()
            s8, T_s8 = st8.next()
            s84 = v3(s8, 4)
            sq3 = v3(sq[:, 0:nh * 64], nh)
            xn3 = v3(xn[:, 0:nh * 64], nh)
            b.cp("act", xn3, src3, r, [T_xn])
            b.act(sq3, src3, AF.Square, r, [T_sq])
            b.red("dve", s84[:, 0, 0:nh], sq3, [T_sq], [T_s8])
            return (xn3, T_xn, s84, T_s8, nh)

        def hn_rstd(hn):
            xn3, T_xn, s84, T_s8, nh = hn
            b.rstd(s84[:, 3, 0:nh], s84[:, 0, 0:nh], [s84[:, 1, 0:nh], s84[:, 2, 0:nh]], 1.0 / 64, [T_s8], [T_s8])

        def hn_apply(hn, gain):
            xn3, T_xn, s84, T_s8, nh = hn
            b.tt("dve", xn3, xn3, s84[:, 3, 0:nh].unsqueeze(2).broadcast_to([128, nh, 64]), ALU.mult,
                 [T_xn, T_s8], [T_xn])
            b.tt("pool", xn3, xn3, gain.unsqueeze(1).broadcast_to([128, nh, 64]), ALU.mult, [T_xn, T_g], [T_xn])
            return xn3, T_xn

        def p1_S3(t):
            hb, T_hb = npipe.s2.pop(t)
            hT, T_hT = hTs.next()
            for k in range(8):
                b.tr(bank16(0)[:, k * 128:(k + 1) * 128], hb[:, k * 128:(k + 1) * 128], ident,
                     [T_hb, T_ident], [T_bank[0]])
            b.cp("act", hT, bank16(0), [T_bank[0]], [T_hT])
            hT_of[t] = (hT, T_hT)

        stg_of = {}

        def p1_S4(t):
            is_ctx = t < 2
            tl = t - 2
            hT, T_hT = hT_of.pop(t)
            hT3 = v3(hT, 8)
            gl = (1, 4, 2) if is_ctx else (0, 1, 3, 4, 2)
            for gi in gl:
                c0, cw = groups[gi]
                for k in range(8):
                    b.mm(banks[1 + gi][:, 0:cw], hT3[:, k, :], win3[:, k, c0:c0 + cw], k == 0, k == 7,
                         [T_hT, T_win[gi]], [T_bank[1 + gi]])
            sg, T_sg = stg.next()
            stg_of[t] = (sg, T_sg)
            qa_tm, ka_tm, qb_tm, kb_tm = sg[:, 0:512], sg[:, 512:1024], sg[:, 1024:1536], sg[:, 1536:1792]
            if is_ctx:
                b.cp("dve", ka_tm, banks[2][:, 0:512], [T_bank[2]], [T_sg])
            else:
                rope(v3(banks[1][:, 0:512], 8), v3(qa_tm, 8), 8, tl, [T_bank[1]], T_sg)
                rope(v3(banks[2][:, 0:512], 8), v3(ka_tm, 8), 8, tl, [T_bank[2]], T_sg)
            hnq = None
            if not is_ctx:
                hnq = hn_stats(v3(banks[4][:, 0:512], 8), 8, [T_bank[4]])
            hnk = hn_stats(v3(banks[5][:, 0:128], 2), 2, [T_bank[5]])
            b.cp("act", Vb4[:, t, :, 0:64], v3(banks[5][:, 128:256], 2), [T_bank[5], T_v1], [T_qk6[t][5]])
            if hnq is not None:
                hn_rstd(hnq)
            hn_rstd(hnk)
            b.cp("act", Va4[:, t, :, 0:128], v3(banks[3][:, 0:512], 4), [T_bank[3], T_v1], [T_qk6[t][4]])
            if hnq is not None:
                xn3, T_xn = hn_apply(hnq, gq)
                rope(xn3, v3(qb_tm, 8), 8, tl, [T_xn], T_sg)
            xk3, T_xk = hn_apply(hnk, gk)
            kb4 = v4(kb_tm, 2, 2)
            if is_ctx:
                b.cp("dve", kb4[:, :, 0, :], xk3, [T_xk], [T_sg])
            else:
                rope(xk3, kb4[:, :, 0, :], 2, tl, [T_xk], T_sg)
            b.cp("pool", kb4[:, :, 1, :], kb4[:, :, 0, :], [T_sg], [T_sg])

        def p1_S5(t):
            is_ctx = t < 2
            tl = t - 2
            sg, T_sg = stg_of.pop(t)
            qa_tm, ka_tm, qb_tm, kb_tm = sg[:, 0:512], sg[:, 512:1024], sg[:, 1024:1536], sg[:, 1536:1792]
            if not is_ctx:
                for c in range(4):
                    b.tr(bank16(6)[:, c * 128:(c + 1) * 128], qa_tm[:, c * 128:(c + 1) * 128], ident,
                         [T_sg, T_ident], [T_bank[6]])
            for c in range(4):
                b.tr(bank16(6)[:, 512 + c * 128:512 + (c + 1) * 128], ka_tm[:, c * 128:(c + 1) * 128], ident,
                     [T_sg, T_ident], [T_bank[6]])
            if not is_ctx:
                for c in range(4):
                    b.tr(bank16(7)[:, c * 128:(c + 1) * 128], qb_tm[:, c * 128:(c + 1) * 128], ident,
                         [T_sg, T_ident], [T_bank[7]])
            for c in range(2):
                b.tr(bank16(7)[:, 512 + c * 128:512 + (c + 1) * 128], kb_tm[:, c * 128:(c + 1) * 128], ident,
                     [T_sg, T_ident], [T_bank[7]])
            if not is_ctx:
                b.cp("act", qTa3[:, :, tl * 128:(tl + 1) * 128], v3(bank16(6)[:, 0:512], 4), [T_bank[6]], [T_qk6[t][0]])
                b.cp("dve", qTb3[:, :, tl * 128:(tl + 1) * 128], v3(bank16(7)[:, 0:512], 4), [T_bank[7]], [T_qk6[t][2]])
            b.cp("act", kTa3[:, :, t * 128:(t + 1) * 128], v3(bank16(6)[:, 512:1024], 4), [T_bank[6]], [T_qk6[t][1]])
            b.cp("dve", kTb3[:, :, t * 128:(t + 1) * 128], v3(bank16(7)[:, 512:768], 2), [T_bank[7]], [T_qk6[t][3]])

        pipeline(NKT, [(npipe.S1, 0), (npipe.S2, 1), (p1_S3, 2), (p1_S4, 3), (p1_S5, 4)])
        S.barrier()
        A.release(m1)

        G1 = A.f32(D); T_G1 = Tok()
        if not ada0:
            load_mod(G1, T_G1, 0, 0, 2)
        wo = A.b16(8 * D); T_wo = Tok()
        S.dma("pool", v3(wo, 8), W["w_out0"].rearrange("(k p) n -> p k n", p=128), writes=[T_wo])
        wo3 = v3(wo, 8)
        lamv = A.f32(256); T_lam = Tok()
        S.dma("sp", lamv, lam_in.partition_broadcast(128), writes=[T_lam])
        lst = A.f32(8)
        lam4 = v3(lamv, 4)
        b.tt("dve", lam4[:, 0, :], lam4[:, 0, :], lam4[:, 1, :], ALU.mult, [T_lam], [T_lam])
        b.tt("dve", lam4[:, 2, :], lam4[:, 2, :], lam4[:, 3, :], ALU.mult, [T_lam], [T_lam])
        b.red("dve", lst[:, 0:1], lam4[:, 0, :], [T_lam], [T_lam])
        b.red("dve", lst[:, 1:2], lam4[:, 2, :], [T_lam], [T_lam])
        b.act(lst[:, 2:4], lst[:, 0:2], AF.Exp, [T_lam], [T_lam])
        b.tt("dve", lst[:, 4:5], lst[:, 3:4], lst[:, 2:3], ALU.subtract, [T_lam], [T_lam])
        b.ts("dve", lst[:, 5:6], lst[:, 4:5], -LAM_INIT0, ALU.add, [T_lam], [T_lam])
        nlam = lst[:, 5:6]
        sub = A.f32(128); T_sub = Tok()
        S.dma("sp", sub, subln_in.partition_broadcast(128), writes=[T_sub])
        b.ts("pool", sub, sub, 1.0 - LAM_INIT0, ALU.mult, [T_sub], [T_sub])

        pslots = Rot(3, lambda: A.b16(1024))
        pslots.toks = [(Tok(), Tok()) for _ in range(3)]
        oacc_r = Rot(2, lambda: A.f32(8 * 129))
        otmp_r = Rot(2, lambda: A.f32(3 * 512))
        ost_r = Rot(2, lambda: A.f32(32))
        mix_r = Rot(2, lambda: A.b16(4 * D))
        mixT_r = Rot(2, lambda: A.b16(8 * 512))
        xts = Rot(2, lambda: A.f32(D))
        yt_r = Rot(2, lambda: A.f32(D))
        SC = 0.125
        ada1 = ada_chunks(1, 256) if ada1_in_attn else []
        pending_post = [None]

        def make_post(isA, pair, oacc, T_oa, Wd, mix3, T_mix):
            def post():
                o4 = v4(oacc[:, 0:8 * Wd], 2, 4)
                o3 = v3(oacc[:, 0:8 * Wd], 8)
                ost, T_os = ost_r.next()
                b.recip(ost[:, 0:8], o3[:, :, Wd - 1], [T_oa], [T_os])
                if isA:
                    otmp, T_ot = otmp_r.next()
                    u = v3(otmp[:, 0:512], 4); tt_ = v3(otmp[:, 512:1024], 4); sq = v3(otmp[:, 1024:1536], 4)
                    b.ts("dve", ost[:, 4:8], ost[:, 4:8], nlam, ALU.mult, [T_os, T_lam], [T_os])
                    b.tt("dve", u, o4[:, 0, :, 0:128], ost[:, 0:4].unsqueeze(2).broadcast_to([128, 4, 128]),
                         ALU.mult, [T_oa, T_os], [T_ot])
                    b.tt("dve", tt_, o4[:, 1, :, 0:128], ost[:, 4:8].unsqueeze(2).broadcast_to([128, 4, 128]),
                         ALU.mult, [T_oa, T_os], [T_ot])
                    b.tt("pool", u, u, tt_, ALU.add, [T_ot], [T_ot])
                    b.tt("pool", sq, u, u, ALU.mult, [T_ot], [T_ot])
                    b.red("dve", ost[:, 8:12], sq, [T_ot], [T_os])
                    b.rstd(ost[:, 20:24], ost[:, 8:12], [ost[:, 12:16], ost[:, 16:20]], 1.0 / 128, [T_os], [T_os])
                    b.tt("dve", u, u, ost[:, 20:24].unsqueeze(2).broadcast_to([128, 4, 128]), ALU.mult,
                         [T_ot, T_os], [T_ot])
                    b.tt("pool", mix3[:, :, pair * 128:(pair + 1) * 128], u,
                         sub.unsqueeze(1).broadcast_to([128, 4, 128]), ALU.mult, [T_ot, T_sub], [T_mix])
                else:
                    for m in range(2):
                        h = 2 * (pair - 4) + m
                        b.tt("dve" if m == 0 else "pool", mix3[:, :, 512 + h * 64:512 + (h + 1) * 64],
                             o4[:, m, :, 0:64], ost[:, 4 * m:4 * m + 4].unsqueeze(2).broadcast_to([128, 4, 64]),
                             ALU.mult, [T_oa, T_os], [T_mix])
            return post

        for qb in range(4):
            mix, T_mix = mix_r.next()
            mix3 = v3(mix, 4)
            T_q = [tk for j in range(4) for tk in T_qk6[2 + qb * 4 + j]]
            for pair in range(8):
                isA = pair < 4
                if isA:
                    qT3, kT3, ch, kch, Wd = qTa3, kTa3, pair, pair, 129
                else:
                    qT3, kT3, ch, kch, Wd = qTb3, kTb3, pair - 4, (pair - 4) // 2, 65
                pend = None
                if ada1:
                    ada1.pop(0)(4)
                if ada0:
                    ada0.pop(0)(5)
                for kt in range(NKT + 1):
                    if kt == 5 and pending_post[0] is not None:
                        pending_post[0]()
                        pending_post[0] = None
                    cur = None
                    if kt < NKT:
                        cur = []
                        ps2, T_ps2 = pslots.next()
                        for m in range(2):
                            sb = 2 * (kt % 2) + m
                            lo = 64 * m
                            b.mm(banks[sb][:, :], kT3[lo:lo + 64, kch, kt * 128:(kt + 1) * 128],
                                 qT3[lo:lo + 64, ch, qb * 512:(qb + 1) * 512], True, True,
                                 T_q + T_qk6[kt], [T_bank[sb]])
                            T_pm = T_ps2[m]
                            b.act(ps2[:, m * 512:(m + 1) * 512], banks[sb][:, :], AF.Exp, [T_bank[sb]], [T_pm], scale=SC)
                            cur.append((ps2[:, m * 512:(m + 1) * 512], T_pm))
                    if pend is not None:
                        pk, pl = pend
                        for m in range(2):
                            ps, T_ps = pl[m]
                            if isA:
                                rhs = Va4[:, pk, pair, :]
                            else:
                                rhs = Vb4[:, pk, kch, :]
                            for j in range(4):
                                ab_ = 4 + 2 * m + j // 2
                                b.mm(banks[ab_][:, (j % 2) * Wd:(j % 2) * Wd + Wd], ps[:, j * 128:(j + 1) * 128], rhs,
                                     pk == 0 and j % 2 == 0, pk == NKT - 1 and j % 2 == 1, [T_ps, T_v1] + T_qk6[pk], [T_bank[ab_]])
                    pend = (kt, cur) if cur is not None else None
                oacc, T_oa = oacc_r.next()
                o4 = v4(oacc[:, 0:8 * Wd], 2, 4)
                for m in range(2):
                    for hf in range(2):
                        ab_ = 4 + 2 * m + hf
                        b.cp("dve", o4[:, m, 2 * hf:2 * hf + 2, :],
                             v3(banks[ab_][:, 0:2 * Wd], 2), [T_bank[ab_]], [T_oa])
                pending_post[0] = make_post(isA, pair, oacc, T_oa, Wd, mix3, T_mix)
            pending_post[0]()
            pending_post[0] = None
            if qb == 0 and ada0_in_p1:
                load_mod(G1, T_G1, 0, 0, 2)
            if debug:
                for j in range(4):
                    tl = qb * 4 + j
                    S.dma("sp", mix_d[tl * 128:(tl + 1) * 128, :], mix3[:, j, :], reads=[T_mix], is_out=True)
            mixT, T_mT = mixT_r.next()
            mixT3 = v3(mixT, 8)
            for j in range(4):
                bk = j % 2
                for k in range(8):
                    b.tr(bank16(bk)[:, k * 128:(k + 1) * 128], mix3[:, j, k * 128:(k + 1) * 128], ident,
                         [T_mix, T_ident], [T_bank[bk]])
                b.cp("act" if j % 2 == 0 else "dve", mixT3[:, :, j * 128:(j + 1) * 128], v3(bank16(bk), 8),
                     [T_bank[bk]], [T_mT])
            for j in range(4):
                tl = qb * 4 + j
                xt, T_xt = xts.next()
                S.dma("sp", xt, x_in[tl * 128:(tl + 1) * 128, :], writes=[T_xt])
                yt, T_yt = yt_r.next()
                for cg in range(2):
                    bk = 2 + cg
                    for k in range(8):
                        b.mm(banks[bk][:, :], mixT3[:, k, j * 128:(j + 1) * 128], wo3[:, k, cg * 512:(cg + 1) * 512],
                             k == 0, k == 7, [T_mT, T_wo], [T_bank[bk]])
                    b.tt("dve", yt[:, cg * 512:(cg + 1) * 512], banks[bk][:, :], G1[:, cg * 512:(cg + 1) * 512],
                         ALU.mult, [T_bank[bk], T_G1], [T_yt])
                b.tt("pool", yt, yt, xt, ALU.add, [T_yt, T_xt], [T_yt])
                S.dma("pool", x1_d[tl * 128:(tl + 1) * 128, :], yt, reads=[T_yt], writes=[T_x1d[tl]])
        while ada1:
            ada1.pop(0)(4)
        while ada0:
            ada0.pop(0)(5)
        S.barrier()
        A.release(m_layer)

    def ffn_layer(l, src_d, T_src, dst_d, T_dst, final):
        m0 = A.mark()
        gf, T_gf = load_gain(W["norm_ffn%d" % l])
        Aap, Bap, T_ab = make_AB(l, 0, 1, gf, T_gf)
        G2 = A.f32(D); T_G2 = Tok()
        load_mod(G2, T_G2, l, 0, 5)
        if final:
            S.dma("sp", gf, fin_in.partition_broadcast(128), reads=[], writes=[T_gf])
        HT = 1024
        NTH = HT // 128
        wgu_v = W["w_gu%d" % l].rearrange("(k p) n -> p k n", p=128)
        wd_v = W["w_d%d" % l].rearrange("(f p) n -> p f n", p=128)
        ring = Rot(3, lambda: A.b16(8 * 256))
        chunks = [(h, f) for h in range(2) for f in range(NF)]
        ring_of = {}

        def ring_load(ci):
            if ci >= len(chunks):
                return
            h, f = chunks[ci]
            wb, T_wb = ring.next()
            wb3 = v3(wb, 8)
            S.dma("pool", wb3[:, :, 0:128], wgu_v[:, :, f * 128:(f + 1) * 128], writes=[T_wb])
            S.dma("pool", wb3[:, :, 128:256], wgu_v[:, :, DFF + f * 128:DFF + (f + 1) * 128], writes=[T_wb])
            ring_of[ci] = (wb3, T_wb)

        ring_load(0)
        ring_load(1)
        wd = A.b16(NF * D); wd3 = v3(wd, NF)
        NG = 11
        T_wd = [Tok() for _ in range(NG)]
        for g in range(NG):
            S.dma("pool", wd3[:, 2 * g:2 * g + 2, :], wd_v[:, 2 * g:2 * g + 2, :], writes=[T_wd[g]])
        h2T_r = Rot(2, lambda: A.b16(8 * HT))
        actT = A.b16(NF * HT); actT3 = v3(actT, NF); T_act = [Tok() for _ in range(NF)]
        sg_r = Rot(2, lambda: A.b16(512))
        cur_half = [0]

        def f_load(t, xt, T_xt):
            tl = cur_half[0] * NTH + t
            S.dma("sp", xt, src_d[tl * 128:(tl + 1) * 128, :], reads=[T_src[tl]], writes=[T_xt])

        npipe = NormPipe(f_load, lambda t: (Aap, Bap, T_ab), "dve", ms_eng="dve", nx=4, nh=4)
        xts, tmps, hbs = npipe.xts, npipe.tmps, npipe.hbs

        def prep_gen(h):
            h2T, T_h2 = h2T_r.next()
            h2T3 = v3(h2T, 8)

            def S3(t):
                hb, T_hb = npipe.s2.pop(t)
                for k in range(8):
                    b.tr(bank16(6)[:, k * 128:(k + 1) * 128], hb[:, k * 128:(k + 1) * 128], ident,
                         [T_hb, T_ident], [T_bank[6]])
                b.cp("act", h2T3[:, :, t * 128:(t + 1) * 128], v3(bank16(6), 8), [T_bank[6]], [T_h2])

            def S3p(p):
                S3(2 * p)
                S3(2 * p + 1)

            def gen():
                cur_half[0] = h
                yield from pipeline_gen(NTH // 2, [(npipe.S1p, 0), (npipe.S2p, 1), (S3p, 2)])
            return (h2T3, T_h2), gen()

        nxt, g0 = prep_gen(0)
        for _ in g0:
            pass
        ci = 0
        for h in range(2):
            h2T3, T_h2 = nxt
            pg = None
            for f in range(NF):
                ring_load(ci + 2)
                wb3, T_wb = ring_of.pop(ci)
                ci += 1
                for nb in range(2):
                    gb, ub = nb, 2 + nb
                    for k in range(8):
                        b.mm(banks[gb][:, :], wb3[:, k, 0:128], h2T3[:, k, nb * 512:(nb + 1) * 512], k == 0, k == 7,
                             [T_wb, T_h2], [T_bank[gb]])
                    for k in range(8):
                        b.mm(banks[ub][:, :], wb3[:, k, 128:256], h2T3[:, k, nb * 512:(nb + 1) * 512], k == 0, k == 7,
                             [T_wb, T_h2], [T_bank[ub]])
                    sg, T_sg = sg_r.next()
                    b.act(sg, banks[gb][:, :], AF.Silu, [T_bank[gb]], [T_sg])
                    b.tt("dve", actT3[:, f, nb * 512:(nb + 1) * 512], banks[ub][:, :], sg, ALU.mult,
                         [T_bank[ub], T_sg], [T_act[f]])
                if h == 0:
                    if f == 6:
                        nxt, pg = prep_gen(1)
                    if pg is not None:
                        next(pg, None)
            if pg is not None:
                for _ in pg:
                    pass
            for j in range(NTH):
                tl = h * NTH + j
                xt, T_xt = xts.next()
                S.dma("sp", xt, src_d[tl * 128:(tl + 1) * 128, :], reads=[T_src[tl]], writes=[T_xt])
                yt, T_yt = tmps.next()
                for cg in range(2):
                    bk = 4 + 2 * (j % 2) + cg
                    if bk == 6:
                        bk = 7 if False else 6
                    for f in range(NF):
                        b.mm(banks[bk][:, :], actT3[:, f, j * 128:(j + 1) * 128], wd3[:, f, cg * 512:(cg + 1) * 512],
                             f == 0, f == NF - 1, [T_act[f], T_wd[f // 2]], [T_bank[bk]])
                    b.tt("dve", yt[:, cg * 512:(cg + 1) * 512], banks[bk][:, :], G2[:, cg * 512:(cg + 1) * 512],
                         ALU.mult, [T_bank[bk], T_G2], [T_yt])
                b.tt("dve", yt, yt, xt, ALU.add, [T_yt, T_xt], [T_yt])
                if not final:
                    S.dma("pool", dst_d[tl * 128:(tl + 1) * 128, :], yt, reads=[T_yt], writes=[T_dst[tl]])
                else:
                    hbj, T_hbj = hbs.next()
                    rstd, T_st = norm_stats(yt, T_yt, hbj, T_hbj, "dve")
                    b.stt("dve", xt, yt, rstd, gf, ALU.mult, ALU.mult, [T_yt, T_st, T_gf], [T_xt])
                    S.dma("pool", dst_d[tl * 128:(tl + 1) * 128, :], xt, reads=[T_xt], writes=[T_dst[tl]],
                          is_out=True)
        S.barrier()
        A.release(m0)

    def fourier_layer():
        m0 = A.mark()
        gm, T_gm = load_gain(W["norm_mix1"])
        Aap, Bap, T_ab = make_AB(1, 0, 0, gm, T_gm)
        G1 = A.f32(D); T_G1 = Tok()
        load_mod(G1, T_G1, 1, 0, 2)
        cs = A.b16(256); T_cs = Tok()
        S.dma("sp", cs, cs128_in, writes=[T_cs])
        wo = A.b16(8 * D); T_wo = Tok()
        S.dma("pool", v3(wo, 8), W["w_out1"].rearrange("(k p) n -> p k n", p=128), writes=[T_wo])
        wo3 = v3(wo, 8)
        Y1 = A.b16(NT * D); Y2 = A.b16(NT * D)
        Y13, Y23 = v3(Y1, NT), v3(Y2, NT)
        T_Y = [Tok() for _ in range(NT)]
        T_Y2 = [Tok() for _ in range(NT)]
        dft_r = Rot(2, lambda: (A.b16(NT * 512), A.b16(NT * 512)))
        fT_r = Rot(1, lambda: A.b16(8 * 512))
        def f_load(t, xt, T_xt):
            S.dma("sp", xt, x2_d[t * 128:(t + 1) * 128, :], reads=[T_x2d[t]], writes=[T_xt])

        npipe = NormPipe(f_load, lambda t: (Aap, Bap, T_ab), "pool", nx=4, nh=4)
        xts, tmps = npipe.xts, npipe.tmps
        hTs = Rot(2, lambda: A.b16(D))
        hT_of = {}

        def f_S3(t):
            hb, T_hb = npipe.s2.pop(t)
            hT, T_hT = hTs.next()
            for k in range(8):
                b.tr(bank16(0)[:, k * 128:(k + 1) * 128], hb[:, k * 128:(k + 1) * 128], ident,
                     [T_hb, T_ident], [T_bank[0]])
            b.cp("act", hT, bank16(0), [T_bank[0]], [T_hT])
            hT_of[t] = (hT, T_hT)

        def f_S4(t):
            hT, T_hT = hT_of.pop(t)
            hT3 = v3(hT, 8)
            for g in range(8):
                bk = 1 + g // 2
                b.mm(banks[bk][:, (g % 2) * 256:(g % 2) * 256 + 256], hT3[:, g, :], cs, True, True,
                     [T_hT, T_cs], [T_bank[bk]])
            for q in range(4):
                bk = 1 + q
                bv = v3(banks[bk][:, :], 2)
                eng = "act" if q % 2 == 0 else "dve"
                b.cp(eng, v3(Y13[:, t, q * 256:(q + 1) * 256], 2), bv[:, :, 0:128], [T_bank[bk]], [T_Y[t]])
                b.cp(eng, v3(Y23[:, t, q * 256:(q + 1) * 256], 2), bv[:, :, 128:256], [T_bank[bk]], [T_Y2[t]])

        pipeline(NT // 2, [(npipe.S1p, 0), (npipe.S2p, 1), (lambda p: (f_S4(2 * p), f_S4(2 * p + 1)), 3),
                           (lambda p: (f_S3(2 * p), f_S3(2 * p + 1)), 2)])
        for ub in range(4):
            (dc, ds), T_d = dft_r.next()
            S.dma("sp", dc, dftc_in[ub], writes=[T_d])
            S.dma("sp", ds, dfts_in[ub], writes=[T_d])
            dc3, ds3 = v3(dc, NT), v3(ds, NT)
            fT, T_fT = fT_r.next()
            fT3 = v3(fT, 8)
            for n in range(8):
                bk = 5 + n % 2
                for st in range(NT):
                    b.mm(banks[bk][:, :], Y13[:, st, n * 128:(n + 1) * 128], dc3[:, st, :], st == 0, False,
                         [T_Y[st], T_d], [T_bank[bk]])
                for st in range(NT):
                    b.mm(banks[bk][:, :], Y23[:, st, n * 128:(n + 1) * 128], ds3[:, st, :], False, st == NT - 1,
                         [T_Y2[st], T_d], [T_bank[bk]])
                b.act(fT3[:, n, :], banks[bk][:, :], AF.Copy, [T_bank[bk]], [T_fT], scale=1.0 / 512)
            for j in range(4):
                t = ub * 4 + j
                xt, T_xt = xts.next()
                S.dma("sp", xt, x2_d[t * 128:(t + 1) * 128, :], reads=[T_x2d[t]], writes=[T_xt])
                yt, T_yt = tmps.next()
                for cg in range(2):
                    bk = 1 + cg
                    for k in range(8):
                        b.mm(banks[bk][:, :], fT3[:, k, j * 128:(j + 1) * 128], wo3[:, k, cg * 512:(cg + 1) * 512],
                             k == 0, k == 7, [T_fT, T_wo], [T_bank[bk]])
                    b.tt("dve", yt[:, cg * 512:(cg + 1) * 512], banks[bk][:, :], G1[:, cg * 512:(cg + 1) * 512],
                         ALU.mult, [T_bank[bk], T_G1], [T_yt])
                b.tt("pool", yt, yt, xt, ALU.add, [T_yt, T_xt], [T_yt])
                S.dma("pool", x3_d[t * 128:(t + 1) * 128, :], yt, reads=[T_yt], writes=[T_x3d[t]])
        S.barrier()
        A.release(m0)

    phases = [(lambda: None) if ada0_in_p1 else (lambda: ada_phase(0)), (lambda: None) if ada1_in_attn else (lambda: ada_phase(1)), attention_layer,
              lambda: ffn_layer(0, x1_d, T_x1d, x2_d, T_x2d, False), fourier_layer,
              lambda: ffn_layer(1, x3_d, T_x3d, out_d, T_outd, True)]
    for ph in phases[:nphase]:
        ph()
    S.finish()
    S.emit_all(nc)
    return nc, A.peak
_CACHE = {}


def _consts():
    if "c" in _CACHE:
        return _CACHE["c"]
    import ml_dtypes
    bf = ml_dtypes.bfloat16
    ident = np.eye(128, dtype=np.float32).astype(bf)
    tok = np.arange(SEQ)
    row = (tok // 64).astype(np.float32)
    col = (tok % 64).astype(np.float32)
    inv = (np.float32(10000.0) ** (-np.arange(16, dtype=np.float32) / np.float32(16))).astype(np.float32)
    ang = np.concatenate([row[:, None] * inv[None], col[:, None] * inv[None]], axis=1).astype(np.float32)
    C = np.cos(ang).astype(np.float32).reshape(NT, 128, 32).transpose(1, 0, 2).reshape(128, NT * 32)
    Sn = np.sin(ang).astype(np.float32).reshape(NT, 128, 32).transpose(1, 0, 2).reshape(128, NT * 32)
    cv = (np.arange(128)[:, None] * np.arange(128)[None, :]) % 128
    a128 = 2.0 * np.pi * cv / 128.0
    cs128 = np.concatenate([np.cos(a128), np.sin(a128)], axis=1).astype(np.float32).astype(bf)
    s_idx = np.arange(SEQ).reshape(NT, 128)
    u_idx = np.arange(SEQ).reshape(4, 512)
    prod = (s_idx[None, :, :, None] * u_idx[:, None, None, :]) % SEQ
    angs = (2.0 * np.pi / SEQ) * prod.astype(np.float64)
    dftc = np.cos(angs).transpose(0, 2, 1, 3).reshape(4, 128, NT * 512).astype(np.float32).astype(bf)
    dfts = (-np.sin(angs)).transpose(0, 2, 1, 3).reshape(4, 128, NT * 512).astype(np.float32).astype(bf)
    _CACHE["c"] = dict(ident=ident, ropeC=np.ascontiguousarray(C), ropeS=np.ascontiguousarray(Sn),
                       cs128=cs128, dftc=np.ascontiguousarray(dftc), dfts=np.ascontiguousarray(dfts))
    return _CACHE["c"]


def _perm64():
    new = np.zeros(64, dtype=np.int64)
    for h in range(2):
        for a in range(2):
            for p in range(16):
                new[h * 32 + a * 16 + p] = a * 32 + h * 16 + p
    return new


def kernel(_ncores=None, _nphase=6, **inp):
    f32 = np.float32
    x = np.asarray(inp["x"], f32)
    c = np.asarray(inp["c"], f32)
    ctx = np.asarray(inp["ctx"], f32)
    c_ctx = np.asarray(inp["c_ctx"], f32)
    nb = x.shape[0] if _ncores is None else _ncores
    p64 = _perm64()
    w_in = np.asarray(inp["l0_w_in"], f32)
    cols = np.arange(INW)
    for c0, nh in ((0, 8), (512, 8), (1536, 8), (2048, 2)):
        for h in range(nh):
            cols[c0 + h * 64:c0 + (h + 1) * 64] = c0 + h * 64 + p64
    w_in_p = np.ascontiguousarray(w_in[:, cols])
    shared = dict(_consts())
    def full(name):
        return np.ascontiguousarray(np.asarray(inp[name], f32))

    def row(name):
        return np.asarray(inp[name], f32).reshape(1, -1)

    shared["l0_ada_w"] = full("l0_ada_w")
    shared["l0_ada_b"] = row("l0_ada_b")
    shared["l0_norm_mix"] = row("l0_norm_mix")
    shared["l0_norm_ffn"] = row("l0_norm_ffn")
    shared["l0_w_gate_up"] = full("l0_w_gate_up")
    shared["l0_w_down"] = full("l0_w_down")
    shared["l0_w_out"] = full("l0_w_out")
    shared["l1_ada_w"] = full("l1_ada_w")
    shared["l1_ada_b"] = row("l1_ada_b")
    shared["l1_norm_mix"] = row("l1_norm_mix")
    shared["l1_norm_ffn"] = row("l1_norm_ffn")
    shared["l1_w_gate_up"] = full("l1_w_gate_up")
    shared["l1_w_down"] = full("l1_w_down")
    shared["l1_w_out"] = full("l1_w_out")
    shared["l0_w_in"] = w_in_p
    shared["l0_lambda"] = np.concatenate([np.asarray(inp["l0_lambda_q1"], f32), np.asarray(inp["l0_lambda_k1"], f32),
                                          np.asarray(inp["l0_lambda_q2"], f32), np.asarray(inp["l0_lambda_k2"], f32)]
                                         ).reshape(1, 256)
    shared["l0_subln"] = np.asarray(inp["l0_subln"], f32).reshape(1, -1)
    shared["l0_q_norm"] = np.ascontiguousarray(np.asarray(inp["l0_q_norm"], f32)[p64]).reshape(1, -1)
    shared["l0_k_norm"] = np.ascontiguousarray(np.asarray(inp["l0_k_norm"], f32)[p64]).reshape(1, -1)
    shared["final_norm"] = np.asarray(inp["final_norm"], f32).reshape(1, -1)
    in_maps = []
    for bi in range(nb):
        m = dict(shared)
        m["x"] = np.ascontiguousarray(x[bi])
        m["ctx"] = np.ascontiguousarray(ctx[bi])
        cT = np.stack([c[bi].reshape(8, 128).T, c_ctx.reshape(8, 128).T], axis=-1)
        m["cT"] = np.ascontiguousarray(cT.reshape(128, 16))
        in_maps.append(m)
    debug = bool(_CACHE.get("debug", False))
    key = ("nc", debug, _nphase)
    if key not in _CACHE:
        _CACHE[key] = build_program(debug=debug, nphase=_nphase, ada1_in_attn=bool(_CACHE.get("ada1", True)))[0]
    nc = _CACHE[key]
    res = run_bass_kernel_spmd(nc, in_maps, core_ids=list(range(nb)))
    if debug:
        _CACHE["last_results"] = res.results
    out = np.stack([np.asarray(r["out"], f32) for r in res.results], axis=0)
    return out
```
